# Optimizing a Trainium2 kernel written in Bass

```python
import math
import jax
import jax.numpy as jnp
from jax import lax
import numpy as np

D_MODEL = 1024
BATCH = 2
SEQ = 16384
DEPTH = 4

GRID_W = 64
CTX_LEN = 256
MIX_W = D_MODEL
GROUP_W = MIX_W // 4
D_FF = 4 * D_MODEL
N_ADA = 6
EPS = 1e-6
GN_EPS = 64e-5
CHUNK = 64

S5_P = 16
S5_G = GROUP_W // S5_P
S5_N = 64

RW_HEAD = 64
RW_HEADS = GROUP_W // RW_HEAD
RW_W_RANK = 32
RW_A_RANK = 32
RW_G_RANK = 64
RW_COLS = 3 * GROUP_W + RW_W_RANK + RW_A_RANK + RW_G_RANK
RW_SPLITS = (GROUP_W, 2 * GROUP_W, 3 * GROUP_W, 3 * GROUP_W + RW_W_RANK, 3 * GROUP_W + RW_W_RANK + RW_A_RANK)

RET_HEADS = 4
RET_DK = GROUP_W // RET_HEADS
RET_DV = GROUP_W // RET_HEADS
ROPE_BASE = 10000.0

GLA_HEADS = 4
GLA_DK = 32
GLA_DV = GROUP_W // GLA_HEADS
GLA_QK = GLA_HEADS * GLA_DK
GLA_RANK = 16
GLA_TAU = 16.0
GLA_COLS = 2 * GLA_QK + 2 * GROUP_W + GLA_RANK
GLA_SPLITS = (GLA_QK, 2 * GLA_QK, 2 * GLA_QK + GROUP_W, 2 * GLA_QK + 2 * GROUP_W)

IN_TOTAL = GROUP_W + RW_COLS + 4 * GROUP_W + GLA_COLS
IN_SPLITS = (GROUP_W, GROUP_W + RW_COLS, GROUP_W + RW_COLS + 4 * GROUP_W)

kernel_name = 'hybrid_s5_rwkv7_retnet_gla_dit'


def _rmsnorm(x, g):
    xf = x.astype(jnp.float32)
    y = xf * lax.rsqrt(jnp.mean(xf * xf, axis=-1, keepdims=True) + EPS)
    return (y * g.astype(jnp.float32)).astype(x.dtype)


def _head_rmsnorm(y, g):
    yf = y.astype(jnp.float32)
    yn = yf * lax.rsqrt(jnp.mean(yf * yf, axis=-1, keepdims=True) + EPS)
    b_, l_, h_, dh = y.shape
    return yn.reshape(b_, l_, h_ * dh) * g.astype(jnp.float32)


def _head_groupnorm(y, g, b):
    yf = y.astype(jnp.float32)
    mu = jnp.mean(yf, axis=-1, keepdims=True)
    var = jnp.mean(jnp.square(yf - mu), axis=-1, keepdims=True)
    yn = (yf - mu) * lax.rsqrt(var + GN_EPS)
    b_, l_, h_, dh = y.shape
    return yn.reshape(b_, l_, h_ * dh) * g.astype(jnp.float32) + b.astype(jnp.float32)


def _gated_head_norm(o, g, ln_g):
    return (_head_rmsnorm(o, ln_g) * jax.nn.silu(g.astype(jnp.float32))).astype(g.dtype)


def _shift_seq(z):
    a, b = jnp.split(z, 2, axis=-1)
    a = jnp.pad(a, ((0, 0), (1, 0), (0, 0)))[:, :-1]
    b = jnp.pad(b, ((0, 0), (0, 1), (0, 0)))[:, 1:]
    return jnp.concatenate([a, b], axis=-1)


def _shift_grid(z, rows):
    b_, l_, c_ = z.shape
    g4 = z.reshape(b_, rows, GRID_W, c_)
    a, b, c, d = jnp.split(g4, 4, axis=-1)
    a = jnp.pad(a, ((0, 0), (0, 0), (1, 0), (0, 0)))[:, :, :-1]
    b = jnp.pad(b, ((0, 0), (0, 0), (0, 1), (0, 0)))[:, :, 1:]
    c = jnp.pad(c, ((0, 0), (1, 0), (0, 0), (0, 0)))[:, :-1]
    d = jnp.pad(d, ((0, 0), (0, 1), (0, 0), (0, 0)))[:, 1:]
    return jnp.concatenate([a, b, c, d], axis=-1).reshape(b_, l_, c_)


def _axial_rope(x, rows_idx, cols_idx):
    half = x.shape[-1] // 2
    nf = half // 2
    inv = ROPE_BASE ** (-jnp.arange(nf, dtype=jnp.float32) / nf)

    def rot(xh, pos):
        ang = pos.astype(jnp.float32)[:, None] * inv[None, :]
        cos = jnp.cos(ang)[None, :, None, :]
        sin = jnp.sin(ang)[None, :, None, :]
        x1 = xh[..., :nf].astype(jnp.float32)
        x2 = xh[..., nf:].astype(jnp.float32)
        return jnp.concatenate([x1 * cos - x2 * sin, x1 * sin + x2 * cos], axis=-1)

    out = jnp.concatenate([rot(x[..., :half], rows_idx), rot(x[..., half:], cols_idx)], axis=-1)
    return out.astype(x.dtype)


def _ssm_combine(e1, e2):
    a1, b1 = e1
    a2, b2 = e2
    return a2 * a1, a2 * b1 + b2


def _s5(u_ctx, u_lat, lam_re, lam_im, log_dt, b_re, b_im, c_re, c_im, d_skip, glu_w, glu_b):
    f32 = jnp.float32
    lam = lax.complex(jnp.minimum(lam_re.astype(f32), -1e-4), lam_im.astype(f32))
    dt = jnp.exp(log_dt.astype(f32))[..., None]
    a_bar = jnp.exp(lam * dt)
    b_bar = ((a_bar - 1.0) / lam)[..., None] * lax.complex(b_re.astype(f32), b_im.astype(f32))
    c_mat = lax.complex(c_re.astype(f32), c_im.astype(f32))

    def run(u, d, h0, reverse):
        b_, l_, _ = u.shape
        ug = u.astype(f32).reshape(b_, l_, S5_G, S5_P)
        if reverse:
            ug = jnp.flip(ug, axis=1)
        bu = jnp.einsum('gnp,blgp->blgn', b_bar[d], ug)
        if h0 is not None:
            bu = bu.at[:, 0].add(a_bar[d] * h0)
        a = jnp.broadcast_to(a_bar[d], (1, l_, S5_G, S5_N))
        _, h = lax.associative_scan(_ssm_combine, (a, bu), axis=1)
        if reverse:
            h = jnp.flip(h, axis=1)
        return h

    def readout(h, d):
        return jnp.einsum('gpn,blgn->blgp', c_mat[d], h).real

    def out(u, hf, hb):
        b_, l_, _ = u.shape
        y = (readout(hf, 0) + readout(hb, 1)).reshape(b_, l_, GROUP_W) + d_skip.astype(f32) * u.astype(f32)
        y = jax.nn.gelu(y)
        y = y * jax.nn.sigmoid(y @ glu_w.astype(f32) + glu_b.astype(f32))
        return y.astype(u.dtype)

    hcf = run(u_ctx, 0, None, False)
    hcb = run(u_ctx, 1, None, True)
    hlf = run(u_lat, 0, hcf[:, -1], False)
    hlb = run(u_lat, 1, hcb[:, 0], True)
    return out(u_ctx, hcf, hcb), out(u_lat, hlf, hlb)


def _rwkv_scan(r, w, k, v, a, b, s0, reverse):
    def tm(t):
        return jnp.moveaxis(t.astype(jnp.float32), 1, 0)

    def step(s, inp):
        r_t, w_t, k_t, v_t, a_t, b_t = inp
        sa = jnp.einsum('bhvk,bhk->bhv', s, a_t)
        s = s * w_t[:, :, None, :] + sa[..., None] * b_t[:, :, None, :] + v_t[..., None] * k_t[:, :, None, :]
        return s, jnp.einsum('bhvk,bhk->bhv', s, r_t)

    xs = (tm(r), tm(w), tm(k), tm(v), tm(a), tm(b))
    s_fin, y = lax.scan(step, s0, xs, reverse=reverse)
    return jnp.moveaxis(y, 0, 1), s_fin


def _rwkv7(z_ctx, z_lat, rows, mu, w0, w_up, a0, a_up, g_up, k_k, k_a, r_k, ln_g, ln_b):
    f32 = jnp.float32

    def prep(z, shifted):
        b_, l_, _ = z.shape

        def hd(t):
            return t.astype(f32).reshape(b_, l_, RW_HEADS, RW_HEAD)

        zm = z + mu * (shifted - z)
        r, k, v, wlo, alo, glo = jnp.split(zm, RW_SPLITS, axis=-1)
        g = jax.nn.sigmoid(glo) @ g_up
        kk = hd(k * k_k)
        kk = kk * lax.rsqrt(jnp.sum(kk * kk, axis=-1, keepdims=True) + 1e-12)
        dirs = []
        for d in range(2):
            w_raw = -jax.nn.softplus(-(w0[d] + jnp.tanh(wlo) @ w_up[d])) - 0.5
            decay = jnp.exp(-jnp.exp(w_raw.astype(f32)))
            a = jax.nn.sigmoid((a0[d] + alo @ a_up[d]).astype(f32))
            k_d = k.astype(f32) * (1.0 + (a - 1.0) * k_a.astype(f32))
            dirs.append((hd(decay), hd(k_d), -kk, kk * hd(a)))
        return hd(r), hd(k), hd(v), g, dirs

    def finish(r, k, v, g, y):
        b_, l_ = r.shape[:2]
        bonus = jnp.sum(r * k * r_k.astype(f32), axis=-1, keepdims=True) * v
        o = _head_groupnorm(y, ln_g, ln_b) + bonus.reshape(b_, l_, GROUP_W)
        return (o * g.astype(f32)).astype(g.dtype)

    rc, kc, vc, gc, dc = prep(z_ctx, _shift_seq(z_ctx))
    rl, kl, vl, gl, dl = prep(z_lat, _shift_grid(z_lat, rows))
    s0 = jnp.zeros((z_lat.shape[0], RW_HEADS, RW_HEAD, RW_HEAD), f32)
    yc_f, sc_f = _rwkv_scan(rc, dc[0][0], dc[0][1], vc, dc[0][2], dc[0][3], s0, False)
    yc_b, sc_b = _rwkv_scan(rc, dc[1][0], dc[1][1], vc, dc[1][2], dc[1][3], s0, True)
    yl_f, _ = _rwkv_scan(rl, dl[0][0], dl[0][1], vl, dl[0][2], dl[0][3], sc_f, False)
    yl_b, _ = _rwkv_scan(rl, dl[1][0], dl[1][1], vl, dl[1][2], dl[1][3], sc_b, True)
    return finish(rc, kc, vc, gc, yc_f + yc_b), finish(rl, kl, vl, gl, yl_f + yl_b)


def _chunked_gla(q, k, v, log_a, s0, reverse):
    if reverse:
        q, k, v, log_a = jnp.flip(q, 1), jnp.flip(k, 1), jnp.flip(v, 1), jnp.flip(log_a, 1)
    b_, l_, h_, _ = q.shape
    dv = v.shape[-1]
    nc = l_ // CHUNK

    def chunks(t):
        return t.astype(jnp.float32).reshape(b_, nc, CHUNK, h_, t.shape[-1]).transpose(1, 0, 3, 2, 4)

    tril = jnp.tril(jnp.ones((CHUNK, CHUNK), dtype=bool))[:, :, None]

    def body(s, inp):
        qc, kc, vc, gc = inp
        bcum = jnp.cumsum(gc, axis=2)
        o_inter = jnp.einsum('bhck,bhkv->bhcv', qc * jnp.exp(bcum), s)
        diff = bcum[:, :, :, None, :] - bcum[:, :, None, :, :]
        dec = jnp.where(tril, jnp.exp(jnp.where(tril, diff, 0.0)), 0.0)
        scores = jnp.einsum('bhtk,bhsk,bhtsk->bhts', qc, kc, dec)
        o = o_inter + jnp.einsum('bhts,bhsv->bhtv', scores, vc)
        b_last = bcum[:, :, -1:, :]
        s = jnp.exp(b_last[:, :, 0, :])[..., None] * s + jnp.einsum('bhsk,bhsv->bhkv', kc * jnp.exp(b_last - bcum), vc)
        return s, o

    s_fin, o = lax.scan(body, s0, (chunks(q), chunks(k), chunks(v), chunks(log_a)))
    o = o.transpose(1, 0, 3, 2, 4).reshape(b_, l_, h_, dv)
    if reverse:
        o = jnp.flip(o, 1)
    return o, s_fin


def _retention(z_ctx, z_lat, rows_idx, cols_idx, decay_logit, ln_g):
    f32 = jnp.float32
    log_gamma = jax.nn.log_sigmoid(decay_logit.astype(f32))

    def prep(z):
        b_, l_, _ = z.shape
        q, k, v, g = jnp.split(z, 4, axis=-1)
        sh = (b_, l_, RET_HEADS, RET_DK)
        return q.reshape(sh), k.reshape(sh) * RET_DK ** -0.5, v.reshape(b_, l_, RET_HEADS, RET_DV), g

    def la(t, d):
        return jnp.broadcast_to(log_gamma[d][None, None, :, None], t.shape[:3] + (RET_DK,))

    qc, kc, vc, gc = prep(z_ctx)
    ql, kl, vl, gl = prep(z_lat)
    ql = _axial_rope(ql, rows_idx, cols_idx)
    kl = _axial_rope(kl, rows_idx, cols_idx)
    s0 = jnp.zeros((z_lat.shape[0], RET_HEADS, RET_DK, RET_DV), f32)
    oc_f, sf = _chunked_gla(qc, kc, vc, la(qc, 0), s0, False)
    oc_b, sb = _chunked_gla(qc, kc, vc, la(qc, 1), s0, True)
    ol_f, _ = _chunked_gla(ql, kl, vl, la(ql, 0), sf, False)
    ol_b, _ = _chunked_gla(ql, kl, vl, la(ql, 1), sb, True)
    return _gated_head_norm(oc_f + oc_b, gc, ln_g), _gated_head_norm(ol_f + ol_b, gl, ln_g)


def _gla(z_ctx, z_lat, a_up, a_b, ln_g):
    f32 = jnp.float32

    def prep(z):
        b_, l_, _ = z.shape
        q, k, v, g, alo = jnp.split(z, GLA_SPLITS, axis=-1)
        sh = (b_, l_, GLA_HEADS, GLA_DK)
        log_a = [(jax.nn.log_sigmoid((alo @ a_up[d] + a_b[d]).astype(f32)) / GLA_TAU).reshape(sh) for d in range(2)]
        return q.reshape(sh), k.reshape(sh) * GLA_DK ** -0.5, v.reshape(b_, l_, GLA_HEADS, GLA_DV), g, log_a

    qc, kc, vc, gc, lac = prep(z_ctx)
    ql, kl, vl, gl, lal = prep(z_lat)
    s0 = jnp.zeros((z_lat.shape[0], GLA_HEADS, GLA_DK, GLA_DV), f32)
    oc_f, sf = _chunked_gla(qc, kc, vc, lac[0], s0, False)
    oc_b, sb = _chunked_gla(qc, kc, vc, lac[1], s0, True)
    ol_f, _ = _chunked_gla(ql, kl, vl, lal[0], sf, False)
    ol_b, _ = _chunked_gla(ql, kl, vl, lal[1], sb, True)
    return _gated_head_norm(oc_f + oc_b, gc, ln_g), _gated_head_norm(ol_f + ol_b, gl, ln_g)


def _sq_relu_mlp(h, w1, w2):
    return jnp.square(jax.nn.relu(h @ w1)) @ w2


def setup_inputs(seed: int = 0) -> dict:
    key = jax.random.key(seed)
    keys = iter(jax.random.split(key, 64))
    f32 = jnp.float32

    def nrm(shape, scale):
        return jax.random.normal(next(keys), shape, f32) * scale

    n_idx = jnp.arange(S5_N, dtype=f32)
    gam = 1.0 - 2.0 ** (-5.0 - jnp.arange(RET_HEADS, dtype=f32))
    inp = {}
    inp['x'] = nrm((BATCH, SEQ, D_MODEL), 1.0)
    inp['c'] = nrm((BATCH, D_MODEL), 1.0)
    inp['ctx'] = nrm((BATCH, CTX_LEN, D_MODEL), 1.0)
    inp['c_ctx'] = nrm((D_MODEL,), 1.0)
    inp['ada_w'] = nrm((DEPTH, D_MODEL, N_ADA * D_MODEL), 0.5 * D_MODEL ** -0.5)
    inp['ada_b'] = nrm((DEPTH, N_ADA * D_MODEL), 0.02)
    inp['norm1_g'] = 1.0 + nrm((DEPTH, D_MODEL), 0.02)
    inp['norm2_g'] = 1.0 + nrm((DEPTH, D_MODEL), 0.02)
    inp['w_in'] = nrm((DEPTH, D_MODEL, IN_TOTAL), D_MODEL ** -0.5)
    inp['w_out'] = nrm((DEPTH, MIX_W, D_MODEL), MIX_W ** -0.5)
    inp['s5_lam_re'] = -0.5 + nrm((DEPTH, 2, S5_G, S5_N), 0.01)
    inp['s5_lam_im'] = math.pi * n_idx + nrm((DEPTH, 2, S5_G, S5_N), 0.01)
    inp['s5_log_dt'] = jax.random.uniform(next(keys), (DEPTH, 2, S5_G), f32, math.log(1e-3), math.log(1e-1))
    inp['s5_b_re'] = nrm((DEPTH, 2, S5_G, S5_N, S5_P), (2 * S5_P) ** -0.5)
    inp['s5_b_im'] = nrm((DEPTH, 2, S5_G, S5_N, S5_P), (2 * S5_P) ** -0.5)
    inp['s5_c_re'] = nrm((DEPTH, 2, S5_G, S5_P, S5_N), 0.5)
    inp['s5_c_im'] = nrm((DEPTH, 2, S5_G, S5_P, S5_N), 0.5)
    inp['s5_d'] = nrm((DEPTH, GROUP_W), 0.5)
    inp['s5_glu_w'] = nrm((DEPTH, GROUP_W, GROUP_W), GROUP_W ** -0.5)
    inp['s5_glu_b'] = nrm((DEPTH, GROUP_W), 0.02)
    inp['rw_mu'] = jax.random.uniform(next(keys), (DEPTH, RW_COLS), f32)
    inp['rw_w0'] = jnp.linspace(-6.5, -1.5, GROUP_W, dtype=f32) + nrm((DEPTH, 2, GROUP_W), 0.1)
    inp['rw_w_up'] = nrm((DEPTH, 2, RW_W_RANK, GROUP_W), 0.5 * RW_W_RANK ** -0.5)
    inp['rw_a0'] = nrm((DEPTH, 2, GROUP_W), 0.1)
    inp['rw_a_up'] = nrm((DEPTH, 2, RW_A_RANK, GROUP_W), 0.5 * RW_A_RANK ** -0.5)
    inp['rw_g_up'] = nrm((DEPTH, RW_G_RANK, GROUP_W), RW_G_RANK ** -0.5)
    inp['rw_k_k'] = 0.85 + nrm((DEPTH, GROUP_W), 0.02)
    inp['rw_k_a'] = 1.0 + nrm((DEPTH, GROUP_W), 0.02)
    inp['rw_r_k'] = nrm((DEPTH, RW_HEADS, RW_HEAD), 0.1)
    inp['rw_ln_g'] = 1.0 + nrm((DEPTH, GROUP_W), 0.02)
    inp['rw_ln_b'] = nrm((DEPTH, GROUP_W), 0.02)
    inp['ret_decay_logit'] = jnp.log(gam / (1.0 - gam)) + nrm((DEPTH, 2, RET_HEADS), 0.05)
    inp['ret_ln_g'] = 1.0 + nrm((DEPTH, GROUP_W), 0.02)
    inp['gla_a_up'] = nrm((DEPTH, 2, GLA_RANK, GLA_QK), GLA_RANK ** -0.5)
    inp['gla_a_b'] = 1.0 + nrm((DEPTH, 2, GLA_QK), 0.5)
    inp['gla_ln_g'] = 1.0 + nrm((DEPTH, GROUP_W), 0.02)
    inp['mlp_w1'] = nrm((DEPTH, D_MODEL, D_FF), D_MODEL ** -0.5)
    inp['mlp_w2'] = nrm((DEPTH, D_FF, D_MODEL), D_FF ** -0.5)
    inp['final_g'] = 1.0 + nrm((D_MODEL,), 0.02)
    return inp


def reference(x, c, ctx, c_ctx, ada_w, ada_b, norm1_g, norm2_g, w_in, w_out,
              s5_lam_re, s5_lam_im, s5_log_dt, s5_b_re, s5_b_im, s5_c_re, s5_c_im, s5_d, s5_glu_w, s5_glu_b,
              rw_mu, rw_w0, rw_w_up, rw_a0, rw_a_up, rw_g_up, rw_k_k, rw_k_a, rw_r_k, rw_ln_g, rw_ln_b,
              ret_decay_logit, ret_ln_g, gla_a_up, gla_a_b, gla_ln_g, mlp_w1, mlp_w2, final_g):
    l_ = x.shape[1]
    rows = l_ // GRID_W
    rows_idx = jnp.repeat(jnp.arange(rows, dtype=jnp.int32), GRID_W)
    cols_idx = jnp.tile(jnp.arange(GRID_W, dtype=jnp.int32), rows)
    cond_lat = jax.nn.silu(c)
    cond_ctx = jax.nn.silu(c_ctx)[None]
    xc = ctx
    for i in range(DEPTH):
        last = i == DEPTH - 1
        mod_l = (cond_lat @ ada_w[i] + ada_b[i])[:, None, :]
        mod_c = (cond_ctx @ ada_w[i] + ada_b[i])[:, None, :]
        sh1, sc1, g1, sh2, sc2, g2 = jnp.split(mod_l, N_ADA, axis=-1)
        sh1c, sc1c, g1c, sh2c, sc2c, g2c = jnp.split(mod_c, N_ADA, axis=-1)

        h_l = _rmsnorm(x, norm1_g[i]) * (1.0 + sc1) + sh1
        h_c = _rmsnorm(xc, norm1_g[i]) * (1.0 + sc1c) + sh1c
        z_l = h_l @ w_in[i]
        z_c = h_c @ w_in[i]
        s5_l, rw_l, ret_l, gla_l = jnp.split(z_l, IN_SPLITS, axis=-1)
        s5_c, rw_c, ret_c, gla_c = jnp.split(z_c, IN_SPLITS, axis=-1)

        ya_c, ya_l = _s5(s5_c, s5_l, s5_lam_re[i], s5_lam_im[i], s5_log_dt[i], s5_b_re[i], s5_b_im[i],
                         s5_c_re[i], s5_c_im[i], s5_d[i], s5_glu_w[i], s5_glu_b[i])
        yb_c, yb_l = _rwkv7(rw_c, rw_l, rows, rw_mu[i], rw_w0[i], rw_w_up[i], rw_a0[i], rw_a_up[i], rw_g_up[i],
                            rw_k_k[i], rw_k_a[i], rw_r_k[i], rw_ln_g[i], rw_ln_b[i])
        yc_c, yc_l = _retention(ret_c, ret_l, rows_idx, cols_idx, ret_decay_logit[i], ret_ln_g[i])
        yd_c, yd_l = _gla(gla_c, gla_l, gla_a_up[i], gla_a_b[i], gla_ln_g[i])

        mix_l = jnp.concatenate([ya_l, yb_l, yc_l, yd_l], axis=-1) @ w_out[i]
        x = x + g1 * mix_l
        h_l = _rmsnorm(x, norm2_g[i]) * (1.0 + sc2) + sh2
        x = x + g2 * _sq_relu_mlp(h_l, mlp_w1[i], mlp_w2[i])

        if not last:
            mix_c = jnp.concatenate([ya_c, yb_c, yc_c, yd_c], axis=-1) @ w_out[i]
            xc = xc + g1c * mix_c
            h_c = _rmsnorm(xc, norm2_g[i]) * (1.0 + sc2c) + sh2c
            xc = xc + g2c * _sq_relu_mlp(h_c, mlp_w1[i], mlp_w2[i])
    return _rmsnorm(x, final_g)
```

```python
import numpy as np
from contextlib import ExitStack
import concourse.bass as bass
import concourse.mybir as mybir
from concourse.bass_utils import run_bass_kernel_spmd

F32 = mybir.dt.float32
BF16 = mybir.dt.bfloat16
AF = mybir.ActivationFunctionType
ALU = mybir.AluOpType

D = 1024
NLAT = 4096
NCTX = 64
NT = NLAT + NCTX
IN_TOTAL = 2960
DFF = 4096
EPS = 1e-6


class Buf:
    __slots__ = ("t", "w", "r", "name")

    def __init__(self, t, name=""):
        self.t = t
        self.w = None
        self.r = {}
        self.name = name

    def __getitem__(self, idx):
        return self.t[idx]


class Prog:
    ENG = ("pe", "dve", "act", "pool", "sp")

    def __init__(self, nc, es, n_dma_sems=32):
        self.nc = nc
        self.eng = {"pe": nc.tensor, "dve": nc.vector, "act": nc.scalar, "pool": nc.gpsimd, "sp": nc.sync}
        self.sem = {}
        self.cnt = {}
        for e in self.ENG:
            self.sem[e] = es.enter_context(nc.semaphore("sem_" + e))
            self.cnt[e] = 0
        self.dsem = [es.enter_context(nc.semaphore("dsem%d" % i)) for i in range(n_dma_sems)]
        self.dcnt = [0] * n_dma_sems
        self.dnext = 0
        self.waited = {e: {} for e in self.ENG}
        self.nbuf = 0

    def sb(self, es, shape, dtype=F32, name=None):
        self.nbuf += 1
        name = (name or "sb") + "_%d" % self.nbuf
        return Buf(es.enter_context(self.nc.sbuf_tensor(name, list(shape), dtype)), name)

    def ps(self, es, shape, dtype=F32, name=None):
        self.nbuf += 1
        name = (name or "ps") + "_%d" % self.nbuf
        return Buf(es.enter_context(self.nc.psum_tensor(name, list(shape), dtype)), name)

    def dram(self, name, shape, dtype=F32, kind="Internal"):
        return Buf(self.nc.dram_tensor(name, list(shape), dtype, kind=kind).ap(), name)

    def _semh(self, key):
        return self.sem[key] if isinstance(key, str) else self.dsem[key[1]]

    def _wait(self, e, tok):
        key, val = tok
        if self.waited[e].get(key, 0) >= val:
            return
        if key == e and e in ("pe",):
            return
        self.eng[e].wait_ge(self._semh(key), val)
        self.waited[e][key] = val

    def _deps(self, reads, writes):
        toks = []
        for b in reads:
            if b.w is not None:
                toks.append(b.w)
        for b in writes:
            if b.w is not None:
                toks.append(b.w)
            toks.extend(b.r.items())
        return toks

    def _mark(self, tok, reads, writes):
        key, val = tok
        for b in reads:
            if b.r.get(key, 0) < val:
                b.r[key] = val
        for b in writes:
            b.w = tok
            b.r = {}

    def op(self, e, fn, reads=(), writes=()):
        for tok in self._deps(reads, writes):
            self._wait(e, tok)
        ins = fn(self.eng[e])
        self.cnt[e] += 1
        ins.then_inc(self.sem[e], 1)
        self._mark((e, self.cnt[e]), reads, writes)

    def dma(self, q, out, in_, reads=(), writes=()):
        i = self.dnext
        self.dnext = (i + 1) % len(self.dsem)
        if self.dcnt[i] > 0:
            self._wait(q, (("d", i), self.dcnt[i] * 16))
        for tok in self._deps(reads, writes):
            self._wait(q, tok)
        ins = self.eng[q].dma_start(out=out, in_=in_)
        self.dcnt[i] += 1
        ins.then_inc(self.dsem[i], 16)
        self._mark((("d", i), self.dcnt[i] * 16), reads, writes)

    def barrier(self):
        for e in self.ENG:
            for e2 in self.ENG:
                if e2 != e and self.cnt[e2] > 0:
                    self._wait(e, (e2, self.cnt[e2]))
            for i in range(len(self.dsem)):
                if self.dcnt[i] > 0:
                    self._wait(e, (("d", i), self.dcnt[i] * 16))

    def finish(self):
        for i in range(len(self.dsem)):
            if self.dcnt[i] > 0:
                self._wait("sp", (("d", i), self.dcnt[i] * 16))
        for e2 in self.ENG:
            if e2 != "sp" and self.cnt[e2] > 0:
                self._wait("sp", (e2, self.cnt[e2]))


def _mm(p, out_buf, out_ap, lhsT_buf, lhsT_ap, rhs_buf, rhs_ap, start, stop, extra_reads=()):
    p.op("pe", lambda e: e.matmul(out_ap, lhsT=lhsT_ap, rhs=rhs_ap, start=start, stop=stop),
         reads=[lhsT_buf, rhs_buf] + list(extra_reads), writes=[out_buf])


def emit_mod(p, es, cond_d, adaw_d, adab_d, chunks, psum):
    n = len(chunks)
    modT = p.sb(es, [128, n * 8, 2], F32, "modT")
    with ExitStack() as es2:
        cond = p.sb(es2, [128, 8, 2], F32, "cond")
        sig = p.sb(es2, [128, 8, 2], F32, "sig")
        scond = p.sb(es2, [128, 8, 2], F32, "scond")
        adab = p.sb(es2, [128, 48], F32, "adab")
        stg = [p.sb(es2, [128, 8, 512], F32, "adastg") for _ in range(2)]
        p.dma("sp", cond[:], cond_d[:], reads=[cond_d], writes=[cond])
        p.dma("sp", adab[:], adab_d[:], reads=[adab_d], writes=[adab])
        p.op("act", lambda e: e.activation(out=sig[:], in_=cond[:], func=AF.Sigmoid), reads=[cond], writes=[sig])
        p.op("dve", lambda e: e.tensor_tensor(out=scond[:], in0=cond[:], in1=sig[:], op=ALU.mult),
             reads=[cond, sig], writes=[scond])
        adaw_v = adaw_d.t.rearrange("(kc p) n -> p kc n", p=128)
        k = 0
        for ci, a in enumerate(chunks):
            for half in range(2):
                st = stg[k % 2]
                k += 1
                c0 = a * 1024 + half * 512
                p.dma("sp", st[:], adaw_v[:, :, c0:c0 + 512], reads=[adaw_d], writes=[st])
                for cc in range(4):
                    idx = ci * 8 + half * 4 + cc
                    for kc in range(8):
                        _mm(p, psum, psum[:, idx, :], st, st[:, kc, cc * 128:(cc + 1) * 128],
                            scond, scond[:, kc, :], kc == 0, kc == 7)
        for ci, a in enumerate(chunks):
            p.op("dve", lambda e, ci=ci, a=a: e.tensor_tensor(
                out=modT[:, ci * 8:(ci + 1) * 8, :], in0=psum[:, ci * 8:(ci + 1) * 8, :],
                in1=adab[:, a * 8:(a + 1) * 8].unsqueeze(2).to_broadcast([128, 8, 2]), op=ALU.add),
                reads=[psum, adab], writes=[modT])
        p.barrier()
    return modT


def load_weight_bf16(p, es_stage, w_d, wbf, K, N, piece):
    kc_n = K // 128
    w_v = w_d.t.rearrange("(kc p) n -> p kc n", p=128)
    stg = [p.sb(es_stage, [128, kc_n, piece], F32, "wstg") for _ in range(2)]
    k = 0
    for c0 in range(0, N, piece):
        w = min(piece, N - c0)
        st = stg[k % 2]
        k += 1
        p.dma("sp", st[:, :, 0:w], w_v[:, :, c0:c0 + w], reads=[w_d], writes=[st])
        p.op("pool", lambda e, st=st, c0=c0, w=w: e.tensor_copy(out=wbf[:, :, c0:c0 + w], in_=st[:, :, 0:w]),
             reads=[st], writes=[wbf])


def emit_rsqrt(p, out, src, T, eps, parts=128, fast=False):
    if fast:
        p.op("act", lambda e: e.activation(out=out[0:parts, 0:T], in_=src[0:parts, 0:T], func=AF.Ln, bias=float(eps), scale=1.0),
             reads=[src], writes=[out])
        p.op("act", lambda e: e.activation(out=out[0:parts, 0:T], in_=out[0:parts, 0:T], func=AF.Exp, scale=-0.5),
             reads=[out], writes=[out])
        return
    p.op("act", lambda e: e.activation(out=out[0:parts, 0:T], in_=src[0:parts, 0:T], func=AF.Sqrt, bias=float(eps), scale=1.0),
         reads=[src], writes=[out])
    p.op("dve", lambda e: e.reciprocal(out=out[0:parts, 0:T], in_=out[0:parts, 0:T]), reads=[out], writes=[out])


def emit_norm_mod(p, x, T, sq, ssum, rstd, hn, hbf, ones, gsc, sh, sh_off, col):
    p.op("act", lambda e: e.activation(out=sq[:, :, 0:T], in_=x[:, :, 0:T], func=AF.Square), reads=[x], writes=[sq])
    for c in range(8):
        _mm(p, ssum, ssum[:, 0:T], ones, ones[:], sq, sq[:, c, 0:T], c == 0, c == 7)
    emit_rsqrt(p, rstd, ssum, T, EPS)
    p.op("dve", lambda e: e.tensor_tensor(out=hn[:, :, 0:T], in0=x[:, :, 0:T],
                                          in1=rstd[:, 0:T].unsqueeze(1).to_broadcast([128, 8, T]), op=ALU.mult),
         reads=[x, rstd], writes=[hn])
    for c in range(8):
        p.op("act", lambda e, c=c: e.activation(out=hbf[:, c, 0:T], in_=hn[:, c, 0:T], func=AF.Identity,
                                                bias=sh[:, sh_off + c, col:col + 1], scale=gsc[:, c, col:col + 1]),
             reads=[hn, gsc, sh], writes=[hbf])


def tiles_of(T):
    tl = [(t0, T, 0) for t0 in range(0, NLAT, T)]
    tl.append((NLAT, NCTX, 1))
    return tl


def emit_d1(p, xT_d, cond_d, adaw_d, adab_d, g_d, win_d, zT_d, tl=None, ncol=IN_TOTAL):
    T = 512
    tl = tl or tiles_of(T)
    nchunk = (ncol + 127) // 128
    with ExitStack() as es:
        psA = p.ps(es, [128, 16, 2], F32, "psmod")
        modT = emit_mod(p, es, cond_d, adaw_d, adab_d, [0, 1], psA)
        gam = p.sb(es, [128, 8], F32, "gam")
        gsc = p.sb(es, [128, 8, 2], F32, "gsc")
        ones = p.sb(es, [128, 128], BF16, "ones")
        wbf = p.sb(es, [128, 8, ncol], BF16, "winbf")
        p.dma("sp", gam[:], g_d[:], reads=[g_d], writes=[gam])
        p.op("pool", lambda e: e.memset(ones[:], 1.0 / D), writes=[ones])
        p.op("dve", lambda e: e.tensor_scalar(out=gsc[:], in0=modT[:, 8:16, :], scalar1=1.0, scalar2=None, op0=ALU.add),
             reads=[modT], writes=[gsc])
        p.op("dve", lambda e: e.tensor_tensor(out=gsc[:], in0=gsc[:], in1=gam[:].unsqueeze(2).to_broadcast([128, 8, 2]),
                                              op=ALU.mult), reads=[gsc, gam], writes=[gsc])
        with ExitStack() as es2:
            load_weight_bf16(p, es2, win_d, wbf, D, ncol, 424)
            p.barrier()
        xs = [p.sb(es, [128, 8, T], F32, "x") for _ in range(2)]
        sq = p.sb(es, [128, 8, T], BF16, "sq")
        hn = p.sb(es, [128, 8, T], F32, "hn")
        hbf = [p.sb(es, [128, 8, T], BF16, "hbf") for _ in range(2)]
        rstd = p.sb(es, [128, T], F32, "rstd")
        zo = [p.sb(es, [128, T], F32, "zo") for _ in range(4)]
        ssum = p.ps(es, [128, T], F32, "ssum")
        pz = [p.ps(es, [128, T], F32, "pz") for _ in range(4)]
        xv = xT_d.t.rearrange("(c p) t -> p c t", p=128)
        nz = 0
        for ti, (t0, Tt, col) in enumerate(tl):
            x = xs[ti % 2]
            h = hbf[ti % 2]
            p.dma("sp", x[:, :, 0:Tt], xv[:, :, t0:t0 + Tt], reads=[xT_d], writes=[x])
            emit_norm_mod(p, x, Tt, sq, ssum, rstd, hn, h, ones, gsc, modT, 0, col)
            for cc in range(nchunk):
                c0 = cc * 128
                M = min(128, ncol - c0)
                pzb = pz[nz % 4]
                zob = zo[nz % 4]
                for kc in range(8):
                    _mm(p, pzb, pzb[0:M, 0:Tt], wbf, wbf[:, kc, c0:c0 + M], h, h[:, kc, 0:Tt], kc == 0, kc == 7)
                if nz % 2 == 0:
                    p.op("dve", lambda e, pzb=pzb, zob=zob, M=M, Tt=Tt: e.tensor_copy(out=zob[0:M, 0:Tt], in_=pzb[0:M, 0:Tt]),
                         reads=[pzb], writes=[zob])
                else:
                    p.op("act", lambda e, pzb=pzb, zob=zob, M=M, Tt=Tt: e.activation(out=zob[0:M, 0:Tt], in_=pzb[0:M, 0:Tt],
                                                                                     func=AF.Copy),
                         reads=[pzb], writes=[zob])
                p.dma("act", zT_d[c0:c0 + M, t0:t0 + Tt], zob[0:M, 0:Tt], reads=[zob])
                nz += 1
        p.barrier()


def build_d1():
    nc = bass.Bass("TRN2", target_bir_lowering=False)
    with ExitStack() as es:
        p = Prog(nc, es)
        xT = p.dram("xT", [D, NT], F32, "ExternalInput")
        cond = p.dram("cond", [128, 8, 2], F32, "ExternalInput")
        adaw = p.dram("ada_w", [D, 6 * D], F32, "ExternalInput")
        adab = p.dram("ada_b", [128, 48], F32, "ExternalInput")
        g = p.dram("norm_g", [128, 8], F32, "ExternalInput")
        win = p.dram("w_in", [D, IN_TOTAL], F32, "ExternalInput")
        zT = p.dram("zT", [IN_TOTAL, NT], F32, "ExternalOutput")
        emit_d1(p, xT, cond, adaw, adab, g, win, zT)
        p.finish()
    return nc


def emit_d2(p, xT_d, yT_d, cond_d, adaw_d, adab_d, g_d, wout_d, gluw_d, glub_d, w1_d, w2_d, x1_d, xo_d, fin_d, fo_d, tl=None, slotmajor=False, fo_off=0):
    T = 256
    tl = tl or tiles_of(T)
    with ExitStack() as es:
        psA = p.ps(es, [128, 32, 2], F32, "psmod")
        modT = emit_mod(p, es, cond_d, adaw_d, adab_d, [2, 3, 4, 5], psA)
        gam = p.sb(es, [128, 8], F32, "gam")
        gsc = p.sb(es, [128, 8, 2], F32, "gsc")
        ones = p.sb(es, [128, 128], BF16, "ones")
        p.dma("sp", gam[:], g_d[:], reads=[g_d], writes=[gam])
        p.op("pool", lambda e: e.memset(ones[:], 1.0 / D), writes=[ones])
        p.op("dve", lambda e: e.tensor_scalar(out=gsc[:], in0=modT[:, 16:24, :], scalar1=1.0, scalar2=None, op0=ALU.add),
             reads=[modT], writes=[gsc])
        p.op("dve", lambda e: e.tensor_tensor(out=gsc[:], in0=gsc[:], in1=gam[:].unsqueeze(2).to_broadcast([128, 8, 2]),
                                              op=ALU.mult), reads=[gsc, gam], writes=[gsc])
        xv = xT_d.t.rearrange("(c p) t -> p c t", p=128)
        yv = yT_d.t.rearrange("(c p) t -> p c t", p=128)
        x1v = x1_d.t.rearrange("(c p) t -> p c t", p=128)
        xov = xo_d.t.rearrange("(c p) t -> p c t", p=128)
        with ExitStack() as esa:
            wo = p.sb(esa, [128, 8, D], BF16, "wo")
            if slotmajor:
                gw = p.sb(esa, [64, 4, 256], BF16, "gw")
                gb = p.sb(esa, [64, 4], F32, "gb")
                gwst = p.sb(esa, [64, 4, 256], F32, "gwst")
                p.dma("sp", gwst[:], gluw_d[:], writes=[gwst])
                p.op("pool", lambda e: e.tensor_copy(out=gw[:], in_=gwst[:]), reads=[gwst], writes=[gw])
            else:
                gw = p.sb(esa, [128, 2, 256], BF16, "gw")
                gb = p.sb(esa, [128, 2], F32, "gb")
            p.dma("sp", gb[:], glub_d[:], reads=[glub_d], writes=[gb])
            with ExitStack() as es2:
                load_weight_bf16(p, es2, wout_d, wo, D, D, 512)
                if not slotmajor:
                    load_weight_bf16(p, es2, gluw_d, gw, 256, 256, 256)
                p.barrier()
            xs = [p.sb(esa, [128, 8, T], F32, "x") for _ in range(2)]
            ys = [p.sb(esa, [128, 8, T], F32, "y") for _ in range(2)]
            ybf = [p.sb(esa, [128, 8, T], BF16, "ybf") for _ in range(2)]
            gate = p.sb(esa, [128, 4, T], F32, "gate")
            pg = [p.ps(esa, [128, T], F32, "pg") for _ in range(2)]
            pm = [p.ps(esa, [128, T], F32, "pm") for _ in range(4)]
            nm = 0
            for ti, (t0, Tt, col) in enumerate(tl):
                x = xs[ti % 2]
                y = ys[ti % 2]
                yb = ybf[ti % 2]
                p.dma("sp", x[:, :, 0:Tt], xv[:, :, t0:t0 + Tt], reads=[xT_d], writes=[x])
                p.dma("sp", y[:, :, 0:Tt], yv[:, :, t0:t0 + Tt], reads=[yT_d], writes=[y])
                p.op("pool", lambda e, y=y, yb=yb, Tt=Tt: e.tensor_copy(out=yb[:, :, 0:Tt], in_=y[:, :, 0:Tt]),
                     reads=[y], writes=[yb])
                if slotmajor:
                    for ob in range(4):
                        pgb = pg[ob % 2]
                        for a in range(4):
                            _mm(p, pgb, pgb[0:64, 0:Tt], gw, gw[:, a, ob * 64:(ob + 1) * 64], yb, yb[0:64, 2 * a, 0:Tt], a == 0, a == 3)
                        p.op("act", lambda e, ob=ob, pgb=pgb, Tt=Tt: e.activation(out=gate[0:64, ob, 0:Tt], in_=pgb[0:64, 0:Tt],
                                                                               func=AF.Sigmoid, bias=gb[:, ob:ob + 1], scale=1.0),
                             reads=[pgb, gb], writes=[gate])
                    p.op("dve", lambda e, y=y, yb=yb, Tt=Tt: e.tensor_tensor(
                        out=yb[0:64, :, 0:Tt].rearrange("p (a two) t -> p a two t", two=2)[:, :, 0, :],
                        in0=y[0:64, :, 0:Tt].rearrange("p (a two) t -> p a two t", two=2)[:, :, 0, :],
                        in1=gate[0:64, :, 0:Tt], op=ALU.mult), reads=[y, gate], writes=[yb])
                else:
                    for oc in range(2):
                        for kc in range(2):
                            _mm(p, pg[oc], pg[oc][:, 0:Tt], gw, gw[:, kc, oc * 128:(oc + 1) * 128], yb, yb[:, kc, 0:Tt],
                                kc == 0, kc == 1)
                        p.op("act", lambda e, oc=oc, Tt=Tt: e.activation(out=gate[:, oc, 0:Tt], in_=pg[oc][:, 0:Tt],
                                                                         func=AF.Sigmoid, bias=gb[:, oc:oc + 1], scale=1.0),
                             reads=[pg[oc], gb], writes=[gate])
                    p.op("dve", lambda e, y=y, yb=yb, Tt=Tt: e.tensor_tensor(out=yb[:, 0:2, 0:Tt], in0=y[:, 0:2, 0:Tt],
                                                                             in1=gate[:, :, 0:Tt], op=ALU.mult),
                         reads=[y, gate], writes=[yb])
                for oc in range(8):
                    pmb = pm[nm % 4]
                    nm += 1
                    for kc in range(8):
                        _mm(p, pmb, pmb[:, 0:Tt], wo, wo[:, kc, oc * 128:(oc + 1) * 128], yb, yb[:, kc, 0:Tt], kc == 0, kc == 7)
                    p.op("dve", lambda e, pmb=pmb, x=x, oc=oc, Tt=Tt, col=col: e.scalar_tensor_tensor(
                        out=x[:, oc, 0:Tt], in0=pmb[:, 0:Tt], scalar=modT[:, oc, col:col + 1], in1=x[:, oc, 0:Tt],
                        op0=ALU.mult, op1=ALU.add), reads=[pmb, x, modT], writes=[x])
                p.dma("act", x1v[:, :, t0:t0 + Tt], x[:, :, 0:Tt], reads=[x])
            p.barrier()
        with ExitStack() as esb:
            w1 = p.sb(esb, [128, 8, DFF], BF16, "w1")
            w2 = p.sb(esb, [128, 32, D], BF16, "w2")
            with ExitStack() as es2:
                load_weight_bf16(p, es2, w1_d, w1, D, DFF, 512)
                load_weight_bf16(p, es2, w2_d, w2, DFF, D, 128)
                p.barrier()
            xs = [p.sb(esb, [128, 8, T], F32, "x") for _ in range(2)]
            sq = p.sb(esb, [128, 8, T], BF16, "sq")
            hn = p.sb(esb, [128, 8, T], F32, "hn")
            hbf = p.sb(esb, [128, 8, T], BF16, "hbf")
            rstd = p.sb(esb, [128, T], F32, "rstd")
            rl = [p.sb(esb, [128, T], F32, "rl") for _ in range(2)]
            act = p.sb(esb, [128, 32, T], BF16, "act")
            fing = p.sb(esb, [128, 8], F32, "fing")
            ssum = p.ps(esb, [128, T], F32, "ssum")
            ph = [p.ps(esb, [128, T], F32, "ph") for _ in range(3)]
            po = [p.ps(esb, [128, T], F32, "po") for _ in range(2)]
            if fin_d is not None:
                p.dma("sp", fing[:], fin_d[:], reads=[fin_d], writes=[fing])
            nh = 0
            no = 0
            for ti, (t0, Tt, col) in enumerate(tl):
                x = xs[ti % 2]
                p.dma("sp", x[:, :, 0:Tt], x1v[:, :, t0:t0 + Tt], reads=[x1_d], writes=[x])
                emit_norm_mod(p, x, Tt, sq, ssum, rstd, hn, hbf, ones, gsc, modT, 8, col)
                for hc in range(32):
                    phb = ph[nh % 3]
                    rlb = rl[nh % 2]
                    nh += 1
                    for kc in range(8):
                        _mm(p, phb, phb[:, 0:Tt], w1, w1[:, kc, hc * 128:(hc + 1) * 128], hbf, hbf[:, kc, 0:Tt], kc == 0, kc == 7)
                    p.op("act", lambda e, phb=phb, rlb=rlb, Tt=Tt: e.activation(out=rlb[:, 0:Tt], in_=phb[:, 0:Tt], func=AF.Relu),
                         reads=[phb], writes=[rlb])
                    p.op("dve", lambda e, rlb=rlb, hc=hc, Tt=Tt: e.tensor_tensor(out=act[:, hc, 0:Tt], in0=rlb[:, 0:Tt],
                                                                                 in1=rlb[:, 0:Tt], op=ALU.mult),
                         reads=[rlb], writes=[act])
                for oc in range(8):
                    pob = po[no % 2]
                    no += 1
                    for hc in range(32):
                        _mm(p, pob, pob[:, 0:Tt], w2, w2[:, hc, oc * 128:(oc + 1) * 128], act, act[:, hc, 0:Tt], hc == 0, hc == 31)
                    p.op("dve", lambda e, pob=pob, x=x, oc=oc, Tt=Tt, col=col: e.scalar_tensor_tensor(
                        out=x[:, oc, 0:Tt], in0=pob[:, 0:Tt], scalar=modT[:, 24 + oc, col:col + 1], in1=x[:, oc, 0:Tt],
                        op0=ALU.mult, op1=ALU.add), reads=[pob, x, modT], writes=[x])
                p.dma("act", xov[:, :, t0:t0 + Tt], x[:, :, 0:Tt], reads=[x])
                if fin_d is not None and col == 0:
                    fov = fo_d.t.rearrange("(c p) t -> p c t", p=128)
                    p.op("act", lambda e, x=x, Tt=Tt: e.activation(out=sq[:, :, 0:Tt], in_=x[:, :, 0:Tt], func=AF.Square),
                         reads=[x], writes=[sq])
                    for c in range(8):
                        _mm(p, ssum, ssum[:, 0:Tt], ones, ones[:], sq, sq[:, c, 0:Tt], c == 0, c == 7)
                    emit_rsqrt(p, rstd, ssum, Tt, EPS)
                    p.op("dve", lambda e, x=x, Tt=Tt: e.tensor_tensor(out=hn[:, :, 0:Tt], in0=x[:, :, 0:Tt],
                                                                      in1=rstd[:, 0:Tt].unsqueeze(1).to_broadcast([128, 8, Tt]),
                                                                      op=ALU.mult), reads=[x, rstd], writes=[hn])
                    p.op("dve", lambda e, Tt=Tt: e.tensor_tensor(out=hn[:, :, 0:Tt], in0=hn[:, :, 0:Tt],
                                                                 in1=fing[:].unsqueeze(2).to_broadcast([128, 8, Tt]),
                                                                 op=ALU.mult), reads=[hn, fing], writes=[hn])
                    p.dma("act", fov[:, :, t0 - fo_off:t0 - fo_off + Tt], hn[:, :, 0:Tt], reads=[hn])
            p.barrier()


def build_d2(final):
    nc = bass.Bass("TRN2", target_bir_lowering=False)
    with ExitStack() as es:
        p = Prog(nc, es)
        xT = p.dram("xT", [D, NT], F32, "ExternalInput")
        yT = p.dram("yT", [D, NT], F32, "ExternalInput")
        cond = p.dram("cond", [128, 8, 2], F32, "ExternalInput")
        adaw = p.dram("ada_w", [D, 6 * D], F32, "ExternalInput")
        adab = p.dram("ada_b", [128, 48], F32, "ExternalInput")
        g = p.dram("norm_g", [128, 8], F32, "ExternalInput")
        wout = p.dram("w_out", [D, D], F32, "ExternalInput")
        gluw = p.dram("glu_w", [256, 256], F32, "ExternalInput")
        glub = p.dram("glu_b", [128, 2], F32, "ExternalInput")
        w1 = p.dram("w1", [D, DFF], F32, "ExternalInput")
        w2 = p.dram("w2", [DFF, D], F32, "ExternalInput")
        x1 = p.dram("x1", [D, NT], F32, "Internal")
        xo = p.dram("xo", [D, NT], F32, "ExternalOutput")
        fin = fo = None
        if final:
            fin = p.dram("final_g", [128, 8], F32, "ExternalInput")
            fo = p.dram("fo", [D, NLAT], F32, "ExternalOutput")
        emit_d2(p, xT, yT, cond, adaw, adab, g, wout, gluw, glub, w1, w2, x1, xo, fin, fo)
        p.finish()
    return nc


LCTX = 256
LLAT = 16384
LSEQ = LCTX + LLAT
TT = 512
CH = 64
ZROWS = 848
R_S5, R_RWR, R_RWK, R_RWV, R_RWWA, R_RWG = 0, 64, 128, 192, 256, 320
R_RETQ, R_RETK, R_RETV, R_RETG = 384, 448, 512, 576
R_GLAQ, R_GLAK, R_GLAV, R_GLAG, R_GLAA = 640, 672, 704, 768, 832
C_ID, C_M4, C_ML, C_CM, C_JS, C_PR, C_ROPE, C_JJ = 0, 128, 256, 320, 832, 960, 1024, 1536
NCONST = 2048


def make_consts():
    c = np.zeros((128, NCONST), np.float32)
    c[:, C_ID:C_ID + 128] = np.eye(128, dtype=np.float32)
    s = np.arange(64)[:, None]
    t = np.arange(64)[None, :]
    strict = (s < t).astype(np.float32)
    incl = (s <= t).astype(np.float32)
    m4 = np.block([[strict, incl], [strict, incl]])
    c[:, C_M4:C_M4 + 128] = m4
    c[0:64, C_ML:C_ML + 64] = (s > t).astype(np.float32)
    cm = np.ones(512, np.float32)
    cm[::64] = 0.0
    c[:, C_CM:C_CM + 512] = cm[None, :]
    js = np.zeros((128, 128), np.float32)
    for m in range(64):
        js[m + 64, m] = -1.0
        js[m, m + 64] = 1.0
    c[:, C_JS:C_JS + 128] = js
    pr = np.zeros((64, 64), np.float32)
    for h in (0, 32):
        for m in range(16):
            pr[h + m + 16, h + m] = -1.0
            pr[h + m, h + m + 16] = 1.0
    c[0:64, C_PR:C_PR + 64] = pr
    inv = (10000.0 ** (-np.arange(16, dtype=np.float32) / 16)).astype(np.float32)
    fr = np.concatenate([inv, inv])
    ang_r = np.arange(256, dtype=np.float32)[None, :] * fr[:, None]
    ang_c = np.arange(64, dtype=np.float32)[None, :] * fr[:, None]
    c[0:32, C_ROPE:C_ROPE + 256] = np.cos(ang_r)
    c[0:32, C_ROPE + 256:C_ROPE + 512] = np.sin(ang_r)
    c[32:64, C_ROPE:C_ROPE + 64] = np.cos(ang_c)
    c[32:64, C_ROPE + 64:C_ROPE + 128] = np.sin(ang_c)
    c[:, C_JJ:C_JJ + 512] = np.arange(1, 513, dtype=np.float32)[None, :]
    return c


class SCtx:
    pass


def s_tiles(rev):
    lat = [(LCTX + i * TT, TT, False) for i in range(LLAT // TT)]
    if rev:
        lat = lat[::-1]
    return [(0, LCTX, True)] + lat


def rv(ap_fn, rev):
    return ap_fn


def emit_chunkscan(p, S, pre, dk, W, dplr, Hst, yout_ap_fn):
    nch = W // CH
    cst = S.cst
    import os
    stop = int(os.environ.get("S_STOP", "9"))
    if stop <= 1:
        return yout_ap_fn(S.psY, S.psA[0:64].rearrange("p a b -> p (a b)"), S.psA)
    QR, KD, LW = pre["QR"], pre["KD"], pre["LW"]
    cs, ex, tmp = S.cs, S.ex, S.tmp
    ARp, BKm, BKh, Wc = S.ARp, S.BKm, S.BKh, S.Wc
    dv = lambda b: b
    p.op("dve", lambda e: e.tensor_tensor_scan(out=cs[0:dk, 0:W], data0=cst[0:dk, C_CM:C_CM + W], data1=LW[0:dk, 0:W],
                                               initial=0.0, op0=ALU.mult, op1=ALU.add), reads=[cst, LW], writes=[cs])
    cs3 = lambda: cs[0:dk, 0:W].rearrange("p (c j) -> p c j", j=CH)
    tot_b = lambda: cs3()[:, :, CH - 1:CH].to_broadcast([dk, nch, CH])

    def v3(buf, idx):
        return buf[0:dk, 0:nch, idx, :]

    def t3(buf):
        return buf[0:dk, 0:W].rearrange("p (c j) -> p c j", j=CH)

    p.op("act", lambda e: e.activation(out=ex[0:dk, 0:W], in_=cs[0:dk, 0:W], func=AF.Exp), reads=[cs], writes=[ex])
    p.op("dve", lambda e: e.tensor_tensor(out=v3(ARp, 1), in0=t3(QR), in1=t3(ex), op=ALU.mult), reads=[QR, ex], writes=[ARp])
    p.op("act", lambda e: e.activation(out=Wc[0:dk, 0:nch], in_=cs3()[:, :, CH - 1], func=AF.Exp), reads=[cs], writes=[Wc])
    if dplr:
        AV, BV = pre["AV"], pre["BV"]
        p.op("dve", lambda e: e.tensor_tensor(out=tmp[0:dk, 0:W], in0=cs[0:dk, 0:W], in1=LW[0:dk, 0:W], op=ALU.subtract),
             reads=[cs, LW], writes=[tmp])
        p.op("act", lambda e: e.activation(out=tmp[0:dk, 0:W], in_=tmp[0:dk, 0:W], func=AF.Exp), reads=[tmp], writes=[tmp])
        p.op("dve", lambda e: e.tensor_tensor(out=v3(ARp, 0), in0=t3(AV), in1=t3(tmp), op=ALU.mult), reads=[AV, tmp], writes=[ARp])
    p.op("act", lambda e: e.activation(out=ex[0:dk, 0:W], in_=cs[0:dk, 0:W], func=AF.Exp, scale=-1.0), reads=[cs], writes=[ex])
    p.op("dve", lambda e: e.tensor_tensor(out=v3(BKm, 1), in0=t3(KD), in1=t3(ex), op=ALU.mult), reads=[KD, ex], writes=[BKm])
    if dplr:
        p.op("pool", lambda e: e.tensor_tensor(out=v3(BKm, 0), in0=t3(BV), in1=t3(ex), op=ALU.mult), reads=[BV, ex], writes=[BKm])
    p.op("dve", lambda e: e.tensor_tensor(out=t3(tmp), in0=tot_b(), in1=cs3(), op=ALU.subtract), reads=[cs], writes=[tmp])
    p.op("act", lambda e: e.activation(out=tmp[0:dk, 0:W], in_=tmp[0:dk, 0:W], func=AF.Exp), reads=[tmp], writes=[tmp])
    p.op("pool", lambda e: e.tensor_tensor(out=v3(BKh, 1), in0=t3(KD), in1=t3(tmp), op=ALU.mult), reads=[KD, tmp], writes=[BKh])
    if dplr:
        p.op("pool", lambda e: e.tensor_tensor(out=v3(BKh, 0), in0=t3(BV), in1=t3(tmp), op=ALU.mult), reads=[BV, tmp], writes=[BKh])
    if stop <= 2:
        return yout_ap_fn(S.psY, S.psA[0:64].rearrange("p a b -> p (a b)"), S.psA)
    AT, VU, BKT = S.AT, S.VU, S.BKT
    for half in range((nch + 3) // 4):
        psb = S.psS[half % 2]
        cl = list(range(half * 4, min(nch, half * 4 + 4)))
        for c in cl:
            _mm(p, psb, psb[:, c - half * 4, :], BKm, BKm[0:dk, c, :, :], ARp, ARp[0:dk, c, :, :], True, True)
        n = len(cl)
        p.op("dve", lambda e, psb=psb, half=half, n=n: e.tensor_tensor(
            out=AT[:, half * 4:half * 4 + n, :], in0=psb[:, 0:n, :],
            in1=cst[:, C_M4:C_M4 + 128].unsqueeze(1).to_broadcast([128, n, 128]), op=ALU.mult),
            reads=[psb, cst], writes=[AT])
    if stop <= 3:
        return yout_ap_fn(S.psY, S.psA[0:64].rearrange("p a b -> p (a b)"), S.psA)
    Vp = pre["V"]
    for c in range(nch):
        _mm(p, S.psT, S.psT[:, c, :], Vp, Vp[:, c, :, :], cst, cst[0:64, C_ID:C_ID + 64], True, True)
    p.op("act", lambda e: e.activation(out=VU[64:128, 0:nch, :], in_=S.psT[64:128, 0:nch, :], func=AF.Copy),
         reads=[S.psT], writes=[VU])
    for c in range(nch):
        _mm(p, S.psT2, S.psT2[:, c, 0:dk], BKh, BKh[0:dk, c, :, :], cst, cst[0:dk, C_ID:C_ID + dk], True, True)
    p.op("act", lambda e: e.activation(out=BKT[:, 0:nch, 0:dk], in_=S.psT2[:, 0:nch, 0:dk], func=AF.Copy),
         reads=[S.psT2], writes=[BKT])
    if stop <= 4:
        return yout_ap_fn(S.psY, S.psA[0:64].rearrange("p a b -> p (a b)"), S.psA)
    if dplr:
        Pm, Nj, Aj = S.Pm, S.Nj, S.Aj
        for c in range(nch):
            _mm(p, S.psA, S.psA[:, c, :], ARp, ARp[0:dk, c, 0, :], BKm, BKm[0:dk, c, 0, :], True, True)
        p.op("dve", lambda e: e.tensor_tensor(out=Aj[0][:, 0:nch, :], in0=S.psA[:, 0:nch, :],
                                              in1=cst[0:64, C_ML:C_ML + 64].unsqueeze(1).to_broadcast([64, nch, 64]),
                                              op=ALU.mult), reads=[S.psA, cst], writes=[Aj[0]])
        p.op("pool", lambda e: e.tensor_copy(out=Nj[0][:, 0:nch, :], in_=AT[0:64, 0:nch, 0:64]), reads=[AT], writes=[Nj[0]])
        p.op("pool", lambda e: e.tensor_tensor(out=Pm[:, 0:nch, :], in0=AT[0:64, 0:nch, 0:64],
                                               in1=cst[0:64, C_ID:C_ID + 64].unsqueeze(1).to_broadcast([64, nch, 64]),
                                               op=ALU.add), reads=[AT, cst], writes=[Pm])
        cur = 0
        for lvl in range(5):
            nxt = 1 - cur
            last = lvl == 4
            for c in range(nch):
                _mm(p, S.psA, S.psA[:, c, :], Nj[cur], Nj[cur][:, c, :], Aj[cur], Aj[cur][:, c, :], True, True)
            p.op("dve", lambda e, nxt=nxt: e.tensor_copy(out=Aj[nxt][:, 0:nch, :], in_=S.psA[:, 0:nch, :]),
                 reads=[S.psA], writes=[Aj[nxt]])
            if not last:
                for c in range(nch):
                    _mm(p, S.psN, S.psN[:, c, :], Aj[cur], Aj[cur][:, c, :], Nj[cur], Nj[cur][:, c, :], True, True)
                p.op("act", lambda e, nxt=nxt: e.activation(out=Nj[nxt][:, 0:nch, :], in_=S.psN[:, 0:nch, :], func=AF.Copy),
                     reads=[S.psN], writes=[Nj[nxt]])
            for c in range(nch):
                _mm(p, S.psP, S.psP[:, c, :], Aj[nxt], Aj[nxt][:, c, :], Pm, Pm[:, c, :], True, True)
            p.op("dve", lambda e: e.tensor_tensor(out=Pm[:, 0:nch, :], in0=Pm[:, 0:nch, :], in1=S.psP[:, 0:nch, :], op=ALU.add),
                 reads=[Pm, S.psP], writes=[Pm])
            cur = nxt
    if stop <= 5:
        return yout_ap_fn(S.psY, S.psA[0:64].rearrange("p a b -> p (a b)"), S.psA)
    psY, psY2 = S.psY, S.psA
    psY2f = psY2[0:64].rearrange("p a b -> p (a b)")
    rs = slice(0, 128) if dplr else slice(64, 128)
    for c in range(nch):
        if dplr:
            bX, bX2, bU = S.psN, S.psP, S.psS[0]
            aX, aX2, aU = bX[0:64, 0, :], bX2[0:64, 0, :], bU[0:64, 0, 0:64]
            _mm(p, bX, aX, AT, AT[64:128, c, 0:64], VU, VU[64:128, c, :], True, True)
            _mm(p, bX2, aX2, ARp, ARp[0:dk, c, 0, :], Hst, Hst[0:dk, :], True, True)
            p.op("act", lambda e: e.activation(out=S.Xs[:, :], in_=aX, func=AF.Copy), reads=[bX], writes=[S.Xs])
            p.op("dve", lambda e: e.tensor_tensor(out=S.Xs[:, :], in0=S.Xs[:, :], in1=aX2, op=ALU.add),
                 reads=[S.Xs, bX2], writes=[S.Xs])
            _mm(p, bU, aU, S.Pm, S.Pm[:, c, :], S.Xs, S.Xs[:, :], True, True)
            p.op("act", lambda e, c=c: e.activation(out=VU[0:64, c, :], in_=aU, func=AF.Copy), reads=[bU], writes=[VU])
        _mm(p, psY, psY[:, c * CH:(c + 1) * CH], VU, VU[rs, c, :], AT, AT[rs, c, 64:128], True, True)
        _mm(p, psY2, psY2f[:, c * CH:(c + 1) * CH], Hst, Hst[0:dk, :], ARp, ARp[0:dk, c, 1, :], True, True)
        _mm(p, S.psH, S.psH[0:dk, :], BKT, BKT[rs, c, 0:dk], VU, VU[rs, c, :], True, True)
        p.op("dve", lambda e, c=c: e.scalar_tensor_tensor(out=Hst[0:dk, :], in0=Hst[0:dk, :], scalar=Wc[0:dk, c:c + 1],
                                                          in1=S.psH[0:dk, :], op0=ALU.mult, op1=ALU.add),
             reads=[Hst, Wc, S.psH], writes=[Hst])
    yout_ap_fn(psY, psY2f, psY2)


PK = {}
_pk_n = 0


def _pk(name, n):
    global _pk_n
    PK[name] = (_pk_n, n)
    _pk_n += n


for _rg in ("r", "k", "v", "wa", "g"):
    _pk("mu_" + _rg, 1)
    _pk("msk_" + _rg, 6)
for _n in ("w0", "a0"):
    _pk(_n, 2)
for _n in ("k_k", "k_a", "r_k", "rw_ln_g", "rw_ln_b", "ret_ln_g", "gla_ln_g", "s5_d"):
    _pk(_n, 1)
_pk("WA", 128)
_pk("g_up", 64)
_pk("ret_dl", 2)
_pk("gla_aup", 64)
_pk("gla_ab", 2)
_pk("s5col", 24)
_pk("s5row", 8 * 3 * 64)
_pk("s5B", 8 * 2 * 64)
_pk("s5C", 8 * 64)
NPK = _pk_n
RW_COL0 = {"r": 0, "k": 256, "v": 512, "wa": 768, "g": 832}


def pack_params(inp, L, slot):
    pk = np.zeros((128, NPK), np.float32)

    def put(name, arr, r0=0):
        c0, n = PK[name]
        arr = np.asarray(arr, np.float32)
        if arr.ndim == 1:
            arr = arr[:, None]
        assert arr.shape[1] == n, (name, arr.shape, n)
        pk[r0:r0 + arr.shape[0], c0:c0 + n] = arr

    j = slot
    mu = inp["rw_mu"][L]
    cols = {"r": np.arange(64 * j, 64 * j + 64), "k": 256 + np.arange(64 * j, 64 * j + 64),
            "v": 512 + np.arange(64 * j, 64 * j + 64), "wa": np.arange(768, 832), "g": np.arange(832, 896)}
    for rg, cc in cols.items():
        put("mu_" + rg, mu[cc])
        q = cc // 224
        m = np.zeros((64, 6), np.float32)
        for qi in range(4):
            m[:, qi] = (q == qi)
        m[:, 4] = cc < 448
        m[:, 5] = cc >= 448
        put("msk_" + rg, m)
    hs = slice(64 * j, 64 * j + 64)
    put("w0", inp["rw_w0"][L][:, hs].T)
    put("a0", inp["rw_a0"][L][:, hs].T)
    put("k_k", inp["rw_k_k"][L][hs])
    put("k_a", inp["rw_k_a"][L][hs])
    put("r_k", inp["rw_r_k"][L][j])
    put("rw_ln_g", inp["rw_ln_g"][L][hs])
    put("rw_ln_b", inp["rw_ln_b"][L][hs])
    put("ret_ln_g", inp["ret_ln_g"][L][hs])
    put("gla_ln_g", inp["gla_ln_g"][L][hs])
    put("s5_d", inp["s5_d"][L][hs])
    wa = np.zeros((64, 2, 64), np.float32)
    wa[0:32] = inp["rw_w_up"][L][:, :, hs].transpose(1, 0, 2)
    wa[32:64] = inp["rw_a_up"][L][:, :, hs].transpose(1, 0, 2)
    put("WA", wa.reshape(64, 128))
    put("g_up", inp["rw_g_up"][L][:, hs])
    put("ret_dl", np.repeat(inp["ret_decay_logit"][L][:, j][None, :], 64, 0))
    put("gla_aup", inp["gla_a_up"][L][:, :, 32 * j:32 * j + 32].transpose(1, 0, 2).reshape(16, 64))
    put("gla_ab", inp["gla_a_b"][L][:, 32 * j:32 * j + 32].T)
    s5col = np.zeros((128, 8, 3), np.float32)
    s5row = np.zeros((64, 8, 3, 64), np.float32)
    s5B = np.zeros((64, 8, 2, 64), np.float32)
    s5C = np.zeros((128, 8, 64), np.float32)
    for gl in range(4):
        g = 4 * j + gl
        for d in range(2):
            gd = gl * 2 + d
            lre = inp["s5_lam_re"][L][d, g]
            lim = inp["s5_lam_im"][L][d, g]
            ldt = inp["s5_log_dt"][L][d, g]
            s5col[:, gd, 0] = np.concatenate([lre, lre])
            s5col[:, gd, 1] = np.concatenate([lim, lim])
            s5col[:, gd, 2] = ldt
            s5row[:, gd, 0, :] = lre[None, :]
            s5row[:, gd, 1, :] = lim[None, :]
            s5row[:, gd, 2, :] = ldt
            s5B[16 * gl:16 * gl + 16, gd, 0, :] = inp["s5_b_re"][L][d, g].T
            s5B[16 * gl:16 * gl + 16, gd, 1, :] = inp["s5_b_im"][L][d, g].T
            s5C[0:64, gd, 16 * gl:16 * gl + 16] = inp["s5_c_re"][L][d, g].T
            s5C[64:128, gd, 16 * gl:16 * gl + 16] = inp["s5_c_im"][L][d, g].T
    put("s5col", s5col.reshape(128, 24))
    put("s5row", s5row.reshape(64, -1))
    put("s5B", s5B.reshape(64, -1))
    put("s5C", s5C.reshape(128, -1))
    return pk


def pkc(S, name, rows=64, r0=0):
    c0, n = PK[name]
    return S.pk[r0:r0 + rows, c0:c0 + n]


def s_alloc_scan(p, es, S):
    f = lambda shape, name: p.sb(es, shape, F32, name)
    S.cs, S.ex, S.tmp = f([64, TT], "cs"), f([64, TT], "ex"), f([64, TT], "tmp")
    S.ARp, S.BKm, S.BKh = f([64, 8, 2, 64], "ARp"), f([64, 8, 2, 64], "BKm"), f([64, 8, 2, 64], "BKh")
    S.Wc = f([64, 8], "Wc")
    S.AT, S.VU, S.BKT = f([128, 8, 128], "AT"), f([128, 8, 64], "VU"), f([128, 8, 64], "BKT")
    S.Pm = f([64, 8, 64], "Pm")
    S.Nj = [f([64, 8, 64], "Nj") for _ in range(2)]
    S.Aj = [f([64, 8, 64], "Aj") for _ in range(2)]
    S.Xs = f([64, 64], "Xs")
    S.psS = [p.ps(es, [128, 4, 128], F32, "psS") for _ in range(2)]
    S.psT = p.ps(es, [128, 8, 64], F32, "psT")
    S.psT2 = S.psT
    S.psA = p.ps(es, [64, 8, 64], F32, "psA")
    S.psN = p.ps(es, [64, 8, 64], F32, "psN")
    S.psP = p.ps(es, [64, 8, 64], F32, "psP")
    misc = p.ps(es, [128, 8, 64], F32, "psmisc").t
    S.psX, S.psU, S.psH = Buf(misc[0:64, 0, :], "psX"), Buf(misc[0:64, 1, :], "psU"), Buf(misc[0:64, 2, :], "psH")
    S.psX2 = Buf(misc[0:64, 3, :], "psX2")
    S.psY = p.ps(es, [64, TT], F32, "psY")
    S.QR, S.KD, S.LW, S.AV, S.BV = (f([64, TT], n) for n in ("QR", "KD", "LW", "AV", "BV"))
    S.Vp = f([64, 8, 2, 64], "Vp")
    S.ones64 = f([64, 64], "ones64")
    S.mean64 = f([64, 64], "mean64")
    for b in (S.ARp, S.BKm, S.BKh, S.Vp, S.VU):
        p.op("pool", lambda e, b=b: e.memset(b[:], 0.0), writes=[b])
    p.op("pool", lambda e: e.memset(S.ones64[:], 1.0), writes=[S.ones64])
    p.op("pool", lambda e: e.memset(S.mean64[:], 1.0 / 64), writes=[S.mean64])


def s_load(p, S, dst, rows, r0, t0, nt, rev, stage, col0=0):
    if not rev:
        p.dma("sp", dst[0:rows, col0:col0 + nt], S.zsrc(r0, rows, t0, nt), writes=[dst])
    else:
        p.dma("sp", stage[0:rows, 0:nt], S.zsrc(r0, rows, t0, nt), writes=[stage])
        p.op("pool", lambda e: e.tensor_copy(out=dst[0:rows, col0:col0 + nt], in_=stage[0:rows, 0:nt][:, ::-1]),
             reads=[stage], writes=[dst])


def s_setup_scan(p, es, S):
    f = lambda shape, name: p.sb(es, shape, F32, name)
    S.coef = {}
    for rg in ("r", "k", "v", "wa", "g"):
        cf = f([64, 8], "coef_" + rg)
        mu = pkc(S, "mu_" + rg)
        p.op("dve", lambda e, cf=cf, mu=mu: e.tensor_scalar(out=cf[:, 0:1], in0=mu, scalar1=-1.0, scalar2=1.0,
                                                            op0=ALU.mult, op1=ALU.add), reads=[S.pkb], writes=[cf])
        p.op("dve", lambda e, cf=cf, mu=mu, rg=rg: e.tensor_tensor(out=cf[:, 1:7], in0=pkc(S, "msk_" + rg),
                                                                    in1=mu.to_broadcast([64, 6]), op=ALU.mult),
             reads=[S.pkb], writes=[cf])
        S.coef[rg] = cf
    S.der = f([64, 8], "der")
    der = S.der
    p.op("dve", lambda e: e.tensor_scalar(out=der[:, 0:1], in0=pkc(S, "k_a"), scalar1=-1.0, scalar2=1.0,
                                          op0=ALU.mult, op1=ALU.add), reads=[S.pkb], writes=[der])
    p.op("act", lambda e: e.activation(out=der[:, 1:3], in_=pkc(S, "ret_dl"), func=AF.Exp, scale=-1.0), reads=[S.pkb], writes=[der])
    p.op("act", lambda e: e.activation(out=der[:, 1:3], in_=der[:, 1:3], func=AF.Ln, bias=1.0, scale=1.0), reads=[der], writes=[der])
    p.op("dve", lambda e: e.tensor_scalar(out=der[:, 1:3], in0=der[:, 1:3], scalar1=-1.0, scalar2=None, op0=ALU.mult),
         reads=[der], writes=[der])
    p.op("dve", lambda e: e.tensor_scalar(out=der[0:32, 3:5], in0=pkc(S, "gla_ab", 32), scalar1=-1.0, scalar2=None, op0=ALU.mult),
         reads=[S.pkb], writes=[der])


def shift_active(slot, rg):
    if slot is None:
        return {1, 2, 3, 4, 5, 6}
    j = slot
    cc = {"r": np.arange(64 * j, 64 * j + 64), "k": 256 + np.arange(64 * j, 64 * j + 64),
          "v": 512 + np.arange(64 * j, 64 * j + 64), "wa": np.arange(768, 832), "g": np.arange(832, 896)}[rg]
    out = set(int(q) + 1 for q in np.unique(cc // 224))
    if (cc < 448).any():
        out.add(5)
    if (cc >= 448).any():
        out.add(6)
    return out


def s_prep_rwkv(p, S, d, t0, W, is_ctx, rev, zt, stage):
    cst = S.cst
    H = 0 if is_ctx else 64
    lo, hi = (0, LCTX) if is_ctx else (LCTX, LSEQ)
    a0_ = max(lo, t0 - H)
    a1_ = min(hi, t0 + W + H)
    rowmap = {"r": R_RWR, "k": R_RWK, "v": R_RWV, "wa": R_RWWA, "g": R_RWG}
    zm = S.zm
    for rg in ("r", "k", "v", "wa", "g"):
        z = zt[rg]
        full = (a0_ == t0 - H) and (a1_ == t0 + W + H)
        if not full:
            p.op("pool", lambda e, z=z: e.memset(z[:, 0:W + 2 * H], 0.0), writes=[z])
        off = a0_ - (t0 - H)
        n = a1_ - a0_
        if rev:
            off = (t0 + W + H) - a1_
        s_load(p, S, z, 64, rowmap[rg], a0_, n, rev, stage, col0=off)
        cf = S.coef[rg]
        o = zm[rg]
        iL, iR, iU, iD, iP, iN = (2, 1, 4, 3, 6, 5) if rev else (1, 2, 3, 4, 5, 6)
        p.op("dve", lambda e, z=z, o=o, cf=cf: e.tensor_scalar(out=o[:, 0:W], in0=z[:, H:H + W], scalar1=cf[:, 0:1], scalar2=None,
                                                               op0=ALU.mult), reads=[z, cf], writes=[o])

        def fma(oap, zap, ci, z=z, o=o, cf=cf):
            p.op("dve", lambda e: e.scalar_tensor_tensor(out=oap, in0=zap, scalar=cf[:, ci:ci + 1], in1=oap,
                                                         op0=ALU.mult, op1=ALU.add), reads=[z, o, cf], writes=[o])
        act_ = shift_active(getattr(S, "slot", None), rg)
        if is_ctx:
            if iP in act_:
                fma(o[:, 1:W], z[:, 0:W - 1], iP)
            if iN in act_:
                fma(o[:, 0:W - 1], z[:, 1:W], iN)
        else:
            if iU in act_:
                fma(o[:, 0:W], z[:, H - 64:H - 64 + W], iU)
            if iD in act_:
                fma(o[:, 0:W], z[:, H + 64:H + 64 + W], iD)
            o3 = o[:, 0:W].rearrange("p (r j) -> p r j", j=64)
            z3 = z[:, H:H + W].rearrange("p (r j) -> p r j", j=64)
            if iL in act_:
                fma(o3[:, :, 1:64], z3[:, :, 0:63], iL)
            if iR in act_:
                fma(o3[:, :, 0:63], z3[:, :, 1:64], iR)
    r, k, v, wa, gl = zm["r"], zm["k"], zm["v"], zm["wa"], zm["g"]
    nch = W // CH
    p.op("act", lambda e: e.activation(out=wa[0:32, 0:W], in_=wa[0:32, 0:W], func=AF.Tanh), reads=[wa], writes=[wa])
    WA = pkc(S, "WA").rearrange("p (d c) -> p d c", d=2)
    psw, psa = S.psS[0], S.psS[1]
    pswf = psw[0:64].rearrange("p a b -> p (a b)")
    psaf = psa[0:64].rearrange("p a b -> p (a b)")
    _mm(p, psw, pswf[:, 0:W], S.pkb, WA[0:32, d, :], wa, wa[0:32, 0:W], True, True)
    _mm(p, psa, psaf[:, 0:W], S.pkb, WA[32:64, d, :], wa, wa[32:64, 0:W], True, True)
    LW, QR, KD, AV, BV = S.LW, S.QR, S.KD, S.AV, S.BV
    asig = S.ex
    p.op("act", lambda e: e.activation(out=LW[:, 0:W], in_=pswf[:, 0:W], func=AF.Sigmoid, bias=pkc(S, "w0")[:, d:d + 1], scale=1.0),
         reads=[psw, S.pkb], writes=[LW])
    p.op("dve", lambda e: e.tensor_scalar(out=LW[:, 0:W], in0=LW[:, 0:W], scalar1=-float(np.exp(-0.5)), scalar2=None, op0=ALU.mult),
         reads=[LW], writes=[LW])
    p.op("act", lambda e: e.activation(out=asig[:, 0:W], in_=psaf[:, 0:W], func=AF.Sigmoid, bias=pkc(S, "a0")[:, d:d + 1], scale=1.0),
         reads=[psa, S.pkb], writes=[asig])
    kk, t2 = S.cs, S.tmp
    p.op("dve", lambda e: e.tensor_scalar(out=kk[:, 0:W], in0=k[:, 0:W], scalar1=pkc(S, "k_k"), scalar2=None, op0=ALU.mult),
         reads=[k, S.pkb], writes=[kk])
    p.op("dve", lambda e: e.tensor_tensor(out=t2[:, 0:W], in0=kk[:, 0:W], in1=kk[:, 0:W], op=ALU.mult), reads=[kk], writes=[t2])
    psk = S.psY
    _mm(p, psk, psk[:, 0:W], S.ones64, S.ones64[:], t2, t2[:, 0:W], True, True)
    emit_rsqrt(p, t2, psk, W, 1e-12, parts=64, fast=True)
    p.op("dve", lambda e: e.tensor_tensor(out=kk[:, 0:W], in0=kk[:, 0:W], in1=t2[:, 0:W], op=ALU.mult), reads=[kk, t2], writes=[kk])
    p.op("dve", lambda e: e.tensor_scalar(out=AV[:, 0:W], in0=kk[:, 0:W], scalar1=-1.0, scalar2=None, op0=ALU.mult),
         reads=[kk], writes=[AV])
    p.op("dve", lambda e: e.tensor_tensor(out=BV[:, 0:W], in0=kk[:, 0:W], in1=asig[:, 0:W], op=ALU.mult), reads=[kk, asig], writes=[BV])
    p.op("dve", lambda e: e.tensor_scalar(out=t2[:, 0:W], in0=asig[:, 0:W], scalar1=pkc(S, "k_a"), scalar2=S.der[:, 0:1],
                                          op0=ALU.mult, op1=ALU.add), reads=[asig, S.pkb, S.der], writes=[t2])
    p.op("dve", lambda e: e.tensor_tensor(out=KD[:, 0:W], in0=k[:, 0:W], in1=t2[:, 0:W], op=ALU.mult), reads=[k, t2], writes=[KD])
    p.op("pool", lambda e: e.tensor_copy(out=QR[:, 0:W], in_=r[:, 0:W]), reads=[r], writes=[QR])
    p.op("pool", lambda e: e.tensor_copy(out=S.Vp[:, 0:nch, 1, :], in_=v[:, 0:W].rearrange("p (c j) -> p c j", j=CH)),
         reads=[v], writes=[S.Vp])
    if not rev:
        aux = S.auxo
        p.dma("act", S.aux_d[0][:, t0:t0 + W], v[:, 0:W], reads=[v])
        p.op("act", lambda e: e.activation(out=gl[:, 0:W], in_=gl[:, 0:W], func=AF.Sigmoid), reads=[gl], writes=[gl])
        _mm(p, psw, pswf[:, 0:W], S.pkb, pkc(S, "g_up"), gl, gl[:, 0:W], True, True)
        p.op("act", lambda e: e.activation(out=aux[0][:, 0:W], in_=pswf[:, 0:W], func=AF.Copy), reads=[psw], writes=[aux[0]])
        p.dma("act", S.aux_d[1][:, t0:t0 + W], aux[0][:, 0:W], reads=[aux[0]])
        p.op("dve", lambda e: e.scalar_tensor_tensor(out=aux[1][:, 0:W], in0=r[:, 0:W], scalar=pkc(S, "r_k"), in1=k[:, 0:W],
                                                     op0=ALU.mult, op1=ALU.mult), reads=[r, k, S.pkb], writes=[aux[1]])
        _mm(p, psa, psaf[:, 0:W], S.ones64, S.ones64[:], aux[1], aux[1][:, 0:W], True, True)
        p.op("act", lambda e: e.activation(out=aux[1][:, 0:W], in_=psaf[:, 0:W], func=AF.Copy), reads=[psa], writes=[aux[1]])
        p.dma("act", S.aux_d[2][:, t0:t0 + W], aux[1][:, 0:W], reads=[aux[1]])
    return dict(QR=QR, KD=KD, V=S.Vp, LW=LW, AV=AV, BV=BV)


def s_rope(p, S, x, t0, W, rev):
    cst = S.cst
    nr = W // 64
    r0 = (t0 - LCTX) // 64
    ps = S.psS[0]
    psf = ps[0:64].rearrange("p a b -> p (a b)")
    _mm(p, ps, psf[:, 0:W], cst, cst[0:64, C_PR:C_PR + 64], x, x[:, 0:W], True, True)
    xr = S.tmp
    sl = slice(None, None, -1) if rev else slice(None)
    for (pp, tabc, tabs) in ((slice(0, 32), cst[0:32, C_ROPE + r0:C_ROPE + r0 + nr][:, sl].unsqueeze(2).to_broadcast([32, nr, 64]),
                              cst[0:32, C_ROPE + 256 + r0:C_ROPE + 256 + r0 + nr][:, sl].unsqueeze(2).to_broadcast([32, nr, 64])),
                             (slice(32, 64), cst[32:64, C_ROPE:C_ROPE + 64][:, sl].unsqueeze(1).to_broadcast([32, nr, 64]),
                              cst[32:64, C_ROPE + 64:C_ROPE + 128][:, sl].unsqueeze(1).to_broadcast([32, nr, 64]))):
        x3 = x[pp, 0:W].rearrange("p (r j) -> p r j", j=64)
        xr3 = xr[pp, 0:W].rearrange("p (r j) -> p r j", j=64)
        ps3 = psf[pp, 0:W].rearrange("p (r j) -> p r j", j=64)
        p.op("dve", lambda e, xr3=xr3, ps3=ps3, tabs=tabs: e.tensor_tensor(out=xr3, in0=ps3, in1=tabs, op=ALU.mult),
             reads=[ps, cst], writes=[xr])
        p.op("dve", lambda e, x3=x3, tabc=tabc: e.tensor_tensor(out=x3, in0=x3, in1=tabc, op=ALU.mult), reads=[x, cst], writes=[x])
    p.op("dve", lambda e: e.tensor_tensor(out=x[:, 0:W], in0=x[:, 0:W], in1=xr[:, 0:W], op=ALU.add), reads=[x, xr], writes=[x])


def s_prep_ret(p, S, d, t0, W, is_ctx, rev, stage):
    QR, KD, LW = S.QR, S.KD, S.LW
    nch = W // CH
    s_load(p, S, QR, 64, R_RETQ, t0, W, rev, stage)
    s_load(p, S, KD, 64, R_RETK, t0, W, rev, stage)
    v = S.AV
    s_load(p, S, v, 64, R_RETV, t0, W, rev, stage)
    p.op("pool", lambda e: e.tensor_copy(out=S.Vp[:, 0:nch, 1, :], in_=v[:, 0:W].rearrange("p (c j) -> p c j", j=CH)),
         reads=[v], writes=[S.Vp])
    if not is_ctx:
        s_rope(p, S, QR, t0, W, rev)
        s_rope(p, S, KD, t0, W, rev)
    p.op("dve", lambda e: e.tensor_scalar(out=KD[:, 0:W], in0=KD[:, 0:W], scalar1=0.125, scalar2=None, op0=ALU.mult),
         reads=[KD], writes=[KD])
    p.op("dve", lambda e: e.tensor_scalar(out=LW[:, 0:W], in0=QR[:, 0:W], scalar1=0.0, scalar2=S.der[:, 1 + d:2 + d],
                                          op0=ALU.mult, op1=ALU.add), reads=[QR, S.der], writes=[LW])
    return dict(QR=QR, KD=KD, V=S.Vp, LW=LW)


def s_prep_gla(p, S, d, t0, W, is_ctx, rev, stage):
    QR, KD, LW = S.QR, S.KD, S.LW
    nch = W // CH
    s_load(p, S, QR, 32, R_GLAQ, t0, W, rev, stage)
    s_load(p, S, KD, 32, R_GLAK, t0, W, rev, stage)
    v = S.AV
    s_load(p, S, v, 64, R_GLAV, t0, W, rev, stage)
    al = S.BV
    s_load(p, S, al, 16, R_GLAA, t0, W, rev, stage)
    p.op("pool", lambda e: e.tensor_copy(out=S.Vp[:, 0:nch, 1, :], in_=v[:, 0:W].rearrange("p (c j) -> p c j", j=CH)),
         reads=[v], writes=[S.Vp])
    p.op("dve", lambda e: e.tensor_scalar(out=KD[0:32, 0:W], in0=KD[0:32, 0:W], scalar1=float(32 ** -0.5), scalar2=None, op0=ALU.mult),
         reads=[KD], writes=[KD])
    ps = S.psS[0]
    psf = ps[0:32].rearrange("p a b -> p (a b)")
    aup = pkc(S, "gla_aup", 16).rearrange("p (d c) -> p d c", d=2)
    _mm(p, ps, psf[:, 0:W], S.pkb, aup[:, d, :], al, al[0:16, 0:W], True, True)
    p.op("act", lambda e: e.activation(out=LW[0:32, 0:W], in_=psf[:, 0:W], func=AF.Exp, bias=S.der[0:32, 3 + d:4 + d], scale=-1.0),
         reads=[ps, S.der], writes=[LW])
    p.op("act", lambda e: e.activation(out=LW[0:32, 0:W], in_=LW[0:32, 0:W], func=AF.Ln, bias=1.0, scale=1.0), reads=[LW], writes=[LW])
    p.op("dve", lambda e: e.tensor_scalar(out=LW[0:32, 0:W], in0=LW[0:32, 0:W], scalar1=-1.0 / 16, scalar2=None, op0=ALU.mult),
         reads=[LW], writes=[LW])
    return dict(QR=QR, KD=KD, V=S.Vp, LW=LW)


def emit_s_scans(p, S, mixers=("rw", "ret", "gla")):
    with ExitStack() as es:
        s_alloc_scan(p, es, S)
        s_setup_scan(p, es, S)
        f = lambda shape, name: p.sb(es, shape, F32, name)
        zt = {rg: f([64, TT + 128], "z_" + rg) for rg in ("r", "k", "v", "wa", "g")}
        S.zm = {rg: f([64, TT], "zm_" + rg) for rg in ("r", "k", "v", "wa", "g")}
        stage = f([64, TT + 128], "stage")
        S.auxo = [f([64, TT], "auxo") for _ in range(2)]
        yo = [f([64, TT], "yo") for _ in range(2)]
        Hst = {m: f([64, 64], "H_" + m) for m in mixers}
        ny = 0
        for d in getattr(S, "dirs", (0, 1)):
            rev = d == 1
            for m in mixers:
                p.op("pool", lambda e, m=m: e.memset(Hst[m][:], 0.0), writes=[Hst[m]])
            for (t0, W, is_ctx) in s_tiles(rev):
                for m in mixers:
                    if m == "rw":
                        pre = s_prep_rwkv(p, S, d, t0, W, is_ctx, rev, zt, stage)
                        dk, dplr = 64, True
                    elif m == "ret":
                        pre = s_prep_ret(p, S, d, t0, W, is_ctx, rev, stage)
                        dk, dplr = 64, False
                    else:
                        pre = s_prep_gla(p, S, d, t0, W, is_ctx, rev, stage)
                        dk, dplr = 32, False
                    yb = yo[ny % 2]
                    ny += 1

                    def yout(psY, psY2f, psY2, yb=yb, m=m, d=d, t0=t0, W=W, rev=rev):
                        src = psY[:, 0:W][:, ::-1] if rev else psY[:, 0:W]
                        src2 = psY2f[:, 0:W][:, ::-1] if rev else psY2f[:, 0:W]
                        p.op("act", lambda e: e.activation(out=yb[:, 0:W], in_=src, func=AF.Copy), reads=[psY], writes=[yb])
                        p.op("dve", lambda e: e.tensor_tensor(out=yb[:, 0:W], in0=yb[:, 0:W], in1=src2, op=ALU.add),
                             reads=[yb, psY2], writes=[yb])
                        p.dma("act", S.ysc[(m, d)][:, t0:t0 + W], yb[:, 0:W], reads=[yb])
                    emit_chunkscan(p, S, pre, dk, W, dplr, Hst[m], yout)
        p.barrier()


def emit_s_post(p, S, do_s5=True, mixers=("rw", "ret", "gla")):
    with ExitStack() as es:
        f = lambda shape, name: p.sb(es, shape, F32, name)
        mean64 = f([64, 64], "mean64")
        p.op("pool", lambda e: e.memset(mean64[:], 1.0 / 64), writes=[mean64])
        yf = [f([64, TT], "yf") for _ in range(2)]
        yb = [f([64, TT], "yb") for _ in range(2)]
        ax = [[f([64, TT], "ax") for _ in range(3)] for _ in range(2)]
        sq = f([64, TT], "sq")
        rs = f([64, TT], "rs")
        ps1 = p.ps(es, [64, TT], F32, "ps1")
        ps2 = p.ps(es, [64, TT], F32, "ps2")
        k = 0
        for (t0, W, is_ctx) in s_tiles(False):
            for mi, m in enumerate(("s5",) * do_s5 + tuple(mixers)):
                a, b, x3 = yf[k % 2], yb[k % 2], ax[k % 2]
                k += 1
                p.dma("sp", a[:, 0:W], S.ysc[(m, 0)][:, t0:t0 + W], writes=[a])
                p.dma("sp", b[:, 0:W], S.ysc[(m, 1 if 1 in getattr(S, "dirs", (0, 1)) else 0)][:, t0:t0 + W], writes=[b])
                p.op("pool", lambda e, a=a, b=b: e.tensor_tensor(out=a[:, 0:W], in0=a[:, 0:W], in1=b[:, 0:W], op=ALU.add),
                     reads=[a, b], writes=[a])
                if m == "rw":
                    for i in range(3):
                        p.dma("sp", x3[i][:, 0:W], S.aux_d[i][:, t0:t0 + W], writes=[x3[i]])
                    _mm(p, ps1, ps1[:, 0:W], mean64, mean64[:], a, a[:, 0:W], True, True)
                    p.op("dve", lambda e, a=a: e.tensor_tensor(out=a[:, 0:W], in0=a[:, 0:W], in1=ps1[:, 0:W], op=ALU.subtract),
                         reads=[a, ps1], writes=[a])
                    p.op("act", lambda e, a=a: e.activation(out=sq[:, 0:W], in_=a[:, 0:W], func=AF.Square), reads=[a], writes=[sq])
                    _mm(p, ps2, ps2[:, 0:W], mean64, mean64[:], sq, sq[:, 0:W], True, True)
                    emit_rsqrt(p, rs, ps2, W, 64e-5, parts=64, fast=True)
                    p.op("dve", lambda e, a=a: e.tensor_tensor(out=a[:, 0:W], in0=a[:, 0:W], in1=rs[:, 0:W], op=ALU.mult),
                         reads=[a, rs], writes=[a])
                    p.op("dve", lambda e, a=a: e.tensor_scalar(out=a[:, 0:W], in0=a[:, 0:W], scalar1=pkc(S, "rw_ln_g"),
                                                               scalar2=pkc(S, "rw_ln_b"), op0=ALU.mult, op1=ALU.add),
                         reads=[a, S.pk], writes=[a])
                    p.op("pool", lambda e, x3=x3: e.tensor_tensor(out=x3[2][:, 0:W], in0=x3[2][:, 0:W], in1=x3[0][:, 0:W], op=ALU.mult),
                         reads=[x3[2], x3[0]], writes=[x3[2]])
                    p.op("dve", lambda e, a=a, x3=x3: e.tensor_tensor(out=a[:, 0:W], in0=a[:, 0:W], in1=x3[2][:, 0:W], op=ALU.add),
                         reads=[a, x3[2]], writes=[a])
                    p.op("dve", lambda e, a=a, x3=x3: e.tensor_tensor(out=a[:, 0:W], in0=a[:, 0:W], in1=x3[1][:, 0:W], op=ALU.mult),
                         reads=[a, x3[1]], writes=[a])
                    r0 = 64
                elif m in ("ret", "gla"):
                    g = x3[0]
                    p.dma("sp", g[:, 0:W], S.zsrc(R_RETG if m == "ret" else R_GLAG, 64, t0, W), writes=[g])
                    p.op("act", lambda e, a=a: e.activation(out=sq[:, 0:W], in_=a[:, 0:W], func=AF.Square), reads=[a], writes=[sq])
                    _mm(p, ps1, ps1[:, 0:W], mean64, mean64[:], sq, sq[:, 0:W], True, True)
                    emit_rsqrt(p, rs, ps1, W, 1e-6, parts=64, fast=True)
                    p.op("dve", lambda e, a=a, m=m: e.scalar_tensor_tensor(out=a[:, 0:W], in0=a[:, 0:W], scalar=pkc(S, m + "_ln_g"),
                                                                           in1=rs[:, 0:W], op0=ALU.mult, op1=ALU.mult),
                         reads=[a, rs, S.pk], writes=[a])
                    p.op("act", lambda e, g=g, x3=x3: e.activation(out=x3[1][:, 0:W], in_=g[:, 0:W], func=AF.Sigmoid),
                         reads=[g], writes=[x3[1]])
                    p.op("pool", lambda e, g=g, x3=x3: e.tensor_tensor(out=g[:, 0:W], in0=g[:, 0:W], in1=x3[1][:, 0:W], op=ALU.mult),
                         reads=[g, x3[1]], writes=[g])
                    p.op("dve", lambda e, a=a, g=g: e.tensor_tensor(out=a[:, 0:W], in0=a[:, 0:W], in1=g[:, 0:W], op=ALU.mult),
                         reads=[a, g], writes=[a])
                    r0 = 128 if m == "ret" else 192
                else:
                    u = x3[0]
                    p.dma("sp", u[:, 0:W], S.zsrc(R_S5, 64, t0, W), writes=[u])
                    p.op("dve", lambda e, a=a, u=u: e.scalar_tensor_tensor(out=a[:, 0:W], in0=u[:, 0:W], scalar=pkc(S, "s5_d"),
                                                                           in1=a[:, 0:W], op0=ALU.mult, op1=ALU.add),
                         reads=[a, u, S.pk], writes=[a])
                    p.op("act", lambda e, a=a: e.activation(out=sq[:, 0:W], in_=a[:, 0:W], func=AF.Square), reads=[a], writes=[sq])
                    p.op("dve", lambda e: e.tensor_scalar(out=sq[:, 0:W], in0=sq[:, 0:W], scalar1=0.044715, scalar2=1.0,
                                                          op0=ALU.mult, op1=ALU.add), reads=[sq], writes=[sq])
                    p.op("dve", lambda e, a=a: e.tensor_tensor(out=sq[:, 0:W], in0=sq[:, 0:W], in1=a[:, 0:W], op=ALU.mult),
                         reads=[sq, a], writes=[sq])
                    p.op("act", lambda e: e.activation(out=sq[:, 0:W], in_=sq[:, 0:W], func=AF.Tanh, scale=0.7978845608028654),
                         reads=[sq], writes=[sq])
                    p.op("dve", lambda e: e.tensor_scalar(out=sq[:, 0:W], in0=sq[:, 0:W], scalar1=1.0, scalar2=0.5,
                                                          op0=ALU.add, op1=ALU.mult), reads=[sq], writes=[sq])
                    p.op("dve", lambda e, a=a: e.tensor_tensor(out=a[:, 0:W], in0=a[:, 0:W], in1=sq[:, 0:W], op=ALU.mult),
                         reads=[a, sq], writes=[a])
                    r0 = 0
                p.dma("act", S.yout_d[r0:r0 + 64, t0:t0 + W], a[:, 0:W], reads=[a])
        p.barrier()


def emit_s(p, S, do_s5=True, mixers=("rw", "ret", "gla")):
    with ExitStack() as es:
        S.cst = p.sb(es, [128, NCONST], F32, "cst")
        S.pk = p.sb(es, [128, NPK], F32, "pk")
        S.pkb = S.pk
        p.dma("sp", S.cst[:], S.cst_d[:], writes=[S.cst])
        p.dma("sp", S.pk[:], S.pk_d[:], writes=[S.pk])
        if do_s5:
            emit_s_s5(p, S)
        emit_s_scans(p, S, mixers)
        emit_s_post(p, S, do_s5, mixers)


def build_s(do_s5=True, mixers=("rw", "ret", "gla"), dirs=(0, 1), slot=None):
    nc = bass.Bass("TRN2", target_bir_lowering=False)
    with ExitStack() as es:
        p = Prog(nc, es)
        S = SCtx()
        S.dirs = dirs
        S.slot = slot
        z = p.dram("zslot", [ZROWS, LSEQ], F32, "ExternalInput")
        S.zsrc = lambda r0, nr, t0, nt: z[r0:r0 + nr, t0:t0 + nt]
        S.cst_d = p.dram("consts", [128, NCONST], F32, "ExternalInput")
        S.pk_d = p.dram("pk", [128, NPK], F32, "ExternalInput")
        S.ysc = {(m, d): p.dram("ysc_%s%d" % (m, d), [64, LSEQ], F32, "Internal") for m in ("s5", "rw", "ret", "gla") for d in range(2)}
        S.aux_d = [p.dram("aux%d" % i, [64, LSEQ], F32, "Internal") for i in range(3)]
        S.yout_d = p.dram("yslot", [256, LSEQ], F32, "ExternalOutput")
        emit_s(p, S, do_s5, mixers)
        p.finish()
    return nc


def slot_cols(j):
    a = np.arange
    rw, ret, gla = 256, 1152, 2176
    return np.concatenate([
        a(64 * j, 64 * j + 64),
        rw + a(64 * j, 64 * j + 64), rw + 256 + a(64 * j, 64 * j + 64), rw + 512 + a(64 * j, 64 * j + 64),
        rw + a(768, 832), rw + a(832, 896),
        ret + a(64 * j, 64 * j + 64), ret + 256 + a(64 * j, 64 * j + 64), ret + 512 + a(64 * j, 64 * j + 64),
        ret + 768 + a(64 * j, 64 * j + 64),
        gla + a(32 * j, 32 * j + 32), gla + 128 + a(32 * j, 32 * j + 32), gla + 256 + a(64 * j, 64 * j + 64),
        gla + 512 + a(64 * j, 64 * j + 64), gla + a(768, 784)])


TWO_PI = float(2 * np.pi)


def emit_sincos(p, ang, s_out, c_out, wk, f):
    ki, a, b, c, d = wk["ki"], wk["a"], wk["b"], wk["c"], wk["d"]
    V = lambda e_, fn, rd, wr: p.op(e_, fn, reads=rd, writes=wr)
    V("dve", lambda e: e.tensor_scalar(out=f(a), in0=f(ang), scalar1=float(1.0 / TWO_PI), scalar2=None, op0=ALU.mult), [ang], [a])
    V("dve", lambda e: e.tensor_copy(out=f(ki), in_=f(a)), [a], [ki])
    V("dve", lambda e: e.tensor_copy(out=f(a), in_=f(ki)), [ki], [a])
    V("dve", lambda e: e.scalar_tensor_tensor(out=f(a), in0=f(a), scalar=-TWO_PI, in1=f(ang), op0=ALU.mult, op1=ALU.add), [a, ang], [a])
    V("act", lambda e: e.activation(out=f(b), in_=f(a), func=AF.Sin, scale=0.25), [a], [b])
    V("act", lambda e: e.activation(out=f(c), in_=f(a), func=AF.Sin, scale=0.25, bias=float(np.pi / 2)), [a], [c])
    for _ in range(2):
        V("dve", lambda e: e.tensor_tensor(out=f(d), in0=f(b), in1=f(c), op=ALU.mult), [b, c], [d])
        V("dve", lambda e: e.tensor_tensor(out=f(b), in0=f(b), in1=f(b), op=ALU.mult), [b], [b])
        V("dve", lambda e: e.tensor_tensor(out=f(c), in0=f(c), in1=f(c), op=ALU.mult), [c], [c])
        V("dve", lambda e: e.tensor_tensor(out=f(c), in0=f(c), in1=f(b), op=ALU.subtract), [c, b], [c])
        V("dve", lambda e: e.tensor_scalar(out=f(b), in0=f(d), scalar1=2.0, scalar2=None, op0=ALU.mult), [d], [b])
    V("dve", lambda e: e.tensor_copy(out=f(s_out), in_=f(b)), [b], [s_out])
    V("dve", lambda e: e.tensor_copy(out=f(c_out), in_=f(c)), [c], [c_out])


def emit_s_s5(p, S):
    cst = S.cst
    with ExitStack() as es:
        f = lambda shape, name: p.sb(es, shape, F32, name)
        col = pkc(S, "s5col", 128).rearrange("p (g k) -> p g k", k=3)
        dtc, rho, th = f([128, 8], "dtc"), f([128, 8], "rho"), f([128, 8], "th")
        p.op("act", lambda e: e.activation(out=dtc[:], in_=col[:, :, 2], func=AF.Exp), reads=[S.pk], writes=[dtc])
        p.op("dve", lambda e: e.tensor_scalar(out=rho[:], in0=col[:, :, 0], scalar1=-1e-4, scalar2=None, op0=ALU.min), reads=[S.pk], writes=[rho])
        p.op("dve", lambda e: e.tensor_tensor(out=rho[:], in0=rho[:], in1=dtc[:], op=ALU.mult), reads=[rho, dtc], writes=[rho])
        p.op("act", lambda e: e.activation(out=rho[:], in_=rho[:], func=AF.Exp), reads=[rho], writes=[rho])
        p.op("dve", lambda e: e.tensor_tensor(out=th[:], in0=col[:, :, 1], in1=dtc[:], op=ALU.mult), reads=[S.pk, dtc], writes=[th])
        COS, SIN = f([128, 8, TT], "COS"), f([128, 8, TT], "SIN")
        ang = f([128, TT], "ang")
        wk = {n: f([128, TT], "wk" + n) for n in ("a", "b", "c", "d")}
        wk["ki"] = p.sb(es, [128, TT], mybir.dt.int32, "wkki")
        for gd in range(8):
            p.op("dve", lambda e, gd=gd: e.tensor_scalar(out=ang[:], in0=cst[:, C_JJ:C_JJ + TT], scalar1=th[:, gd:gd + 1], scalar2=None,
                                                         op0=ALU.mult), reads=[cst, th], writes=[ang])
            emit_sincos(p, ang, _Sl(SIN, gd), _Sl(COS, gd), wk, lambda b: b[:] if not isinstance(b, _Sl) else b.ap())
        row = pkc(S, "s5row", 64).rearrange("p (g k n) -> p g k n", k=3, n=64)
        Bt = pkc(S, "s5B", 64).rearrange("p (g k n) -> p g k n", k=2, n=64)
        R = lambda name: f([64, 8, 64], name)
        dtr, lr, x, ar, ai, den, cr, ci, t1, t2 = (R(n) for n in ("dtr", "lr", "x", "ar", "ai", "den", "cr", "ci", "t1", "t2"))
        V = lambda e_, fn, rd, wr: p.op(e_, fn, reads=rd, writes=wr)
        V("act", lambda e: e.activation(out=dtr[:], in_=row[:, :, 2, :], func=AF.Exp), [S.pk], [dtr])
        V("dve", lambda e: e.tensor_scalar(out=lr[:], in0=row[:, :, 0, :], scalar1=-1e-4, scalar2=None, op0=ALU.min), [S.pk], [lr])
        V("dve", lambda e: e.tensor_tensor(out=x[:], in0=lr[:], in1=dtr[:], op=ALU.mult), [lr, dtr], [x])
        V("act", lambda e: e.activation(out=x[:], in_=x[:], func=AF.Exp), [x], [x])
        V("dve", lambda e: e.tensor_tensor(out=t1[:], in0=row[:, :, 1, :], in1=dtr[:], op=ALU.mult), [S.pk, dtr], [t1])
        f3 = lambda b: (b[:] if b.t.shape[0] == 64 else b[0:64, 0:512].rearrange("p (g n) -> p g n", n=64))
        emit_sincos(p, t1, ai, ar, wk, f3)
        V("dve", lambda e: e.tensor_tensor(out=ar[:], in0=ar[:], in1=x[:], op=ALU.mult), [ar, x], [ar])
        V("dve", lambda e: e.tensor_tensor(out=ai[:], in0=ai[:], in1=x[:], op=ALU.mult), [ai, x], [ai])
        V("dve", lambda e: e.tensor_scalar(out=ar[:], in0=ar[:], scalar1=-1.0, scalar2=None, op0=ALU.add), [ar], [ar])
        lim = row[:, :, 1, :]
        V("dve", lambda e: e.tensor_tensor(out=den[:], in0=lr[:], in1=lr[:], op=ALU.mult), [lr], [den])
        V("dve", lambda e: e.tensor_tensor(out=t1[:], in0=lim, in1=lim, op=ALU.mult), [S.pk], [t1])
        V("dve", lambda e: e.tensor_tensor(out=den[:], in0=den[:], in1=t1[:], op=ALU.add), [den, t1], [den])
        V("dve", lambda e: e.reciprocal(out=den[:], in_=den[:]), [den], [den])
        V("dve", lambda e: e.tensor_tensor(out=cr[:], in0=ar[:], in1=lr[:], op=ALU.mult), [ar, lr], [cr])
        V("dve", lambda e: e.tensor_tensor(out=t1[:], in0=ai[:], in1=lim, op=ALU.mult), [ai, S.pk], [t1])
        V("dve", lambda e: e.tensor_tensor(out=cr[:], in0=cr[:], in1=t1[:], op=ALU.add), [cr, t1], [cr])
        V("dve", lambda e: e.tensor_tensor(out=cr[:], in0=cr[:], in1=den[:], op=ALU.mult), [cr, den], [cr])
        V("dve", lambda e: e.tensor_tensor(out=ci[:], in0=ai[:], in1=lr[:], op=ALU.mult), [ai, lr], [ci])
        V("dve", lambda e: e.tensor_tensor(out=t1[:], in0=ar[:], in1=lim, op=ALU.mult), [ar, S.pk], [t1])
        V("dve", lambda e: e.tensor_tensor(out=ci[:], in0=ci[:], in1=t1[:], op=ALU.subtract), [ci, t1], [ci])
        V("dve", lambda e: e.tensor_tensor(out=ci[:], in0=ci[:], in1=den[:], op=ALU.mult), [ci, den], [ci])
        LB, LW_ = f([64, 8, 128], "LB"), f([64, 8, 128], "LWs5")
        Bre, Bim = Bt[:, :, 0, :], Bt[:, :, 1, :]
        V("dve", lambda e: e.tensor_tensor(out=t1[:], in0=cr[:], in1=Bre, op=ALU.mult), [cr, S.pk], [t1])
        V("dve", lambda e: e.tensor_tensor(out=t2[:], in0=ci[:], in1=Bim, op=ALU.mult), [ci, S.pk], [t2])
        V("dve", lambda e: e.tensor_tensor(out=LB[:, :, 0:64], in0=t1[:], in1=t2[:], op=ALU.subtract), [t1, t2], [LB])
        V("dve", lambda e: e.tensor_tensor(out=t1[:], in0=cr[:], in1=Bim, op=ALU.mult), [cr, S.pk], [t1])
        V("dve", lambda e: e.tensor_tensor(out=t2[:], in0=ci[:], in1=Bre, op=ALU.mult), [ci, S.pk], [t2])
        V("dve", lambda e: e.tensor_tensor(out=LB[:, :, 64:128], in0=t1[:], in1=t2[:], op=ALU.add), [t1, t2], [LB])
        V("dve", lambda e: e.tensor_copy(out=LW_[:, :, 0:64], in_=LB[:, :, 64:128]), [LB], [LW_])
        V("dve", lambda e: e.tensor_scalar(out=LW_[:, :, 64:128], in0=LB[:, :, 0:64], scalar1=-1.0, scalar2=None, op0=ALU.mult), [LB], [LW_])
        Cm = f([128, 8, 64], "Cm")
        Cpk = pkc(S, "s5C", 128).rearrange("p (g c) -> p g c", c=64)
        V("dve", lambda e: e.tensor_copy(out=Cm[0:64], in_=Cpk[0:64]), [S.pk], [Cm])
        V("dve", lambda e: e.tensor_scalar(out=Cm[64:128], in0=Cpk[64:128], scalar1=-1.0, scalar2=None, op0=ALU.mult), [S.pk], [Cm])
        u_t = [f([64, TT], "u") for _ in range(2)]
        stage = f([64, TT], "stage5")
        cb = [f([128, TT], "c5") for _ in range(2)]
        tb = [f([128, TT], "t5") for _ in range(2)]
        gb = [f([128, TT], "g5") for _ in range(2)]
        hb = [f([128, TT], "h5") for _ in range(2)]
        yo = [f([64, TT], "yo5") for _ in range(2)]
        carry = f([128, 4], "carry")
        psB = [p.ps(es, [128, TT], F32, "psB") for _ in range(2)]
        psW = [p.ps(es, [128, TT], F32, "psW") for _ in range(2)]
        psG = [p.ps(es, [128, TT], F32, "psG") for _ in range(2)]
        psYs = p.ps(es, [64, TT], F32, "psYs")
        k = 0
        for d in getattr(S, "dirs", (0, 1)):
            rev = d == 1
            p.op("pool", lambda e: e.memset(carry[:], 0.0), writes=[carry])
            for ti, (t0, W, is_ctx) in enumerate(s_tiles(rev)):
                u = u_t[ti % 2]
                s_load(p, S, u, 64, R_S5, t0, W, rev, stage)
                for gl in range(4):
                    gd = gl * 2 + d
                    c5, t5, g5, h5 = cb[k % 2], tb[k % 2], gb[k % 2], hb[k % 2]
                    pB, pW, pG = psB[k % 2], psW[k % 2], psG[k % 2]
                    k += 1
                    _mm(p, pB, pB[:, 0:W], LB, LB[:, gd, :], u, u[:, 0:W], True, True)
                    _mm(p, pW, pW[:, 0:W], LW_, LW_[:, gd, :], u, u[:, 0:W], True, True)
                    V("dve", lambda e, c5=c5, pB=pB, gd=gd: e.tensor_tensor(out=c5[:, 0:W], in0=pB[:, 0:W], in1=COS[:, gd, 0:W], op=ALU.mult),
                      [pB, COS], [c5])
                    V("dve", lambda e, t5=t5, pW=pW, gd=gd: e.tensor_tensor(out=t5[:, 0:W], in0=pW[:, 0:W], in1=SIN[:, gd, 0:W], op=ALU.mult),
                      [pW, SIN], [t5])
                    V("pool", lambda e, c5=c5, t5=t5: e.tensor_tensor(out=c5[:, 0:W], in0=c5[:, 0:W], in1=t5[:, 0:W], op=ALU.add), [c5, t5], [c5])
                    V("dve", lambda e, g5=g5, c5=c5, gd=gd, gl=gl: e.tensor_tensor_scan(
                        out=g5[:, 0:W], data0=rho[:, gd:gd + 1].to_broadcast([128, W]), data1=c5[:, 0:W],
                        initial=carry[:, gl:gl + 1], op0=ALU.mult, op1=ALU.add), [rho, c5, carry], [g5])
                    _mm(p, pG, pG[:, 0:W], cst, cst[:, C_JS:C_JS + 128], g5, g5[:, 0:W], True, True)
                    V("pool", lambda e, h5=h5, g5=g5, gd=gd: e.tensor_tensor(out=h5[:, 0:W], in0=g5[:, 0:W], in1=COS[:, gd, 0:W], op=ALU.mult),
                      [g5, COS], [h5])
                    V("dve", lambda e, t5=t5, pG=pG, gd=gd: e.tensor_tensor(out=t5[:, 0:W], in0=pG[:, 0:W], in1=SIN[:, gd, 0:W], op=ALU.mult),
                      [pG, SIN], [t5])
                    V("dve", lambda e, h5=h5, t5=t5: e.tensor_tensor(out=h5[:, 0:W], in0=h5[:, 0:W], in1=t5[:, 0:W], op=ALU.add), [h5, t5], [h5])
                    V("act", lambda e, h5=h5, gl=gl: e.activation(out=carry[:, gl:gl + 1], in_=h5[:, W - 1:W], func=AF.Copy), [h5], [carry])
                    _mm(p, psYs, psYs[:, 0:W], Cm, Cm[:, gd, :], h5, h5[:, 0:W], gl == 0, gl == 3)
                yb = yo[ti % 2]
                src = psYs[:, 0:W][:, ::-1] if rev else psYs[:, 0:W]
                V("act", lambda e, yb=yb, src=src: e.activation(out=yb[:, 0:W], in_=src, func=AF.Copy), [psYs], [yb])
                p.dma("act", S.ysc[("s5", d)][:, t0:t0 + W], yb[:, 0:W], reads=[yb])
        p.barrier()


class _Sl:
    def __init__(self, buf, gd):
        self.__dict__["b"] = buf
        self.__dict__["gd"] = gd

    def ap(self):
        return self.b[:, self.gd, :]

    def __getattr__(self, k):
        return getattr(self.b, k)

    def __setattr__(self, k, v):
        setattr(self.b, k, v)


DEPTH = 4
ZALL = 4 * ZROWS


def dense_tiles(T):
    return [(0, LCTX, 1)] + [(LCTX + i * T, T, 0) for i in range(LLAT // T)]


def build_full(depth=DEPTH, slots=(0, 1, 2, 3)):
    nc = bass.Bass("TRN2", target_bir_lowering=False)
    with ExitStack() as es:
        p = Prog(nc, es)
        ein = lambda name, shape: p.dram(name, shape, F32, "ExternalInput")
        xT = ein("xT", [D, LSEQ])
        cond = ein("cond", [128, 8, 2])
        cst_d = ein("consts", [128, NCONST])
        fin = ein("final_g", [128, 8])
        fo = p.dram("fo", [D, LLAT], F32, "ExternalOutput")
        z_d = p.dram("z_scr", [ZALL, LSEQ], F32, "Internal")
        y_d = p.dram("y_scr", [D, LSEQ], F32, "Internal")
        x_d = p.dram("x_scr", [D, LSEQ], F32, "Internal")
        x1_d = p.dram("x1_scr", [D, LSEQ], F32, "Internal")
        ysc = {(m, d): p.dram("ysc_%s%d" % (m, d), [64, LSEQ], F32, "Internal") for m in ("s5", "rw", "ret", "gla") for d in range(2)}
        aux_d = [p.dram("aux%d" % i, [64, LSEQ], F32, "Internal") for i in range(3)]
        for L in range(depth):
            sfx = "_%d" % L
            adaw, adab = ein("ada_w" + sfx, [D, 6 * D]), ein("ada_b" + sfx, [128, 48])
            g1n, g2n = ein("norm1_g" + sfx, [128, 8]), ein("norm2_g" + sfx, [128, 8])
            win = ein("w_in" + sfx, [D, ZALL])
            wout = ein("w_out" + sfx, [D, D])
            gluw, glub = ein("glu_w" + sfx, [64, 4, 256]), ein("glu_b" + sfx, [64, 4])
            w1, w2 = ein("w1" + sfx, [D, DFF]), ein("w2" + sfx, [DFF, D])
            pks = [ein("pk%s_%d" % (sfx, j), [128, NPK]) for j in range(4)]
            xin = xT if L == 0 else x_d
            emit_d1(p, xin, cond, adaw, adab, g1n, win, z_d, tl=dense_tiles(512), ncol=ZALL)
            for j in slots:
                S = SCtx()
                S.slot = j
                S.zsrc = lambda r0, nr, t0, nt, j=j: z_d[j * ZROWS + r0:j * ZROWS + r0 + nr, t0:t0 + nt]
                S.cst_d, S.pk_d, S.ysc, S.aux_d = cst_d, pks[j], ysc, aux_d
                S.yout_d = Buf(y_d[j * 256:(j + 1) * 256, :], "yslot%d" % j)
                emit_s(p, S)
            last = L == depth - 1
            emit_d2(p, xin, y_d, cond, adaw, adab, g2n, wout, gluw, glub, w1, w2, x1_d, x_d,
                    fin if last else None, fo if last else None, tl=dense_tiles(256), slotmajor=True, fo_off=LCTX)
        p.finish()
    return nc


def _fm(v):
    return np.ascontiguousarray(np.asarray(v, np.float32).reshape(8, 128).T)


def prep_inputs(inp, b, depth=DEPTH):
    f32 = lambda a: np.ascontiguousarray(np.asarray(a, np.float32))
    m = {}
    m["xT"] = f32(np.concatenate([inp["ctx"][b], inp["x"][b]], 0).T)
    cc = np.stack([inp["c"][b], inp["c_ctx"]], -1)
    m["cond"] = f32(cc.reshape(8, 128, 2).transpose(1, 0, 2))
    m["consts"] = make_consts()
    m["final_g"] = _fm(inp["final_g"])
    allcols = np.concatenate([slot_cols(j) for j in range(4)])
    r = np.arange(1024)
    orig = ((r % 256) // 64) * 256 + (r // 256) * 64 + (r % 64)
    for L in range(depth):
        sfx = "_%d" % L
        m["ada_w" + sfx] = f32(inp["ada_w"][L])
        m["ada_b" + sfx] = f32(inp["ada_b"][L].reshape(48, 128).T)
        m["norm1_g" + sfx] = _fm(inp["norm1_g"][L])
        m["norm2_g" + sfx] = _fm(inp["norm2_g"][L])
        m["w_in" + sfx] = f32(inp["w_in"][L][:, allcols])
        m["w_out" + sfx] = f32(inp["w_out"][L][orig, :])
        m["glu_w" + sfx] = f32(inp["s5_glu_w"][L].reshape(4, 64, 256).transpose(1, 0, 2))
        m["glu_b" + sfx] = f32(inp["s5_glu_b"][L].reshape(4, 64).T)
        m["w1" + sfx] = f32(inp["mlp_w1"][L])
        m["w2" + sfx] = f32(inp["mlp_w2"][L])
        for j in range(4):
            m["pk%s_%d" % (sfx, j)] = pack_params(inp, L, j)
    return m


_NC_CACHE = {}


def kernel(**inputs):
    inp = {k: np.asarray(v) for k, v in inputs.items()}
    if "full" not in _NC_CACHE:
        _NC_CACHE["full"] = build_full()
    nc = _NC_CACHE["full"]
    in_maps = [prep_inputs(inp, b) for b in range(2)]
    res = run_bass_kernel_spmd(nc, in_maps, core_ids=[0, 1])
    out = np.stack([np.ascontiguousarray(res.results[b]["fo"].T) for b in range(2)], 0)
    return out.astype(np.float32)
```

```python
import numpy as np
from contextlib import ExitStack
import concourse.bass as bass
import concourse.mybir as mybir
from concourse.bass_utils import run_bass_kernel_spmd

F32 = mybir.dt.float32
BF16 = mybir.dt.bfloat16
AF = mybir.ActivationFunctionType
ALU = mybir.AluOpType

D = 1024
NLAT = 4096
NCTX = 64
NT = NLAT + NCTX
IN_TOTAL = 2960
DFF = 4096
EPS = 1e-6


class Buf:
    __slots__ = ("t", "w", "r", "name")

    def __init__(self, t, name=""):
        self.t = t
        self.w = None
        self.r = {}
        self.name = name

    def __getitem__(self, idx):
        return self.t[idx]


class Prog:
    ENG = ("pe", "dve", "act", "pool", "sp")

    def __init__(self, nc, es, n_dma_sems=32):
        self.nc = nc
        self.eng = {"pe": nc.tensor, "dve": nc.vector, "act": nc.scalar, "pool": nc.gpsimd, "sp": nc.sync}
        self.sem = {}
        self.cnt = {}
        for e in self.ENG:
            self.sem[e] = es.enter_context(nc.semaphore("sem_" + e))
            self.cnt[e] = 0
        self.dsem = [es.enter_context(nc.semaphore("dsem%d" % i)) for i in range(n_dma_sems)]
        self.dcnt = [0] * n_dma_sems
        self.dnext = 0
        self.waited = {e: {} for e in self.ENG}
        self.nbuf = 0

    def sb(self, es, shape, dtype=F32, name=None):
        self.nbuf += 1
        name = (name or "sb") + "_%d" % self.nbuf
        return Buf(es.enter_context(self.nc.sbuf_tensor(name, list(shape), dtype)), name)

    def ps(self, es, shape, dtype=F32, name=None):
        self.nbuf += 1
        name = (name or "ps") + "_%d" % self.nbuf
        return Buf(es.enter_context(self.nc.psum_tensor(name, list(shape), dtype)), name)

    def dram(self, name, shape, dtype=F32, kind="Internal"):
        return Buf(self.nc.dram_tensor(name, list(shape), dtype, kind=kind).ap(), name)

    def _semh(self, key):
        return self.sem[key] if isinstance(key, str) else self.dsem[key[1]]

    def _wait(self, e, tok):
        key, val = tok
        if self.waited[e].get(key, 0) >= val:
            return
        if key == e and e in ("pe",):
            return
        self.eng[e].wait_ge(self._semh(key), val)
        self.waited[e][key] = val

    def _deps(self, reads, writes):
        toks = []
        for b in reads:
            if b.w is not None:
                toks.append(b.w)
        for b in writes:
            if b.w is not None:
                toks.append(b.w)
            toks.extend(b.r.items())
        return toks

    def _mark(self, tok, reads, writes):
        key, val = tok
        for b in reads:
            if b.r.get(key, 0) < val:
                b.r[key] = val
        for b in writes:
            b.w = tok
            b.r = {}

    def op(self, e, fn, reads=(), writes=()):
        for tok in self._deps(reads, writes):
            self._wait(e, tok)
        ins = fn(self.eng[e])
        self.cnt[e] += 1
        ins.then_inc(self.sem[e], 1)
        self._mark((e, self.cnt[e]), reads, writes)

    def dma(self, q, out, in_, reads=(), writes=()):
        i = self.dnext
        self.dnext = (i + 1) % len(self.dsem)
        if self.dcnt[i] > 0:
            self._wait(q, (("d", i), self.dcnt[i] * 16))
        for tok in self._deps(reads, writes):
            self._wait(q, tok)
        ins = self.eng[q].dma_start(out=out, in_=in_)
        self.dcnt[i] += 1
        ins.then_inc(self.dsem[i], 16)
        self._mark((("d", i), self.dcnt[i] * 16), reads, writes)

    def barrier(self):
        for e in self.ENG:
            for e2 in self.ENG:
                if e2 != e and self.cnt[e2] > 0:
                    self._wait(e, (e2, self.cnt[e2]))
            for i in range(len(self.dsem)):
                if self.dcnt[i] > 0:
                    self._wait(e, (("d", i), self.dcnt[i] * 16))

    def finish(self):
        for i in range(len(self.dsem)):
            if self.dcnt[i] > 0:
                self._wait("sp", (("d", i), self.dcnt[i] * 16))
        for e2 in self.ENG:
            if e2 != "sp" and self.cnt[e2] > 0:
                self._wait("sp", (e2, self.cnt[e2]))


def _mm(p, out_buf, out_ap, lhsT_buf, lhsT_ap, rhs_buf, rhs_ap, start, stop, extra_reads=()):
    p.op("pe", lambda e: e.matmul(out_ap, lhsT=lhsT_ap, rhs=rhs_ap, start=start, stop=stop),
         reads=[lhsT_buf, rhs_buf] + list(extra_reads), writes=[out_buf])


def emit_mod(p, es, cond_d, adaw_d, adab_d, chunks, psum):
    n = len(chunks)
    modT = p.sb(es, [128, n * 8, 2], F32, "modT")
    with ExitStack() as es2:
        cond = p.sb(es2, [128, 8, 2], F32, "cond")
        sig = p.sb(es2, [128, 8, 2], F32, "sig")
        scond = p.sb(es2, [128, 8, 2], F32, "scond")
        adab = p.sb(es2, [128, 48], F32, "adab")
        stg = [p.sb(es2, [128, 8, 512], F32, "adastg") for _ in range(2)]
        p.dma("sp", cond[:], cond_d[:], reads=[cond_d], writes=[cond])
        p.dma("sp", adab[:], adab_d[:], reads=[adab_d], writes=[adab])
        p.op("act", lambda e: e.activation(out=sig[:], in_=cond[:], func=AF.Sigmoid), reads=[cond], writes=[sig])
        p.op("dve", lambda e: e.tensor_tensor(out=scond[:], in0=cond[:], in1=sig[:], op=ALU.mult),
             reads=[cond, sig], writes=[scond])
        adaw_v = adaw_d.t.rearrange("(kc p) n -> p kc n", p=128)
        k = 0
        for ci, a in enumerate(chunks):
            for half in range(2):
                st = stg[k % 2]
                k += 1
                c0 = a * 1024 + half * 512
                p.dma("sp", st[:], adaw_v[:, :, c0:c0 + 512], reads=[adaw_d], writes=[st])
                for cc in range(4):
                    idx = ci * 8 + half * 4 + cc
                    for kc in range(8):
                        _mm(p, psum, psum[:, idx, :], st, st[:, kc, cc * 128:(cc + 1) * 128],
                            scond, scond[:, kc, :], kc == 0, kc == 7)
        for ci, a in enumerate(chunks):
            p.op("dve", lambda e, ci=ci, a=a: e.tensor_tensor(
                out=modT[:, ci * 8:(ci + 1) * 8, :], in0=psum[:, ci * 8:(ci + 1) * 8, :],
                in1=adab[:, a * 8:(a + 1) * 8].unsqueeze(2).to_broadcast([128, 8, 2]), op=ALU.add),
                reads=[psum, adab], writes=[modT])
        p.barrier()
    return modT


def load_weight_bf16(p, es_stage, w_d, wbf, K, N, piece):
    kc_n = K // 128
    w_v = w_d.t.rearrange("(kc p) n -> p kc n", p=128)
    stg = [p.sb(es_stage, [128, kc_n, piece], F32, "wstg") for _ in range(2)]
    k = 0
    for c0 in range(0, N, piece):
        w = min(piece, N - c0)
        st = stg[k % 2]
        k += 1
        p.dma("sp", st[:, :, 0:w], w_v[:, :, c0:c0 + w], reads=[w_d], writes=[st])
        p.op("pool", lambda e, st=st, c0=c0, w=w: e.tensor_copy(out=wbf[:, :, c0:c0 + w], in_=st[:, :, 0:w]),
             reads=[st], writes=[wbf])


def emit_rsqrt(p, out, src, T, eps, parts=128, fast=False):
    if fast:
        p.op("act", lambda e: e.activation(out=out[0:parts, 0:T], in_=src[0:parts, 0:T], func=AF.Ln, bias=float(eps), scale=1.0),
             reads=[src], writes=[out])
        p.op("act", lambda e: e.activation(out=out[0:parts, 0:T], in_=out[0:parts, 0:T], func=AF.Exp, scale=-0.5),
             reads=[out], writes=[out])
        return
    p.op("act", lambda e: e.activation(out=out[0:parts, 0:T], in_=src[0:parts, 0:T], func=AF.Sqrt, bias=float(eps), scale=1.0),
         reads=[src], writes=[out])
    p.op("dve", lambda e: e.reciprocal(out=out[0:parts, 0:T], in_=out[0:parts, 0:T]), reads=[out], writes=[out])


def emit_norm_mod(p, x, T, sq, ssum, rstd, hn, hbf, ones, gsc, sh, sh_off, col):
    p.op("act", lambda e: e.activation(out=sq[:, :, 0:T], in_=x[:, :, 0:T], func=AF.Square), reads=[x], writes=[sq])
    for c in range(8):
        _mm(p, ssum, ssum[:, 0:T], ones, ones[:], sq, sq[:, c, 0:T], c == 0, c == 7)
    emit_rsqrt(p, rstd, ssum, T, EPS)
    p.op("dve", lambda e: e.tensor_tensor(out=hn[:, :, 0:T], in0=x[:, :, 0:T],
                                          in1=rstd[:, 0:T].unsqueeze(1).to_broadcast([128, 8, T]), op=ALU.mult),
         reads=[x, rstd], writes=[hn])
    for c in range(8):
        p.op("act", lambda e, c=c: e.activation(out=hbf[:, c, 0:T], in_=hn[:, c, 0:T], func=AF.Identity,
                                                bias=sh[:, sh_off + c, col:col + 1], scale=gsc[:, c, col:col + 1]),
             reads=[hn, gsc, sh], writes=[hbf])


def tiles_of(T):
    tl = [(t0, T, 0) for t0 in range(0, NLAT, T)]
    tl.append((NLAT, NCTX, 1))
    return tl


def emit_d1(p, xT_d, cond_d, adaw_d, adab_d, g_d, win_d, zT_d, tl=None, ncol=IN_TOTAL):
    T = 512
    tl = tl or tiles_of(T)
    nchunk = (ncol + 127) // 128
    with ExitStack() as es:
        psA = p.ps(es, [128, 16, 2], F32, "psmod")
        modT = emit_mod(p, es, cond_d, adaw_d, adab_d, [0, 1], psA)
        gam = p.sb(es, [128, 8], F32, "gam")
        gsc = p.sb(es, [128, 8, 2], F32, "gsc")
        ones = p.sb(es, [128, 128], BF16, "ones")
        wbf = p.sb(es, [128, 8, ncol], BF16, "winbf")
        p.dma("sp", gam[:], g_d[:], reads=[g_d], writes=[gam])
        p.op("pool", lambda e: e.memset(ones[:], 1.0 / D), writes=[ones])
        p.op("dve", lambda e: e.tensor_scalar(out=gsc[:], in0=modT[:, 8:16, :], scalar1=1.0, scalar2=None, op0=ALU.add),
             reads=[modT], writes=[gsc])
        p.op("dve", lambda e: e.tensor_tensor(out=gsc[:], in0=gsc[:], in1=gam[:].unsqueeze(2).to_broadcast([128, 8, 2]),
                                              op=ALU.mult), reads=[gsc, gam], writes=[gsc])
        with ExitStack() as es2:
            load_weight_bf16(p, es2, win_d, wbf, D, ncol, 424)
            p.barrier()
        xs = [p.sb(es, [128, 8, T], F32, "x") for _ in range(2)]
        sq = p.sb(es, [128, 8, T], BF16, "sq")
        hn = p.sb(es, [128, 8, T], F32, "hn")
        hbf = [p.sb(es, [128, 8, T], BF16, "hbf") for _ in range(2)]
        rstd = p.sb(es, [128, T], F32, "rstd")
        zo = [p.sb(es, [128, T], F32, "zo") for _ in range(4)]
        ssum = p.ps(es, [128, T], F32, "ssum")
        pz = [p.ps(es, [128, T], F32, "pz") for _ in range(4)]
        xv = xT_d.t.rearrange("(c p) t -> p c t", p=128)
        nz = 0
        for ti, (t0, Tt, col) in enumerate(tl):
            x = xs[ti % 2]
            h = hbf[ti % 2]
            p.dma("sp", x[:, :, 0:Tt], xv[:, :, t0:t0 + Tt], reads=[xT_d], writes=[x])
            emit_norm_mod(p, x, Tt, sq, ssum, rstd, hn, h, ones, gsc, modT, 0, col)
            for cc in range(nchunk):
                c0 = cc * 128
                M = min(128, ncol - c0)
                pzb = pz[nz % 4]
                zob = zo[nz % 4]
                for kc in range(8):
                    _mm(p, pzb, pzb[0:M, 0:Tt], wbf, wbf[:, kc, c0:c0 + M], h, h[:, kc, 0:Tt], kc == 0, kc == 7)
                if nz % 2 == 0:
                    p.op("dve", lambda e, pzb=pzb, zob=zob, M=M, Tt=Tt: e.tensor_copy(out=zob[0:M, 0:Tt], in_=pzb[0:M, 0:Tt]),
                         reads=[pzb], writes=[zob])
                else:
                    p.op("act", lambda e, pzb=pzb, zob=zob, M=M, Tt=Tt: e.activation(out=zob[0:M, 0:Tt], in_=pzb[0:M, 0:Tt],
                                                                                     func=AF.Copy),
                         reads=[pzb], writes=[zob])
                p.dma("act", zT_d[c0:c0 + M, t0:t0 + Tt], zob[0:M, 0:Tt], reads=[zob])
                nz += 1
        p.barrier()


def build_d1():
    nc = bass.Bass("TRN2", target_bir_lowering=False)
    with ExitStack() as es:
        p = Prog(nc, es)
        xT = p.dram("xT", [D, NT], F32, "ExternalInput")
        cond = p.dram("cond", [128, 8, 2], F32, "ExternalInput")
        adaw = p.dram("ada_w", [D, 6 * D], F32, "ExternalInput")
        adab = p.dram("ada_b", [128, 48], F32, "ExternalInput")
        g = p.dram("norm_g", [128, 8], F32, "ExternalInput")
        win = p.dram("w_in", [D, IN_TOTAL], F32, "ExternalInput")
        zT = p.dram("zT", [IN_TOTAL, NT], F32, "ExternalOutput")
        emit_d1(p, xT, cond, adaw, adab, g, win, zT)
        p.finish()
    return nc


def emit_d2(p, xT_d, yT_d, cond_d, adaw_d, adab_d, g_d, wout_d, gluw_d, glub_d, w1_d, w2_d, x1_d, xo_d, fin_d, fo_d, tl=None, slotmajor=False, fo_off=0):
    T = 256
    tl = tl or tiles_of(T)
    with ExitStack() as es:
        psA = p.ps(es, [128, 32, 2], F32, "psmod")
        modT = emit_mod(p, es, cond_d, adaw_d, adab_d, [2, 3, 4, 5], psA)
        gam = p.sb(es, [128, 8], F32, "gam")
        gsc = p.sb(es, [128, 8, 2], F32, "gsc")
        ones = p.sb(es, [128, 128], BF16, "ones")
        p.dma("sp", gam[:], g_d[:], reads=[g_d], writes=[gam])
        p.op("pool", lambda e: e.memset(ones[:], 1.0 / D), writes=[ones])
        p.op("dve", lambda e: e.tensor_scalar(out=gsc[:], in0=modT[:, 16:24, :], scalar1=1.0, scalar2=None, op0=ALU.add),
             reads=[modT], writes=[gsc])
        p.op("dve", lambda e: e.tensor_tensor(out=gsc[:], in0=gsc[:], in1=gam[:].unsqueeze(2).to_broadcast([128, 8, 2]),
                                              op=ALU.mult), reads=[gsc, gam], writes=[gsc])
        xv = xT_d.t.rearrange("(c p) t -> p c t", p=128)
        yv = yT_d.t.rearrange("(c p) t -> p c t", p=128)
        x1v = x1_d.t.rearrange("(c p) t -> p c t", p=128)
        xov = xo_d.t.rearrange("(c p) t -> p c t", p=128)
        with ExitStack() as esa:
            wo = p.sb(esa, [128, 8, D], BF16, "wo")
            if slotmajor:
                gw = p.sb(esa, [64, 4, 256], BF16, "gw")
                gb = p.sb(esa, [64, 4], F32, "gb")
                gwst = p.sb(esa, [64, 4, 256], F32, "gwst")
                p.dma("sp", gwst[:], gluw_d[:], writes=[gwst])
                p.op("pool", lambda e: e.tensor_copy(out=gw[:], in_=gwst[:]), reads=[gwst], writes=[gw])
            else:
                gw = p.sb(esa, [128, 2, 256], BF16, "gw")
                gb = p.sb(esa, [128, 2], F32, "gb")
            p.dma("sp", gb[:], glub_d[:], reads=[glub_d], writes=[gb])
            with ExitStack() as es2:
                load_weight_bf16(p, es2, wout_d, wo, D, D, 512)
                if not slotmajor:
                    load_weight_bf16(p, es2, gluw_d, gw, 256, 256, 256)
                p.barrier()
            xs = [p.sb(esa, [128, 8, T], F32, "x") for _ in range(2)]
            ys = [p.sb(esa, [128, 8, T], F32, "y") for _ in range(2)]
            ybf = [p.sb(esa, [128, 8, T], BF16, "ybf") for _ in range(2)]
            gate = p.sb(esa, [128, 4, T], F32, "gate")
            pg = [p.ps(esa, [128, T], F32, "pg") for _ in range(2)]
            pm = [p.ps(esa, [128, T], F32, "pm") for _ in range(4)]
            nm = 0
            for ti, (t0, Tt, col) in enumerate(tl):
                x = xs[ti % 2]
                y = ys[ti % 2]
                yb = ybf[ti % 2]
                p.dma("sp", x[:, :, 0:Tt], xv[:, :, t0:t0 + Tt], reads=[xT_d], writes=[x])
                p.dma("sp", y[:, :, 0:Tt], yv[:, :, t0:t0 + Tt], reads=[yT_d], writes=[y])
                p.op("pool", lambda e, y=y, yb=yb, Tt=Tt: e.tensor_copy(out=yb[:, :, 0:Tt], in_=y[:, :, 0:Tt]),
                     reads=[y], writes=[yb])
                if slotmajor:
                    for ob in range(4):
                        pgb = pg[ob % 2]
                        for a in range(4):
                            _mm(p, pgb, pgb[0:64, 0:Tt], gw, gw[:, a, ob * 64:(ob + 1) * 64], yb, yb[0:64, 2 * a, 0:Tt], a == 0, a == 3)
                        p.op("act", lambda e, ob=ob, pgb=pgb, Tt=Tt: e.activation(out=gate[0:64, ob, 0:Tt], in_=pgb[0:64, 0:Tt],
                                                                               func=AF.Sigmoid, bias=gb[:, ob:ob + 1], scale=1.0),
                             reads=[pgb, gb], writes=[gate])
                    p.op("dve", lambda e, y=y, yb=yb, Tt=Tt: e.tensor_tensor(
                        out=yb[0:64, :, 0:Tt].rearrange("p (a two) t -> p a two t", two=2)[:, :, 0, :],
                        in0=y[0:64, :, 0:Tt].rearrange("p (a two) t -> p a two t", two=2)[:, :, 0, :],
                        in1=gate[0:64, :, 0:Tt], op=ALU.mult), reads=[y, gate], writes=[yb])
                else:
                    for oc in range(2):
                        for kc in range(2):
                            _mm(p, pg[oc], pg[oc][:, 0:Tt], gw, gw[:, kc, oc * 128:(oc + 1) * 128], yb, yb[:, kc, 0:Tt],
                                kc == 0, kc == 1)
                        p.op("act", lambda e, oc=oc, Tt=Tt: e.activation(out=gate[:, oc, 0:Tt], in_=pg[oc][:, 0:Tt],
                                                                         func=AF.Sigmoid, bias=gb[:, oc:oc + 1], scale=1.0),
                             reads=[pg[oc], gb], writes=[gate])
                    p.op("dve", lambda e, y=y, yb=yb, Tt=Tt: e.tensor_tensor(out=yb[:, 0:2, 0:Tt], in0=y[:, 0:2, 0:Tt],
                                                                             in1=gate[:, :, 0:Tt], op=ALU.mult),
                         reads=[y, gate], writes=[yb])
                for oc in range(8):
                    pmb = pm[nm % 4]
                    nm += 1
                    for kc in range(8):
                        _mm(p, pmb, pmb[:, 0:Tt], wo, wo[:, kc, oc * 128:(oc + 1) * 128], yb, yb[:, kc, 0:Tt], kc == 0, kc == 7)
                    p.op("dve", lambda e, pmb=pmb, x=x, oc=oc, Tt=Tt, col=col: e.scalar_tensor_tensor(
                        out=x[:, oc, 0:Tt], in0=pmb[:, 0:Tt], scalar=modT[:, oc, col:col + 1], in1=x[:, oc, 0:Tt],
                        op0=ALU.mult, op1=ALU.add), reads=[pmb, x, modT], writes=[x])
                p.dma("act", x1v[:, :, t0:t0 + Tt], x[:, :, 0:Tt], reads=[x])
            p.barrier()
        with ExitStack() as esb:
            w1 = p.sb(esb, [128, 8, DFF], BF16, "w1")
            w2 = p.sb(esb, [128, 32, D], BF16, "w2")
            with ExitStack() as es2:
                load_weight_bf16(p, es2, w1_d, w1, D, DFF, 512)
                load_weight_bf16(p, es2, w2_d, w2, DFF, D, 128)
                p.barrier()
            xs = [p.sb(esb, [128, 8, T], F32, "x") for _ in range(2)]
            sq = p.sb(esb, [128, 8, T], BF16, "sq")
            hn = p.sb(esb, [128, 8, T], F32, "hn")
            hbf = p.sb(esb, [128, 8, T], BF16, "hbf")
            rstd = p.sb(esb, [128, T], F32, "rstd")
            rl = [p.sb(esb, [128, T], F32, "rl") for _ in range(2)]
            act = p.sb(esb, [128, 32, T], BF16, "act")
            fing = p.sb(esb, [128, 8], F32, "fing")
            ssum = p.ps(esb, [128, T], F32, "ssum")
            ph = [p.ps(esb, [128, T], F32, "ph") for _ in range(3)]
            po = [p.ps(esb, [128, T], F32, "po") for _ in range(2)]
            if fin_d is not None:
                p.dma("sp", fing[:], fin_d[:], reads=[fin_d], writes=[fing])
            nh = 0
            no = 0
            for ti, (t0, Tt, col) in enumerate(tl):
                x = xs[ti % 2]
                p.dma("sp", x[:, :, 0:Tt], x1v[:, :, t0:t0 + Tt], reads=[x1_d], writes=[x])
                emit_norm_mod(p, x, Tt, sq, ssum, rstd, hn, hbf, ones, gsc, modT, 8, col)
                for hc in range(32):
                    phb = ph[nh % 3]
                    rlb = rl[nh % 2]
                    nh += 1
                    for kc in range(8):
                        _mm(p, phb, phb[:, 0:Tt], w1, w1[:, kc, hc * 128:(hc + 1) * 128], hbf, hbf[:, kc, 0:Tt], kc == 0, kc == 7)
                    p.op("act", lambda e, phb=phb, rlb=rlb, Tt=Tt: e.activation(out=rlb[:, 0:Tt], in_=phb[:, 0:Tt], func=AF.Relu),
                         reads=[phb], writes=[rlb])
                    p.op("dve", lambda e, rlb=rlb, hc=hc, Tt=Tt: e.tensor_tensor(out=act[:, hc, 0:Tt], in0=rlb[:, 0:Tt],
                                                                                 in1=rlb[:, 0:Tt], op=ALU.mult),
                         reads=[rlb], writes=[act])
                for oc in range(8):
                    pob = po[no % 2]
                    no += 1
                    for hc in range(32):
                        _mm(p, pob, pob[:, 0:Tt], w2, w2[:, hc, oc * 128:(oc + 1) * 128], act, act[:, hc, 0:Tt], hc == 0, hc == 31)
                    p.op("dve", lambda e, pob=pob, x=x, oc=oc, Tt=Tt, col=col: e.scalar_tensor_tensor(
                        out=x[:, oc, 0:Tt], in0=pob[:, 0:Tt], scalar=modT[:, 24 + oc, col:col + 1], in1=x[:, oc, 0:Tt],
                        op0=ALU.mult, op1=ALU.add), reads=[pob, x, modT], writes=[x])
                p.dma("act", xov[:, :, t0:t0 + Tt], x[:, :, 0:Tt], reads=[x])
                if fin_d is not None and col == 0:
                    fov = fo_d.t.rearrange("(c p) t -> p c t", p=128)
                    p.op("act", lambda e, x=x, Tt=Tt: e.activation(out=sq[:, :, 0:Tt], in_=x[:, :, 0:Tt], func=AF.Square),
                         reads=[x], writes=[sq])
                    for c in range(8):
                        _mm(p, ssum, ssum[:, 0:Tt], ones, ones[:], sq, sq[:, c, 0:Tt], c == 0, c == 7)
                    emit_rsqrt(p, rstd, ssum, Tt, EPS)
                    p.op("dve", lambda e, x=x, Tt=Tt: e.tensor_tensor(out=hn[:, :, 0:Tt], in0=x[:, :, 0:Tt],
                                                                      in1=rstd[:, 0:Tt].unsqueeze(1).to_broadcast([128, 8, Tt]),
                                                                      op=ALU.mult), reads=[x, rstd], writes=[hn])
                    p.op("dve", lambda e, Tt=Tt: e.tensor_tensor(out=hn[:, :, 0:Tt], in0=hn[:, :, 0:Tt],
                                                                 in1=fing[:].unsqueeze(2).to_broadcast([128, 8, Tt]),
                                                                 op=ALU.mult), reads=[hn, fing], writes=[hn])
                    p.dma("act", fov[:, :, t0 - fo_off:t0 - fo_off + Tt], hn[:, :, 0:Tt], reads=[hn])
            p.barrier()


def build_d2(final):
    nc = bass.Bass("TRN2", target_bir_lowering=False)
    with ExitStack() as es:
        p = Prog(nc, es)
        xT = p.dram("xT", [D, NT], F32, "ExternalInput")
        yT = p.dram("yT", [D, NT], F32, "ExternalInput")
        cond = p.dram("cond", [128, 8, 2], F32, "ExternalInput")
        adaw = p.dram("ada_w", [D, 6 * D], F32, "ExternalInput")
        adab = p.dram("ada_b", [128, 48], F32, "ExternalInput")
        g = p.dram("norm_g", [128, 8], F32, "ExternalInput")
        wout = p.dram("w_out", [D, D], F32, "ExternalInput")
        gluw = p.dram("glu_w", [256, 256], F32, "ExternalInput")
        glub = p.dram("glu_b", [128, 2], F32, "ExternalInput")
        w1 = p.dram("w1", [D, DFF], F32, "ExternalInput")
        w2 = p.dram("w2", [DFF, D], F32, "ExternalInput")
        x1 = p.dram("x1", [D, NT], F32, "Internal")
        xo = p.dram("xo", [D, NT], F32, "ExternalOutput")
        fin = fo = None
        if final:
            fin = p.dram("final_g", [128, 8], F32, "ExternalInput")
            fo = p.dram("fo", [D, NLAT], F32, "ExternalOutput")
        emit_d2(p, xT, yT, cond, adaw, adab, g, wout, gluw, glub, w1, w2, x1, xo, fin, fo)
        p.finish()
    return nc


LCTX = 256
LLAT = 16384
LSEQ = LCTX + LLAT
TT = 512
CH = 64
ZROWS = 848
R_S5, R_RWR, R_RWK, R_RWV, R_RWWA, R_RWG = 0, 64, 128, 192, 256, 320
R_RETQ, R_RETK, R_RETV, R_RETG = 384, 448, 512, 576
R_GLAQ, R_GLAK, R_GLAV, R_GLAG, R_GLAA = 640, 672, 704, 768, 832
C_ID, C_M4, C_ML, C_CM, C_JS, C_PR, C_ROPE, C_JJ = 0, 128, 256, 320, 832, 960, 1024, 1536
NCONST = 2048


def make_consts():
    c = np.zeros((128, NCONST), np.float32)
    c[:, C_ID:C_ID + 128] = np.eye(128, dtype=np.float32)
    s = np.arange(64)[:, None]
    t = np.arange(64)[None, :]
    strict = (s < t).astype(np.float32)
    incl = (s <= t).astype(np.float32)
    m4 = np.block([[strict, incl], [strict, incl]])
    c[:, C_M4:C_M4 + 128] = m4
    c[0:64, C_ML:C_ML + 64] = (s > t).astype(np.float32)
    cm = np.ones(512, np.float32)
    cm[::64] = 0.0
    c[:, C_CM:C_CM + 512] = cm[None, :]
    js = np.zeros((128, 128), np.float32)
    for m in range(64):
        js[m + 64, m] = -1.0
        js[m, m + 64] = 1.0
    c[:, C_JS:C_JS + 128] = js
    pr = np.zeros((64, 64), np.float32)
    for h in (0, 32):
        for m in range(16):
            pr[h + m + 16, h + m] = -1.0
            pr[h + m, h + m + 16] = 1.0
    c[0:64, C_PR:C_PR + 64] = pr
    inv = (10000.0 ** (-np.arange(16, dtype=np.float32) / 16)).astype(np.float32)
    fr = np.concatenate([inv, inv])
    ang_r = np.arange(256, dtype=np.float32)[None, :] * fr[:, None]
    ang_c = np.arange(64, dtype=np.float32)[None, :] * fr[:, None]
    c[0:32, C_ROPE:C_ROPE + 256] = np.cos(ang_r)
    c[0:32, C_ROPE + 256:C_ROPE + 512] = np.sin(ang_r)
    c[32:64, C_ROPE:C_ROPE + 64] = np.cos(ang_c)
    c[32:64, C_ROPE + 64:C_ROPE + 128] = np.sin(ang_c)
    c[:, C_JJ:C_JJ + 512] = np.arange(1, 513, dtype=np.float32)[None, :]
    return c


class SCtx:
    pass


def s_tiles(rev):
    lat = [(LCTX + i * TT, TT, False) for i in range(LLAT // TT)]
    if rev:
        lat = lat[::-1]
    return [(0, LCTX, True)] + lat


def rv(ap_fn, rev):
    return ap_fn


def emit_chunkscan(p, S, pre, dk, W, dplr, Hst, yout_ap_fn):
    nch = W // CH
    cst = S.cst
    import os
    stop = int(os.environ.get("S_STOP", "9"))
    if stop <= 1:
        return yout_ap_fn(S.psY, S.psA[0:64].rearrange("p a b -> p (a b)"), S.psA)
    QR, KD, LW = pre["QR"], pre["KD"], pre["LW"]
    cs, ex, tmp = S.cs, S.ex, S.tmp
    ARp, BKm, BKh, Wc = S.ARp, S.BKm, S.BKh, S.Wc
    dv = lambda b: b
    p.op("dve", lambda e: e.tensor_tensor_scan(out=cs[0:dk, 0:W], data0=cst[0:dk, C_CM:C_CM + W], data1=LW[0:dk, 0:W],
                                               initial=0.0, op0=ALU.mult, op1=ALU.add), reads=[cst, LW], writes=[cs])
    cs3 = lambda: cs[0:dk, 0:W].rearrange("p (c j) -> p c j", j=CH)
    tot_b = lambda: cs3()[:, :, CH - 1:CH].to_broadcast([dk, nch, CH])

    def v3(buf, idx):
        return buf[0:dk, 0:nch, idx, :]

    def t3(buf):
        return buf[0:dk, 0:W].rearrange("p (c j) -> p c j", j=CH)

    p.op("act", lambda e: e.activation(out=ex[0:dk, 0:W], in_=cs[0:dk, 0:W], func=AF.Exp), reads=[cs], writes=[ex])
    p.op("dve", lambda e: e.tensor_tensor(out=v3(ARp, 1), in0=t3(QR), in1=t3(ex), op=ALU.mult), reads=[QR, ex], writes=[ARp])
    p.op("act", lambda e: e.activation(out=Wc[0:dk, 0:nch], in_=cs3()[:, :, CH - 1], func=AF.Exp), reads=[cs], writes=[Wc])
    if dplr:
        AV, BV = pre["AV"], pre["BV"]
        p.op("dve", lambda e: e.tensor_tensor(out=tmp[0:dk, 0:W], in0=cs[0:dk, 0:W], in1=LW[0:dk, 0:W], op=ALU.subtract),
             reads=[cs, LW], writes=[tmp])
        p.op("act", lambda e: e.activation(out=tmp[0:dk, 0:W], in_=tmp[0:dk, 0:W], func=AF.Exp), reads=[tmp], writes=[tmp])
        p.op("dve", lambda e: e.tensor_tensor(out=v3(ARp, 0), in0=t3(AV), in1=t3(tmp), op=ALU.mult), reads=[AV, tmp], writes=[ARp])
    p.op("act", lambda e: e.activation(out=ex[0:dk, 0:W], in_=cs[0:dk, 0:W], func=AF.Exp, scale=-1.0), reads=[cs], writes=[ex])
    p.op("dve", lambda e: e.tensor_tensor(out=v3(BKm, 1), in0=t3(KD), in1=t3(ex), op=ALU.mult), reads=[KD, ex], writes=[BKm])
    if dplr:
        p.op("pool", lambda e: e.tensor_tensor(out=v3(BKm, 0), in0=t3(BV), in1=t3(ex), op=ALU.mult), reads=[BV, ex], writes=[BKm])
    p.op("dve", lambda e: e.tensor_tensor(out=t3(tmp), in0=tot_b(), in1=cs3(), op=ALU.subtract), reads=[cs], writes=[tmp])
    p.op("act", lambda e: e.activation(out=tmp[0:dk, 0:W], in_=tmp[0:dk, 0:W], func=AF.Exp), reads=[tmp], writes=[tmp])
    p.op("pool", lambda e: e.tensor_tensor(out=v3(BKh, 1), in0=t3(KD), in1=t3(tmp), op=ALU.mult), reads=[KD, tmp], writes=[BKh])
    if dplr:
        p.op("pool", lambda e: e.tensor_tensor(out=v3(BKh, 0), in0=t3(BV), in1=t3(tmp), op=ALU.mult), reads=[BV, tmp], writes=[BKh])
    if stop <= 2:
        return yout_ap_fn(S.psY, S.psA[0:64].rearrange("p a b -> p (a b)"), S.psA)
    AT, VU, BKT = S.AT, S.VU, S.BKT
    for half in range((nch + 3) // 4):
        psb = S.psS[half % 2]
        cl = list(range(half * 4, min(nch, half * 4 + 4)))
        for c in cl:
            _mm(p, psb, psb[:, c - half * 4, :], BKm, BKm[0:dk, c, :, :], ARp, ARp[0:dk, c, :, :], True, True)
        n = len(cl)
        p.op("dve", lambda e, psb=psb, half=half, n=n: e.tensor_tensor(
            out=AT[:, half * 4:half * 4 + n, :], in0=psb[:, 0:n, :],
            in1=cst[:, C_M4:C_M4 + 128].unsqueeze(1).to_broadcast([128, n, 128]), op=ALU.mult),
            reads=[psb, cst], writes=[AT])
    if stop <= 3:
        return yout_ap_fn(S.psY, S.psA[0:64].rearrange("p a b -> p (a b)"), S.psA)
    Vp = pre["V"]
    for c in range(nch):
        p.op("pe", lambda e, c=c: e.transpose(S.psT[:, c, :], Vp[:, c, :, :].rearrange("p a b -> p (a b)"), cst[0:64, C_ID:C_ID + 64]),
             reads=[Vp, cst], writes=[S.psT])
    p.op("act", lambda e: e.activation(out=VU[64:128, 0:nch, :], in_=S.psT[64:128, 0:nch, :], func=AF.Copy),
         reads=[S.psT], writes=[VU])
    for c in range(nch):
        p.op("pe", lambda e, c=c: e.transpose(S.psT2[:, c, 0:dk], BKh[0:dk, c, :, :].rearrange("p a b -> p (a b)"),
                                              cst[0:dk, C_ID:C_ID + dk]), reads=[BKh, cst], writes=[S.psT2])
    p.op("act", lambda e: e.activation(out=BKT[:, 0:nch, 0:dk], in_=S.psT2[:, 0:nch, 0:dk], func=AF.Copy),
         reads=[S.psT2], writes=[BKT])
    if stop <= 4:
        return yout_ap_fn(S.psY, S.psA[0:64].rearrange("p a b -> p (a b)"), S.psA)
    if dplr:
        Pm, Nj, Aj = S.Pm, S.Nj, S.Aj
        for c in range(nch):
            _mm(p, S.psA, S.psA[:, c, :], ARp, ARp[0:dk, c, 0, :], BKm, BKm[0:dk, c, 0, :], True, True)
        p.op("dve", lambda e: e.tensor_tensor(out=Aj[0][:, 0:nch, :], in0=S.psA[:, 0:nch, :],
                                              in1=cst[0:64, C_ML:C_ML + 64].unsqueeze(1).to_broadcast([64, nch, 64]),
                                              op=ALU.mult), reads=[S.psA, cst], writes=[Aj[0]])
        p.op("pool", lambda e: e.tensor_copy(out=Nj[0][:, 0:nch, :], in_=AT[0:64, 0:nch, 0:64]), reads=[AT], writes=[Nj[0]])
        p.op("pool", lambda e: e.tensor_tensor(out=Pm[:, 0:nch, :], in0=AT[0:64, 0:nch, 0:64],
                                               in1=cst[0:64, C_ID:C_ID + 64].unsqueeze(1).to_broadcast([64, nch, 64]),
                                               op=ALU.add), reads=[AT, cst], writes=[Pm])
        cur = 0
        for lvl in range(5):
            nxt = 1 - cur
            last = lvl == 4
            for c in range(nch):
                _mm(p, S.psA, S.psA[:, c, :], Nj[cur], Nj[cur][:, c, :], Aj[cur], Aj[cur][:, c, :], True, True)
            p.op("dve", lambda e, nxt=nxt: e.tensor_copy(out=Aj[nxt][:, 0:nch, :], in_=S.psA[:, 0:nch, :]),
                 reads=[S.psA], writes=[Aj[nxt]])
            if not last:
                for c in range(nch):
                    _mm(p, S.psN, S.psN[:, c, :], Aj[cur], Aj[cur][:, c, :], Nj[cur], Nj[cur][:, c, :], True, True)
                p.op("act", lambda e, nxt=nxt: e.activation(out=Nj[nxt][:, 0:nch, :], in_=S.psN[:, 0:nch, :], func=AF.Copy),
                     reads=[S.psN], writes=[Nj[nxt]])
            for c in range(nch):
                _mm(p, S.psP, S.psP[:, c, :], Aj[nxt], Aj[nxt][:, c, :], Pm, Pm[:, c, :], True, True)
            p.op("dve", lambda e: e.tensor_tensor(out=Pm[:, 0:nch, :], in0=Pm[:, 0:nch, :], in1=S.psP[:, 0:nch, :], op=ALU.add),
                 reads=[Pm, S.psP], writes=[Pm])
            cur = nxt
    if stop <= 5:
        return yout_ap_fn(S.psY, S.psA[0:64].rearrange("p a b -> p (a b)"), S.psA)
    psY, psY2 = S.psY, S.psA
    psY2f = psY2[0:64].rearrange("p a b -> p (a b)")
    rs = slice(0, 128) if dplr else slice(64, 128)
    for c in range(nch):
        if dplr:
            bX, bX2, bU = S.psN, S.psP, S.psS[0]
            aX, aX2, aU = bX[0:64, 0, :], bX2[0:64, 0, :], bU[0:64, 0, 0:64]
            _mm(p, bX, aX, AT, AT[64:128, c, 0:64], VU, VU[64:128, c, :], True, True)
            _mm(p, bX2, aX2, ARp, ARp[0:dk, c, 0, :], Hst, Hst[0:dk, :], True, True)
            p.op("act", lambda e: e.activation(out=S.Xs[:, :], in_=aX, func=AF.Copy), reads=[bX], writes=[S.Xs])
            p.op("dve", lambda e: e.tensor_tensor(out=S.Xs[:, :], in0=S.Xs[:, :], in1=aX2, op=ALU.add),
                 reads=[S.Xs, bX2], writes=[S.Xs])
            _mm(p, bU, aU, S.Pm, S.Pm[:, c, :], S.Xs, S.Xs[:, :], True, True)
            p.op("act", lambda e, c=c: e.activation(out=VU[0:64, c, :], in_=aU, func=AF.Copy), reads=[bU], writes=[VU])
        _mm(p, psY, psY[:, c * CH:(c + 1) * CH], VU, VU[rs, c, :], AT, AT[rs, c, 64:128], True, True)
        _mm(p, psY2, psY2f[:, c * CH:(c + 1) * CH], Hst, Hst[0:dk, :], ARp, ARp[0:dk, c, 1, :], True, True)
        _mm(p, S.psH, S.psH[0:dk, :], BKT, BKT[rs, c, 0:dk], VU, VU[rs, c, :], True, True)
        p.op("dve", lambda e, c=c: e.scalar_tensor_tensor(out=Hst[0:dk, :], in0=Hst[0:dk, :], scalar=Wc[0:dk, c:c + 1],
                                                          in1=S.psH[0:dk, :], op0=ALU.mult, op1=ALU.add),
             reads=[Hst, Wc, S.psH], writes=[Hst])
    yout_ap_fn(psY, psY2f, psY2)


PK = {}
_pk_n = 0


def _pk(name, n):
    global _pk_n
    PK[name] = (_pk_n, n)
    _pk_n += n


for _rg in ("r", "k", "v", "wa", "g"):
    _pk("mu_" + _rg, 1)
    _pk("msk_" + _rg, 6)
for _n in ("w0", "a0"):
    _pk(_n, 2)
for _n in ("k_k", "k_a", "r_k", "rw_ln_g", "rw_ln_b", "ret_ln_g", "gla_ln_g", "s5_d"):
    _pk(_n, 1)
_pk("WA", 128)
_pk("g_up", 64)
_pk("ret_dl", 2)
_pk("gla_aup", 64)
_pk("gla_ab", 2)
_pk("s5col", 24)
_pk("s5row", 8 * 3 * 64)
_pk("s5B", 8 * 2 * 64)
_pk("s5C", 8 * 64)
NPK = _pk_n
RW_COL0 = {"r": 0, "k": 256, "v": 512, "wa": 768, "g": 832}


def pack_params(inp, L, slot):
    pk = np.zeros((128, NPK), np.float32)

    def put(name, arr, r0=0):
        c0, n = PK[name]
        arr = np.asarray(arr, np.float32)
        if arr.ndim == 1:
            arr = arr[:, None]
        assert arr.shape[1] == n, (name, arr.shape, n)
        pk[r0:r0 + arr.shape[0], c0:c0 + n] = arr

    j = slot
    mu = inp["rw_mu"][L]
    cols = {"r": np.arange(64 * j, 64 * j + 64), "k": 256 + np.arange(64 * j, 64 * j + 64),
            "v": 512 + np.arange(64 * j, 64 * j + 64), "wa": np.arange(768, 832), "g": np.arange(832, 896)}
    for rg, cc in cols.items():
        put("mu_" + rg, mu[cc])
        q = cc // 224
        m = np.zeros((64, 6), np.float32)
        for qi in range(4):
            m[:, qi] = (q == qi)
        m[:, 4] = cc < 448
        m[:, 5] = cc >= 448
        put("msk_" + rg, m)
    hs = slice(64 * j, 64 * j + 64)
    put("w0", inp["rw_w0"][L][:, hs].T)
    put("a0", inp["rw_a0"][L][:, hs].T)
    put("k_k", inp["rw_k_k"][L][hs])
    put("k_a", inp["rw_k_a"][L][hs])
    put("r_k", inp["rw_r_k"][L][j])
    put("rw_ln_g", inp["rw_ln_g"][L][hs])
    put("rw_ln_b", inp["rw_ln_b"][L][hs])
    put("ret_ln_g", inp["ret_ln_g"][L][hs])
    put("gla_ln_g", inp["gla_ln_g"][L][hs])
    put("s5_d", inp["s5_d"][L][hs])
    wa = np.zeros((64, 2, 64), np.float32)
    wa[0:32] = inp["rw_w_up"][L][:, :, hs].transpose(1, 0, 2)
    wa[32:64] = inp["rw_a_up"][L][:, :, hs].transpose(1, 0, 2)
    put("WA", wa.reshape(64, 128))
    put("g_up", inp["rw_g_up"][L][:, hs])
    put("ret_dl", np.repeat(inp["ret_decay_logit"][L][:, j][None, :], 64, 0))
    put("gla_aup", inp["gla_a_up"][L][:, :, 32 * j:32 * j + 32].transpose(1, 0, 2).reshape(16, 64))
    put("gla_ab", inp["gla_a_b"][L][:, 32 * j:32 * j + 32].T)
    s5col = np.zeros((128, 8, 3), np.float32)
    s5row = np.zeros((64, 8, 3, 64), np.float32)
    s5B = np.zeros((64, 8, 2, 64), np.float32)
    s5C = np.zeros((128, 8, 64), np.float32)
    for gl in range(4):
        g = 4 * j + gl
        for d in range(2):
            gd = gl * 2 + d
            lre = inp["s5_lam_re"][L][d, g]
            lim = inp["s5_lam_im"][L][d, g]
            ldt = inp["s5_log_dt"][L][d, g]
            s5col[:, gd, 0] = np.concatenate([lre, lre])
            s5col[:, gd, 1] = np.concatenate([lim, lim])
            s5col[:, gd, 2] = ldt
            s5row[:, gd, 0, :] = lre[None, :]
            s5row[:, gd, 1, :] = lim[None, :]
            s5row[:, gd, 2, :] = ldt
            s5B[16 * gl:16 * gl + 16, gd, 0, :] = inp["s5_b_re"][L][d, g].T
            s5B[16 * gl:16 * gl + 16, gd, 1, :] = inp["s5_b_im"][L][d, g].T
            s5C[0:64, gd, 16 * gl:16 * gl + 16] = inp["s5_c_re"][L][d, g].T
            s5C[64:128, gd, 16 * gl:16 * gl + 16] = inp["s5_c_im"][L][d, g].T
    put("s5col", s5col.reshape(128, 24))
    put("s5row", s5row.reshape(64, -1))
    put("s5B", s5B.reshape(64, -1))
    put("s5C", s5C.reshape(128, -1))
    return pk


def pkc(S, name, rows=64, r0=0):
    c0, n = PK[name]
    return S.pk[r0:r0 + rows, c0:c0 + n]


def s_alloc_scan(p, es, S):
    f = lambda shape, name: p.sb(es, shape, F32, name)
    S.cs, S.ex, S.tmp = f([64, TT], "cs"), f([64, TT], "ex"), f([64, TT], "tmp")
    S.ARp, S.BKm, S.BKh = f([64, 8, 2, 64], "ARp"), f([64, 8, 2, 64], "BKm"), f([64, 8, 2, 64], "BKh")
    S.Wc = f([64, 8], "Wc")
    S.AT, S.VU, S.BKT = f([128, 8, 128], "AT"), f([128, 8, 64], "VU"), f([128, 8, 64], "BKT")
    S.Pm = f([64, 8, 64], "Pm")
    S.Nj = [f([64, 8, 64], "Nj") for _ in range(2)]
    S.Aj = [f([64, 8, 64], "Aj") for _ in range(2)]
    S.Xs = f([64, 64], "Xs")
    S.psS = [p.ps(es, [128, 4, 128], F32, "psS") for _ in range(2)]
    S.psT = p.ps(es, [128, 8, 64], F32, "psT")
    S.psT2 = S.psT
    S.psA = p.ps(es, [64, 8, 64], F32, "psA")
    S.psN = p.ps(es, [64, 8, 64], F32, "psN")
    S.psP = p.ps(es, [64, 8, 64], F32, "psP")
    misc = p.ps(es, [128, 8, 64], F32, "psmisc").t
    S.psX, S.psU, S.psH = Buf(misc[0:64, 0, :], "psX"), Buf(misc[0:64, 1, :], "psU"), Buf(misc[0:64, 2, :], "psH")
    S.psX2 = Buf(misc[0:64, 3, :], "psX2")
    S.psY = p.ps(es, [64, TT], F32, "psY")
    S.QR, S.KD, S.LW, S.AV, S.BV = (f([64, TT], n) for n in ("QR", "KD", "LW", "AV", "BV"))
    S.Vp = f([64, 8, 2, 64], "Vp")
    S.ones64 = f([64, 64], "ones64")
    S.mean64 = f([64, 64], "mean64")
    for b in (S.ARp, S.BKm, S.BKh, S.Vp, S.VU):
        p.op("pool", lambda e, b=b: e.memset(b[:], 0.0), writes=[b])
    p.op("pool", lambda e: e.memset(S.ones64[:], 1.0), writes=[S.ones64])
    p.op("pool", lambda e: e.memset(S.mean64[:], 1.0 / 64), writes=[S.mean64])


def s_load(p, S, dst, rows, r0, t0, nt, rev, stage, col0=0):
    if not rev:
        p.dma("sp", dst[0:rows, col0:col0 + nt], S.zsrc(r0, rows, t0, nt), writes=[dst])
    else:
        p.dma("sp", stage[0:rows, 0:nt], S.zsrc(r0, rows, t0, nt), writes=[stage])
        p.op("pool", lambda e: e.tensor_copy(out=dst[0:rows, col0:col0 + nt], in_=stage[0:rows, 0:nt][:, ::-1]),
             reads=[stage], writes=[dst])


def s_setup_scan(p, es, S):
    f = lambda shape, name: p.sb(es, shape, F32, name)
    S.coef = {}
    for rg in ("r", "k", "v", "wa", "g"):
        cf = f([64, 8], "coef_" + rg)
        mu = pkc(S, "mu_" + rg)
        p.op("dve", lambda e, cf=cf, mu=mu: e.tensor_scalar(out=cf[:, 0:1], in0=mu, scalar1=-1.0, scalar2=1.0,
                                                            op0=ALU.mult, op1=ALU.add), reads=[S.pkb], writes=[cf])
        p.op("dve", lambda e, cf=cf, mu=mu, rg=rg: e.tensor_tensor(out=cf[:, 1:7], in0=pkc(S, "msk_" + rg),
                                                                    in1=mu.to_broadcast([64, 6]), op=ALU.mult),
             reads=[S.pkb], writes=[cf])
        S.coef[rg] = cf
    S.der = f([64, 8], "der")
    der = S.der
    p.op("dve", lambda e: e.tensor_scalar(out=der[:, 0:1], in0=pkc(S, "k_a"), scalar1=-1.0, scalar2=1.0,
                                          op0=ALU.mult, op1=ALU.add), reads=[S.pkb], writes=[der])
    p.op("act", lambda e: e.activation(out=der[:, 1:3], in_=pkc(S, "ret_dl"), func=AF.Exp, scale=-1.0), reads=[S.pkb], writes=[der])
    p.op("act", lambda e: e.activation(out=der[:, 1:3], in_=der[:, 1:3], func=AF.Ln, bias=1.0, scale=1.0), reads=[der], writes=[der])
    p.op("dve", lambda e: e.tensor_scalar(out=der[:, 1:3], in0=der[:, 1:3], scalar1=-1.0, scalar2=None, op0=ALU.mult),
         reads=[der], writes=[der])
    p.op("dve", lambda e: e.tensor_scalar(out=der[0:32, 3:5], in0=pkc(S, "gla_ab", 32), scalar1=-1.0, scalar2=None, op0=ALU.mult),
         reads=[S.pkb], writes=[der])


def shift_active(slot, rg):
    if slot is None:
        return {1, 2, 3, 4, 5, 6}
    j = slot
    cc = {"r": np.arange(64 * j, 64 * j + 64), "k": 256 + np.arange(64 * j, 64 * j + 64),
          "v": 512 + np.arange(64 * j, 64 * j + 64), "wa": np.arange(768, 832), "g": np.arange(832, 896)}[rg]
    out = set(int(q) + 1 for q in np.unique(cc // 224))
    if (cc < 448).any():
        out.add(5)
    if (cc >= 448).any():
        out.add(6)
    return out


def s_prep_rwkv(p, S, d, t0, W, is_ctx, rev, zt, stage):
    cst = S.cst
    H = 0 if is_ctx else 64
    lo, hi = (0, LCTX) if is_ctx else (LCTX, LSEQ)
    a0_ = max(lo, t0 - H)
    a1_ = min(hi, t0 + W + H)
    rowmap = {"r": R_RWR, "k": R_RWK, "v": R_RWV, "wa": R_RWWA, "g": R_RWG}
    zm = S.zm
    for rg in ("r", "k", "v", "wa", "g"):
        z = zt[rg]
        full = (a0_ == t0 - H) and (a1_ == t0 + W + H)
        if not full:
            p.op("pool", lambda e, z=z: e.memset(z[:, 0:W + 2 * H], 0.0), writes=[z])
        off = a0_ - (t0 - H)
        n = a1_ - a0_
        if rev:
            off = (t0 + W + H) - a1_
        s_load(p, S, z, 64, rowmap[rg], a0_, n, rev, stage, col0=off)
        cf = S.coef[rg]
        o = zm[rg]
        iL, iR, iU, iD, iP, iN = (2, 1, 4, 3, 6, 5) if rev else (1, 2, 3, 4, 5, 6)
        p.op("dve", lambda e, z=z, o=o, cf=cf: e.tensor_scalar(out=o[:, 0:W], in0=z[:, H:H + W], scalar1=cf[:, 0:1], scalar2=None,
                                                               op0=ALU.mult), reads=[z, cf], writes=[o])

        def fma(oap, zap, ci, z=z, o=o, cf=cf):
            p.op("dve", lambda e: e.scalar_tensor_tensor(out=oap, in0=zap, scalar=cf[:, ci:ci + 1], in1=oap,
                                                         op0=ALU.mult, op1=ALU.add), reads=[z, o, cf], writes=[o])
        act_ = shift_active(getattr(S, "slot", None), rg)
        if is_ctx:
            if iP in act_:
                fma(o[:, 1:W], z[:, 0:W - 1], iP)
            if iN in act_:
                fma(o[:, 0:W - 1], z[:, 1:W], iN)
        else:
            if iU in act_:
                fma(o[:, 0:W], z[:, H - 64:H - 64 + W], iU)
            if iD in act_:
                fma(o[:, 0:W], z[:, H + 64:H + 64 + W], iD)
            o3 = o[:, 0:W].rearrange("p (r j) -> p r j", j=64)
            z3 = z[:, H:H + W].rearrange("p (r j) -> p r j", j=64)
            if iL in act_:
                fma(o3[:, :, 1:64], z3[:, :, 0:63], iL)
            if iR in act_:
                fma(o3[:, :, 0:63], z3[:, :, 1:64], iR)
    r, k, v, wa, gl = zm["r"], zm["k"], zm["v"], zm["wa"], zm["g"]
    nch = W // CH
    p.op("act", lambda e: e.activation(out=wa[0:32, 0:W], in_=wa[0:32, 0:W], func=AF.Tanh), reads=[wa], writes=[wa])
    WA = pkc(S, "WA").rearrange("p (d c) -> p d c", d=2)
    psw, psa = S.psS[0], S.psS[1]
    pswf = psw[0:64].rearrange("p a b -> p (a b)")
    psaf = psa[0:64].rearrange("p a b -> p (a b)")
    _mm(p, psw, pswf[:, 0:W], S.pkb, WA[0:32, d, :], wa, wa[0:32, 0:W], True, True)
    _mm(p, psa, psaf[:, 0:W], S.pkb, WA[32:64, d, :], wa, wa[32:64, 0:W], True, True)
    LW, QR, KD, AV, BV = S.LW, S.QR, S.KD, S.AV, S.BV
    asig = S.ex
    p.op("act", lambda e: e.activation(out=LW[:, 0:W], in_=pswf[:, 0:W], func=AF.Sigmoid, bias=pkc(S, "w0")[:, d:d + 1], scale=1.0),
         reads=[psw, S.pkb], writes=[LW])
    p.op("dve", lambda e: e.tensor_scalar(out=LW[:, 0:W], in0=LW[:, 0:W], scalar1=-float(np.exp(-0.5)), scalar2=None, op0=ALU.mult),
         reads=[LW], writes=[LW])
    p.op("act", lambda e: e.activation(out=asig[:, 0:W], in_=psaf[:, 0:W], func=AF.Sigmoid, bias=pkc(S, "a0")[:, d:d + 1], scale=1.0),
         reads=[psa, S.pkb], writes=[asig])
    kk, t2 = S.cs, S.tmp
    p.op("dve", lambda e: e.tensor_scalar(out=kk[:, 0:W], in0=k[:, 0:W], scalar1=pkc(S, "k_k"), scalar2=None, op0=ALU.mult),
         reads=[k, S.pkb], writes=[kk])
    p.op("dve", lambda e: e.tensor_tensor(out=t2[:, 0:W], in0=kk[:, 0:W], in1=kk[:, 0:W], op=ALU.mult), reads=[kk], writes=[t2])
    psk = S.psY
    _mm(p, psk, psk[:, 0:W], S.ones64, S.ones64[:], t2, t2[:, 0:W], True, True)
    emit_rsqrt(p, t2, psk, W, 1e-12, parts=64, fast=True)
    p.op("dve", lambda e: e.tensor_tensor(out=kk[:, 0:W], in0=kk[:, 0:W], in1=t2[:, 0:W], op=ALU.mult), reads=[kk, t2], writes=[kk])
    p.op("dve", lambda e: e.tensor_scalar(out=AV[:, 0:W], in0=kk[:, 0:W], scalar1=-1.0, scalar2=None, op0=ALU.mult),
         reads=[kk], writes=[AV])
    p.op("dve", lambda e: e.tensor_tensor(out=BV[:, 0:W], in0=kk[:, 0:W], in1=asig[:, 0:W], op=ALU.mult), reads=[kk, asig], writes=[BV])
    p.op("dve", lambda e: e.tensor_scalar(out=t2[:, 0:W], in0=asig[:, 0:W], scalar1=pkc(S, "k_a"), scalar2=S.der[:, 0:1],
                                          op0=ALU.mult, op1=ALU.add), reads=[asig, S.pkb, S.der], writes=[t2])
    p.op("dve", lambda e: e.tensor_tensor(out=KD[:, 0:W], in0=k[:, 0:W], in1=t2[:, 0:W], op=ALU.mult), reads=[k, t2], writes=[KD])
    p.op("pool", lambda e: e.tensor_copy(out=QR[:, 0:W], in_=r[:, 0:W]), reads=[r], writes=[QR])
    p.op("pool", lambda e: e.tensor_copy(out=S.Vp[:, 0:nch, 1, :], in_=v[:, 0:W].rearrange("p (c j) -> p c j", j=CH)),
         reads=[v], writes=[S.Vp])
    if not rev:
        aux = S.auxo
        p.dma("act", S.aux_d[0][:, t0:t0 + W], v[:, 0:W], reads=[v])
        p.op("act", lambda e: e.activation(out=gl[:, 0:W], in_=gl[:, 0:W], func=AF.Sigmoid), reads=[gl], writes=[gl])
        _mm(p, psw, pswf[:, 0:W], S.pkb, pkc(S, "g_up"), gl, gl[:, 0:W], True, True)
        p.op("act", lambda e: e.activation(out=aux[0][:, 0:W], in_=pswf[:, 0:W], func=AF.Copy), reads=[psw], writes=[aux[0]])
        p.dma("act", S.aux_d[1][:, t0:t0 + W], aux[0][:, 0:W], reads=[aux[0]])
        p.op("dve", lambda e: e.scalar_tensor_tensor(out=aux[1][:, 0:W], in0=r[:, 0:W], scalar=pkc(S, "r_k"), in1=k[:, 0:W],
                                                     op0=ALU.mult, op1=ALU.mult), reads=[r, k, S.pkb], writes=[aux[1]])
        _mm(p, psa, psaf[:, 0:W], S.ones64, S.ones64[:], aux[1], aux[1][:, 0:W], True, True)
        p.op("act", lambda e: e.activation(out=aux[1][:, 0:W], in_=psaf[:, 0:W], func=AF.Copy), reads=[psa], writes=[aux[1]])
        p.dma("act", S.aux_d[2][:, t0:t0 + W], aux[1][:, 0:W], reads=[aux[1]])
    return dict(QR=QR, KD=KD, V=S.Vp, LW=LW, AV=AV, BV=BV)


def s_rope(p, S, x, t0, W, rev):
    cst = S.cst
    nr = W // 64
    r0 = (t0 - LCTX) // 64
    ps = S.psS[0]
    psf = ps[0:64].rearrange("p a b -> p (a b)")
    _mm(p, ps, psf[:, 0:W], cst, cst[0:64, C_PR:C_PR + 64], x, x[:, 0:W], True, True)
    xr = S.tmp
    sl = slice(None, None, -1) if rev else slice(None)
    for (pp, tabc, tabs) in ((slice(0, 32), cst[0:32, C_ROPE + r0:C_ROPE + r0 + nr][:, sl].unsqueeze(2).to_broadcast([32, nr, 64]),
                              cst[0:32, C_ROPE + 256 + r0:C_ROPE + 256 + r0 + nr][:, sl].unsqueeze(2).to_broadcast([32, nr, 64])),
                             (slice(32, 64), cst[32:64, C_ROPE:C_ROPE + 64][:, sl].unsqueeze(1).to_broadcast([32, nr, 64]),
                              cst[32:64, C_ROPE + 64:C_ROPE + 128][:, sl].unsqueeze(1).to_broadcast([32, nr, 64]))):
        x3 = x[pp, 0:W].rearrange("p (r j) -> p r j", j=64)
        xr3 = xr[pp, 0:W].rearrange("p (r j) -> p r j", j=64)
        ps3 = psf[pp, 0:W].rearrange("p (r j) -> p r j", j=64)
        p.op("dve", lambda e, xr3=xr3, ps3=ps3, tabs=tabs: e.tensor_tensor(out=xr3, in0=ps3, in1=tabs, op=ALU.mult),
             reads=[ps, cst], writes=[xr])
        p.op("dve", lambda e, x3=x3, tabc=tabc: e.tensor_tensor(out=x3, in0=x3, in1=tabc, op=ALU.mult), reads=[x, cst], writes=[x])
    p.op("dve", lambda e: e.tensor_tensor(out=x[:, 0:W], in0=x[:, 0:W], in1=xr[:, 0:W], op=ALU.add), reads=[x, xr], writes=[x])


def s_prep_ret(p, S, d, t0, W, is_ctx, rev, stage):
    QR, KD, LW = S.QR, S.KD, S.LW
    nch = W // CH
    s_load(p, S, QR, 64, R_RETQ, t0, W, rev, stage)
    s_load(p, S, KD, 64, R_RETK, t0, W, rev, stage)
    v = S.AV
    s_load(p, S, v, 64, R_RETV, t0, W, rev, stage)
    p.op("pool", lambda e: e.tensor_copy(out=S.Vp[:, 0:nch, 1, :], in_=v[:, 0:W].rearrange("p (c j) -> p c j", j=CH)),
         reads=[v], writes=[S.Vp])
    if not is_ctx:
        s_rope(p, S, QR, t0, W, rev)
        s_rope(p, S, KD, t0, W, rev)
    p.op("dve", lambda e: e.tensor_scalar(out=KD[:, 0:W], in0=KD[:, 0:W], scalar1=0.125, scalar2=None, op0=ALU.mult),
         reads=[KD], writes=[KD])
    p.op("dve", lambda e: e.tensor_scalar(out=LW[:, 0:W], in0=QR[:, 0:W], scalar1=0.0, scalar2=S.der[:, 1 + d:2 + d],
                                          op0=ALU.mult, op1=ALU.add), reads=[QR, S.der], writes=[LW])
    return dict(QR=QR, KD=KD, V=S.Vp, LW=LW)


def s_prep_gla(p, S, d, t0, W, is_ctx, rev, stage):
    QR, KD, LW = S.QR, S.KD, S.LW
    nch = W // CH
    s_load(p, S, QR, 32, R_GLAQ, t0, W, rev, stage)
    s_load(p, S, KD, 32, R_GLAK, t0, W, rev, stage)
    v = S.AV
    s_load(p, S, v, 64, R_GLAV, t0, W, rev, stage)
    al = S.BV
    s_load(p, S, al, 16, R_GLAA, t0, W, rev, stage)
    p.op("pool", lambda e: e.tensor_copy(out=S.Vp[:, 0:nch, 1, :], in_=v[:, 0:W].rearrange("p (c j) -> p c j", j=CH)),
         reads=[v], writes=[S.Vp])
    p.op("dve", lambda e: e.tensor_scalar(out=KD[0:32, 0:W], in0=KD[0:32, 0:W], scalar1=float(32 ** -0.5), scalar2=None, op0=ALU.mult),
         reads=[KD], writes=[KD])
    ps = S.psS[0]
    psf = ps[0:32].rearrange("p a b -> p (a b)")
    aup = pkc(S, "gla_aup", 16).rearrange("p (d c) -> p d c", d=2)
    _mm(p, ps, psf[:, 0:W], S.pkb, aup[:, d, :], al, al[0:16, 0:W], True, True)
    p.op("act", lambda e: e.activation(out=LW[0:32, 0:W], in_=psf[:, 0:W], func=AF.Exp, bias=S.der[0:32, 3 + d:4 + d], scale=-1.0),
         reads=[ps, S.der], writes=[LW])
    p.op("act", lambda e: e.activation(out=LW[0:32, 0:W], in_=LW[0:32, 0:W], func=AF.Ln, bias=1.0, scale=1.0), reads=[LW], writes=[LW])
    p.op("dve", lambda e: e.tensor_scalar(out=LW[0:32, 0:W], in0=LW[0:32, 0:W], scalar1=-1.0 / 16, scalar2=None, op0=ALU.mult),
         reads=[LW], writes=[LW])
    return dict(QR=QR, KD=KD, V=S.Vp, LW=LW)


def emit_s_scans(p, S, mixers=("rw", "ret", "gla")):
    with ExitStack() as es:
        s_alloc_scan(p, es, S)
        s_setup_scan(p, es, S)
        f = lambda shape, name: p.sb(es, shape, F32, name)
        zt = {rg: f([64, TT + 128], "z_" + rg) for rg in ("r", "k", "v", "wa", "g")}
        S.zm = {rg: f([64, TT], "zm_" + rg) for rg in ("r", "k", "v", "wa", "g")}
        stage = f([64, TT + 128], "stage")
        S.auxo = [f([64, TT], "auxo") for _ in range(2)]
        yo = [f([64, TT], "yo") for _ in range(2)]
        Hst = {m: f([64, 64], "H_" + m) for m in mixers}
        ny = 0
        for d in getattr(S, "dirs", (0, 1)):
            rev = d == 1
            for m in mixers:
                p.op("pool", lambda e, m=m: e.memset(Hst[m][:], 0.0), writes=[Hst[m]])
            for (t0, W, is_ctx) in s_tiles(rev):
                for m in mixers:
                    if m == "rw":
                        pre = s_prep_rwkv(p, S, d, t0, W, is_ctx, rev, zt, stage)
                        dk, dplr = 64, True
                    elif m == "ret":
                        pre = s_prep_ret(p, S, d, t0, W, is_ctx, rev, stage)
                        dk, dplr = 64, False
                    else:
                        pre = s_prep_gla(p, S, d, t0, W, is_ctx, rev, stage)
                        dk, dplr = 32, False
                    yb = yo[ny % 2]
                    ny += 1

                    def yout(psY, psY2f, psY2, yb=yb, m=m, d=d, t0=t0, W=W, rev=rev):
                        src = psY[:, 0:W][:, ::-1] if rev else psY[:, 0:W]
                        src2 = psY2f[:, 0:W][:, ::-1] if rev else psY2f[:, 0:W]
                        p.op("act", lambda e: e.activation(out=yb[:, 0:W], in_=src, func=AF.Copy), reads=[psY], writes=[yb])
                        p.op("dve", lambda e: e.tensor_tensor(out=yb[:, 0:W], in0=yb[:, 0:W], in1=src2, op=ALU.add),
                             reads=[yb, psY2], writes=[yb])
                        p.dma("act", S.ysc[(m, d)][:, t0:t0 + W], yb[:, 0:W], reads=[yb])
                    emit_chunkscan(p, S, pre, dk, W, dplr, Hst[m], yout)
        p.barrier()


def emit_s_post(p, S, do_s5=True, mixers=("rw", "ret", "gla")):
    with ExitStack() as es:
        f = lambda shape, name: p.sb(es, shape, F32, name)
        mean64 = f([64, 64], "mean64")
        p.op("pool", lambda e: e.memset(mean64[:], 1.0 / 64), writes=[mean64])
        yf = [f([64, TT], "yf") for _ in range(2)]
        yb = [f([64, TT], "yb") for _ in range(2)]
        ax = [[f([64, TT], "ax") for _ in range(3)] for _ in range(2)]
        sq = f([64, TT], "sq")
        rs = f([64, TT], "rs")
        ps1 = p.ps(es, [64, TT], F32, "ps1")
        ps2 = p.ps(es, [64, TT], F32, "ps2")
        k = 0
        for (t0, W, is_ctx) in s_tiles(False):
            for mi, m in enumerate(("s5",) * do_s5 + tuple(mixers)):
                a, b, x3 = yf[k % 2], yb[k % 2], ax[k % 2]
                k += 1
                p.dma("sp", a[:, 0:W], S.ysc[(m, 0)][:, t0:t0 + W], writes=[a])
                p.dma("sp", b[:, 0:W], S.ysc[(m, 1 if 1 in getattr(S, "dirs", (0, 1)) else 0)][:, t0:t0 + W], writes=[b])
                p.op("pool", lambda e, a=a, b=b: e.tensor_tensor(out=a[:, 0:W], in0=a[:, 0:W], in1=b[:, 0:W], op=ALU.add),
                     reads=[a, b], writes=[a])
                if m == "rw":
                    for i in range(3):
                        p.dma("sp", x3[i][:, 0:W], S.aux_d[i][:, t0:t0 + W], writes=[x3[i]])
                    _mm(p, ps1, ps1[:, 0:W], mean64, mean64[:], a, a[:, 0:W], True, True)
                    p.op("dve", lambda e, a=a: e.tensor_tensor(out=a[:, 0:W], in0=a[:, 0:W], in1=ps1[:, 0:W], op=ALU.subtract),
                         reads=[a, ps1], writes=[a])
                    p.op("act", lambda e, a=a: e.activation(out=sq[:, 0:W], in_=a[:, 0:W], func=AF.Square), reads=[a], writes=[sq])
                    _mm(p, ps2, ps2[:, 0:W], mean64, mean64[:], sq, sq[:, 0:W], True, True)
                    emit_rsqrt(p, rs, ps2, W, 64e-5, parts=64, fast=True)
                    p.op("dve", lambda e, a=a: e.tensor_tensor(out=a[:, 0:W], in0=a[:, 0:W], in1=rs[:, 0:W], op=ALU.mult),
                         reads=[a, rs], writes=[a])
                    p.op("dve", lambda e, a=a: e.tensor_scalar(out=a[:, 0:W], in0=a[:, 0:W], scalar1=pkc(S, "rw_ln_g"),
                                                               scalar2=pkc(S, "rw_ln_b"), op0=ALU.mult, op1=ALU.add),
                         reads=[a, S.pk], writes=[a])
                    p.op("pool", lambda e, x3=x3: e.tensor_tensor(out=x3[2][:, 0:W], in0=x3[2][:, 0:W], in1=x3[0][:, 0:W], op=ALU.mult),
                         reads=[x3[2], x3[0]], writes=[x3[2]])
                    p.op("dve", lambda e, a=a, x3=x3: e.tensor_tensor(out=a[:, 0:W], in0=a[:, 0:W], in1=x3[2][:, 0:W], op=ALU.add),
                         reads=[a, x3[2]], writes=[a])
                    p.op("dve", lambda e, a=a, x3=x3: e.tensor_tensor(out=a[:, 0:W], in0=a[:, 0:W], in1=x3[1][:, 0:W], op=ALU.mult),
                         reads=[a, x3[1]], writes=[a])
                    r0 = 64
                elif m in ("ret", "gla"):
                    g = x3[0]
                    p.dma("sp", g[:, 0:W], S.zsrc(R_RETG if m == "ret" else R_GLAG, 64, t0, W), writes=[g])
                    p.op("act", lambda e, a=a: e.activation(out=sq[:, 0:W], in_=a[:, 0:W], func=AF.Square), reads=[a], writes=[sq])
                    _mm(p, ps1, ps1[:, 0:W], mean64, mean64[:], sq, sq[:, 0:W], True, True)
                    emit_rsqrt(p, rs, ps1, W, 1e-6, parts=64, fast=True)
                    p.op("dve", lambda e, a=a, m=m: e.scalar_tensor_tensor(out=a[:, 0:W], in0=a[:, 0:W], scalar=pkc(S, m + "_ln_g"),
                                                                           in1=rs[:, 0:W], op0=ALU.mult, op1=ALU.mult),
                         reads=[a, rs, S.pk], writes=[a])
                    p.op("act", lambda e, g=g, x3=x3: e.activation(out=x3[1][:, 0:W], in_=g[:, 0:W], func=AF.Sigmoid),
                         reads=[g], writes=[x3[1]])
                    p.op("pool", lambda e, g=g, x3=x3: e.tensor_tensor(out=g[:, 0:W], in0=g[:, 0:W], in1=x3[1][:, 0:W], op=ALU.mult),
                         reads=[g, x3[1]], writes=[g])
                    p.op("dve", lambda e, a=a, g=g: e.tensor_tensor(out=a[:, 0:W], in0=a[:, 0:W], in1=g[:, 0:W], op=ALU.mult),
                         reads=[a, g], writes=[a])
                    r0 = 128 if m == "ret" else 192
                else:
                    u = x3[0]
                    p.dma("sp", u[:, 0:W], S.zsrc(R_S5, 64, t0, W), writes=[u])
                    p.op("dve", lambda e, a=a, u=u: e.scalar_tensor_tensor(out=a[:, 0:W], in0=u[:, 0:W], scalar=pkc(S, "s5_d"),
                                                                           in1=a[:, 0:W], op0=ALU.mult, op1=ALU.add),
                         reads=[a, u, S.pk], writes=[a])
                    p.op("act", lambda e, a=a: e.activation(out=sq[:, 0:W], in_=a[:, 0:W], func=AF.Square), reads=[a], writes=[sq])
                    p.op("dve", lambda e: e.tensor_scalar(out=sq[:, 0:W], in0=sq[:, 0:W], scalar1=0.044715, scalar2=1.0,
                                                          op0=ALU.mult, op1=ALU.add), reads=[sq], writes=[sq])
                    p.op("dve", lambda e, a=a: e.tensor_tensor(out=sq[:, 0:W], in0=sq[:, 0:W], in1=a[:, 0:W], op=ALU.mult),
                         reads=[sq, a], writes=[sq])
                    p.op("act", lambda e: e.activation(out=sq[:, 0:W], in_=sq[:, 0:W], func=AF.Tanh, scale=0.7978845608028654),
                         reads=[sq], writes=[sq])
                    p.op("dve", lambda e: e.tensor_scalar(out=sq[:, 0:W], in0=sq[:, 0:W], scalar1=1.0, scalar2=0.5,
                                                          op0=ALU.add, op1=ALU.mult), reads=[sq], writes=[sq])
                    p.op("dve", lambda e, a=a: e.tensor_tensor(out=a[:, 0:W], in0=a[:, 0:W], in1=sq[:, 0:W], op=ALU.mult),
                         reads=[a, sq], writes=[a])
                    r0 = 0
                p.dma("act", S.yout_d[r0:r0 + 64, t0:t0 + W], a[:, 0:W], reads=[a])
        p.barrier()


def emit_s(p, S, do_s5=True, mixers=("rw", "ret", "gla")):
    with ExitStack() as es:
        S.cst = p.sb(es, [128, NCONST], F32, "cst")
        S.pk = p.sb(es, [128, NPK], F32, "pk")
        S.pkb = S.pk
        p.dma("sp", S.cst[:], S.cst_d[:], writes=[S.cst])
        p.dma("sp", S.pk[:], S.pk_d[:], writes=[S.pk])
        if do_s5:
            emit_s_s5(p, S)
        emit_s_scans(p, S, mixers)
        emit_s_post(p, S, do_s5, mixers)


def build_s(do_s5=True, mixers=("rw", "ret", "gla"), dirs=(0, 1), slot=None):
    nc = bass.Bass("TRN2", target_bir_lowering=False)
    with ExitStack() as es:
        p = Prog(nc, es)
        S = SCtx()
        S.dirs = dirs
        S.slot = slot
        z = p.dram("zslot", [ZROWS, LSEQ], F32, "ExternalInput")
        S.zsrc = lambda r0, nr, t0, nt: z[r0:r0 + nr, t0:t0 + nt]
        S.cst_d = p.dram("consts", [128, NCONST], F32, "ExternalInput")
        S.pk_d = p.dram("pk", [128, NPK], F32, "ExternalInput")
        S.ysc = {(m, d): p.dram("ysc_%s%d" % (m, d), [64, LSEQ], F32, "Internal") for m in ("s5", "rw", "ret", "gla") for d in range(2)}
        S.aux_d = [p.dram("aux%d" % i, [64, LSEQ], F32, "Internal") for i in range(3)]
        S.yout_d = p.dram("yslot", [256, LSEQ], F32, "ExternalOutput")
        emit_s(p, S, do_s5, mixers)
        p.finish()
    return nc


def slot_cols(j):
    a = np.arange
    rw, ret, gla = 256, 1152, 2176
    return np.concatenate([
        a(64 * j, 64 * j + 64),
        rw + a(64 * j, 64 * j + 64), rw + 256 + a(64 * j, 64 * j + 64), rw + 512 + a(64 * j, 64 * j + 64),
        rw + a(768, 832), rw + a(832, 896),
        ret + a(64 * j, 64 * j + 64), ret + 256 + a(64 * j, 64 * j + 64), ret + 512 + a(64 * j, 64 * j + 64),
        ret + 768 + a(64 * j, 64 * j + 64),
        gla + a(32 * j, 32 * j + 32), gla + 128 + a(32 * j, 32 * j + 32), gla + 256 + a(64 * j, 64 * j + 64),
        gla + 512 + a(64 * j, 64 * j + 64), gla + a(768, 784)])


TWO_PI = float(2 * np.pi)


def emit_sincos(p, ang, s_out, c_out, wk, f):
    ki, a, b, c, d = wk["ki"], wk["a"], wk["b"], wk["c"], wk["d"]
    V = lambda e_, fn, rd, wr: p.op(e_, fn, reads=rd, writes=wr)
    V("dve", lambda e: e.tensor_scalar(out=f(a), in0=f(ang), scalar1=float(1.0 / TWO_PI), scalar2=None, op0=ALU.mult), [ang], [a])
    V("dve", lambda e: e.tensor_copy(out=f(ki), in_=f(a)), [a], [ki])
    V("dve", lambda e: e.tensor_copy(out=f(a), in_=f(ki)), [ki], [a])
    V("dve", lambda e: e.scalar_tensor_tensor(out=f(a), in0=f(a), scalar=-TWO_PI, in1=f(ang), op0=ALU.mult, op1=ALU.add), [a, ang], [a])
    V("act", lambda e: e.activation(out=f(b), in_=f(a), func=AF.Sin, scale=0.25), [a], [b])
    V("act", lambda e: e.activation(out=f(c), in_=f(a), func=AF.Sin, scale=0.25, bias=float(np.pi / 2)), [a], [c])
    for _ in range(2):
        V("dve", lambda e: e.tensor_tensor(out=f(d), in0=f(b), in1=f(c), op=ALU.mult), [b, c], [d])
        V("dve", lambda e: e.tensor_tensor(out=f(b), in0=f(b), in1=f(b), op=ALU.mult), [b], [b])
        V("dve", lambda e: e.tensor_tensor(out=f(c), in0=f(c), in1=f(c), op=ALU.mult), [c], [c])
        V("dve", lambda e: e.tensor_tensor(out=f(c), in0=f(c), in1=f(b), op=ALU.subtract), [c, b], [c])
        V("dve", lambda e: e.tensor_scalar(out=f(b), in0=f(d), scalar1=2.0, scalar2=None, op0=ALU.mult), [d], [b])
    V("dve", lambda e: e.tensor_copy(out=f(s_out), in_=f(b)), [b], [s_out])
    V("dve", lambda e: e.tensor_copy(out=f(c_out), in_=f(c)), [c], [c_out])


def emit_s_s5(p, S):
    cst = S.cst
    with ExitStack() as es:
        f = lambda shape, name: p.sb(es, shape, F32, name)
        col = pkc(S, "s5col", 128).rearrange("p (g k) -> p g k", k=3)
        dtc, rho, th = f([128, 8], "dtc"), f([128, 8], "rho"), f([128, 8], "th")
        p.op("act", lambda e: e.activation(out=dtc[:], in_=col[:, :, 2], func=AF.Exp), reads=[S.pk], writes=[dtc])
        p.op("dve", lambda e: e.tensor_scalar(out=rho[:], in0=col[:, :, 0], scalar1=-1e-4, scalar2=None, op0=ALU.min), reads=[S.pk], writes=[rho])
        p.op("dve", lambda e: e.tensor_tensor(out=rho[:], in0=rho[:], in1=dtc[:], op=ALU.mult), reads=[rho, dtc], writes=[rho])
        p.op("act", lambda e: e.activation(out=rho[:], in_=rho[:], func=AF.Exp), reads=[rho], writes=[rho])
        p.op("dve", lambda e: e.tensor_tensor(out=th[:], in0=col[:, :, 1], in1=dtc[:], op=ALU.mult), reads=[S.pk, dtc], writes=[th])
        COS, SIN = f([128, 8, TT], "COS"), f([128, 8, TT], "SIN")
        ang = f([128, TT], "ang")
        wk = {n: f([128, TT], "wk" + n) for n in ("a", "b", "c", "d")}
        wk["ki"] = p.sb(es, [128, TT], mybir.dt.int32, "wkki")
        for gd in range(8):
            p.op("dve", lambda e, gd=gd: e.tensor_scalar(out=ang[:], in0=cst[:, C_JJ:C_JJ + TT], scalar1=th[:, gd:gd + 1], scalar2=None,
                                                         op0=ALU.mult), reads=[cst, th], writes=[ang])
            emit_sincos(p, ang, _Sl(SIN, gd), _Sl(COS, gd), wk, lambda b: b[:] if not isinstance(b, _Sl) else b.ap())
        row = pkc(S, "s5row", 64).rearrange("p (g k n) -> p g k n", k=3, n=64)
        Bt = pkc(S, "s5B", 64).rearrange("p (g k n) -> p g k n", k=2, n=64)
        R = lambda name: f([64, 8, 64], name)
        dtr, lr, x, ar, ai, den, cr, ci, t1, t2 = (R(n) for n in ("dtr", "lr", "x", "ar", "ai", "den", "cr", "ci", "t1", "t2"))
        V = lambda e_, fn, rd, wr: p.op(e_, fn, reads=rd, writes=wr)
        V("act", lambda e: e.activation(out=dtr[:], in_=row[:, :, 2, :], func=AF.Exp), [S.pk], [dtr])
        V("dve", lambda e: e.tensor_scalar(out=lr[:], in0=row[:, :, 0, :], scalar1=-1e-4, scalar2=None, op0=ALU.min), [S.pk], [lr])
        V("dve", lambda e: e.tensor_tensor(out=x[:], in0=lr[:], in1=dtr[:], op=ALU.mult), [lr, dtr], [x])
        V("act", lambda e: e.activation(out=x[:], in_=x[:], func=AF.Exp), [x], [x])
        V("dve", lambda e: e.tensor_tensor(out=t1[:], in0=row[:, :, 1, :], in1=dtr[:], op=ALU.mult), [S.pk, dtr], [t1])
        f3 = lambda b: (b[:] if b.t.shape[0] == 64 else b[0:64, 0:512].rearrange("p (g n) -> p g n", n=64))
        emit_sincos(p, t1, ai, ar, wk, f3)
        V("dve", lambda e: e.tensor_tensor(out=ar[:], in0=ar[:], in1=x[:], op=ALU.mult), [ar, x], [ar])
        V("dve", lambda e: e.tensor_tensor(out=ai[:], in0=ai[:], in1=x[:], op=ALU.mult), [ai, x], [ai])
        V("dve", lambda e: e.tensor_scalar(out=ar[:], in0=ar[:], scalar1=-1.0, scalar2=None, op0=ALU.add), [ar], [ar])
        lim = row[:, :, 1, :]
        V("dve", lambda e: e.tensor_tensor(out=den[:], in0=lr[:], in1=lr[:], op=ALU.mult), [lr], [den])
        V("dve", lambda e: e.tensor_tensor(out=t1[:], in0=lim, in1=lim, op=ALU.mult), [S.pk], [t1])
        V("dve", lambda e: e.tensor_tensor(out=den[:], in0=den[:], in1=t1[:], op=ALU.add), [den, t1], [den])
        V("dve", lambda e: e.reciprocal(out=den[:], in_=den[:]), [den], [den])
        V("dve", lambda e: e.tensor_tensor(out=cr[:], in0=ar[:], in1=lr[:], op=ALU.mult), [ar, lr], [cr])
        V("dve", lambda e: e.tensor_tensor(out=t1[:], in0=ai[:], in1=lim, op=ALU.mult), [ai, S.pk], [t1])
        V("dve", lambda e: e.tensor_tensor(out=cr[:], in0=cr[:], in1=t1[:], op=ALU.add), [cr, t1], [cr])
        V("dve", lambda e: e.tensor_tensor(out=cr[:], in0=cr[:], in1=den[:], op=ALU.mult), [cr, den], [cr])
        V("dve", lambda e: e.tensor_tensor(out=ci[:], in0=ai[:], in1=lr[:], op=ALU.mult), [ai, lr], [ci])
        V("dve", lambda e: e.tensor_tensor(out=t1[:], in0=ar[:], in1=lim, op=ALU.mult), [ar, S.pk], [t1])
        V("dve", lambda e: e.tensor_tensor(out=ci[:], in0=ci[:], in1=t1[:], op=ALU.subtract), [ci, t1], [ci])
        V("dve", lambda e: e.tensor_tensor(out=ci[:], in0=ci[:], in1=den[:], op=ALU.mult), [ci, den], [ci])
        LB, LW_ = f([64, 8, 128], "LB"), f([64, 8, 128], "LWs5")
        Bre, Bim = Bt[:, :, 0, :], Bt[:, :, 1, :]
        V("dve", lambda e: e.tensor_tensor(out=t1[:], in0=cr[:], in1=Bre, op=ALU.mult), [cr, S.pk], [t1])
        V("dve", lambda e: e.tensor_tensor(out=t2[:], in0=ci[:], in1=Bim, op=ALU.mult), [ci, S.pk], [t2])
        V("dve", lambda e: e.tensor_tensor(out=LB[:, :, 0:64], in0=t1[:], in1=t2[:], op=ALU.subtract), [t1, t2], [LB])
        V("dve", lambda e: e.tensor_tensor(out=t1[:], in0=cr[:], in1=Bim, op=ALU.mult), [cr, S.pk], [t1])
        V("dve", lambda e: e.tensor_tensor(out=t2[:], in0=ci[:], in1=Bre, op=ALU.mult), [ci, S.pk], [t2])
        V("dve", lambda e: e.tensor_tensor(out=LB[:, :, 64:128], in0=t1[:], in1=t2[:], op=ALU.add), [t1, t2], [LB])
        V("dve", lambda e: e.tensor_copy(out=LW_[:, :, 0:64], in_=LB[:, :, 64:128]), [LB], [LW_])
        V("dve", lambda e: e.tensor_scalar(out=LW_[:, :, 64:128], in0=LB[:, :, 0:64], scalar1=-1.0, scalar2=None, op0=ALU.mult), [LB], [LW_])
        LBb = p.sb(es, [64, 8, 128], BF16, "LBb")
        LWb = p.sb(es, [64, 8, 128], BF16, "LWb")
        V("dve", lambda e: e.tensor_copy(out=LBb[:], in_=LB[:]), [LB], [LBb])
        V("dve", lambda e: e.tensor_copy(out=LWb[:], in_=LW_[:]), [LW_], [LWb])
        Cm = f([128, 8, 64], "Cm")
        Cpk = pkc(S, "s5C", 128).rearrange("p (g c) -> p g c", c=64)
        V("dve", lambda e: e.tensor_copy(out=Cm[0:64], in_=Cpk[0:64]), [S.pk], [Cm])
        V("dve", lambda e: e.tensor_scalar(out=Cm[64:128], in0=Cpk[64:128], scalar1=-1.0, scalar2=None, op0=ALU.mult), [S.pk], [Cm])
        u_t = [f([64, TT], "u") for _ in range(2)]
        ub_t = [p.sb(es, [64, TT], BF16, "ub") for _ in range(2)]
        stage = f([64, TT], "stage5")
        cb = [f([128, TT], "c5") for _ in range(2)]
        tb = [f([128, TT], "t5") for _ in range(2)]
        gb = [f([128, TT], "g5") for _ in range(2)]
        hb = [f([128, TT], "h5") for _ in range(2)]
        yo = [f([64, TT], "yo5") for _ in range(2)]
        carry = f([128, 4], "carry")
        psB = [p.ps(es, [128, TT], F32, "psB") for _ in range(2)]
        psW = [p.ps(es, [128, TT], F32, "psW") for _ in range(2)]
        psG = [p.ps(es, [128, TT], F32, "psG") for _ in range(2)]
        psYs = p.ps(es, [64, TT], F32, "psYs")
        k = 0
        for d in getattr(S, "dirs", (0, 1)):
            rev = d == 1
            p.op("pool", lambda e: e.memset(carry[:], 0.0), writes=[carry])
            for ti, (t0, W, is_ctx) in enumerate(s_tiles(rev)):
                u = u_t[ti % 2]
                s_load(p, S, u, 64, R_S5, t0, W, rev, stage)
                ub = ub_t[ti % 2]
                V("act", lambda e, u=u, ub=ub: e.activation(out=ub[:, 0:W], in_=u[:, 0:W], func=AF.Copy), [u], [ub])
                for gl in range(4):
                    gd = gl * 2 + d
                    c5, t5, g5, h5 = cb[k % 2], tb[k % 2], gb[k % 2], hb[k % 2]
                    pB, pW, pG = psB[k % 2], psW[k % 2], psG[k % 2]
                    k += 1
                    _mm(p, pB, pB[:, 0:W], LBb, LBb[:, gd, :], ub, ub[:, 0:W], True, True)
                    _mm(p, pW, pW[:, 0:W], LWb, LWb[:, gd, :], ub, ub[:, 0:W], True, True)
                    V("dve", lambda e, c5=c5, pB=pB, gd=gd: e.tensor_tensor(out=c5[:, 0:W], in0=pB[:, 0:W], in1=COS[:, gd, 0:W], op=ALU.mult),
                      [pB, COS], [c5])
                    V("dve", lambda e, t5=t5, pW=pW, gd=gd: e.tensor_tensor(out=t5[:, 0:W], in0=pW[:, 0:W], in1=SIN[:, gd, 0:W], op=ALU.mult),
                      [pW, SIN], [t5])
                    V("pool", lambda e, c5=c5, t5=t5: e.tensor_tensor(out=c5[:, 0:W], in0=c5[:, 0:W], in1=t5[:, 0:W], op=ALU.add), [c5, t5], [c5])
                    V("dve", lambda e, g5=g5, c5=c5, gd=gd, gl=gl: e.tensor_tensor_scan(
                        out=g5[:, 0:W], data0=rho[:, gd:gd + 1].to_broadcast([128, W]), data1=c5[:, 0:W],
                        initial=carry[:, gl:gl + 1], op0=ALU.mult, op1=ALU.add), [rho, c5, carry], [g5])
                    _mm(p, pG, pG[:, 0:W], cst, cst[:, C_JS:C_JS + 128], g5, g5[:, 0:W], True, True)
                    V("pool", lambda e, h5=h5, g5=g5, gd=gd: e.tensor_tensor(out=h5[:, 0:W], in0=g5[:, 0:W], in1=COS[:, gd, 0:W], op=ALU.mult),
                      [g5, COS], [h5])
                    V("dve", lambda e, t5=t5, pG=pG, gd=gd: e.tensor_tensor(out=t5[:, 0:W], in0=pG[:, 0:W], in1=SIN[:, gd, 0:W], op=ALU.mult),
                      [pG, SIN], [t5])
                    V("dve", lambda e, h5=h5, t5=t5: e.tensor_tensor(out=h5[:, 0:W], in0=h5[:, 0:W], in1=t5[:, 0:W], op=ALU.add), [h5, t5], [h5])
                    V("act", lambda e, h5=h5, gl=gl: e.activation(out=carry[:, gl:gl + 1], in_=h5[:, W - 1:W], func=AF.Copy), [h5], [carry])
                    _mm(p, psYs, psYs[:, 0:W], Cm, Cm[:, gd, :], h5, h5[:, 0:W], gl == 0, gl == 3)
                yb = yo[ti % 2]
                src = psYs[:, 0:W][:, ::-1] if rev else psYs[:, 0:W]
                V("act", lambda e, yb=yb, src=src: e.activation(out=yb[:, 0:W], in_=src, func=AF.Copy), [psYs], [yb])
                p.dma("act", S.ysc[("s5", d)][:, t0:t0 + W], yb[:, 0:W], reads=[yb])
        p.barrier()


class _Sl:
    def __init__(self, buf, gd):
        self.__dict__["b"] = buf
        self.__dict__["gd"] = gd

    def ap(self):
        return self.b[:, self.gd, :]

    def __getattr__(self, k):
        return getattr(self.b, k)

    def __setattr__(self, k, v):
        setattr(self.b, k, v)


DEPTH = 4
ZALL = 4 * ZROWS


def dense_tiles(T):
    return [(0, LCTX, 1)] + [(LCTX + i * T, T, 0) for i in range(LLAT // T)]


def build_full(depth=DEPTH, slots=(0, 1, 2, 3)):
    nc = bass.Bass("TRN2", target_bir_lowering=False)
    with ExitStack() as es:
        p = Prog(nc, es)
        ein = lambda name, shape: p.dram(name, shape, F32, "ExternalInput")
        xT = ein("xT", [D, LSEQ])
        cond = ein("cond", [128, 8, 2])
        cst_d = ein("consts", [128, NCONST])
        fin = ein("final_g", [128, 8])
        fo = p.dram("fo", [D, LLAT], F32, "ExternalOutput")
        z_d = p.dram("z_scr", [ZALL, LSEQ], F32, "Internal")
        y_d = p.dram("y_scr", [D, LSEQ], F32, "Internal")
        x_d = p.dram("x_scr", [D, LSEQ], F32, "Internal")
        x1_d = p.dram("x1_scr", [D, LSEQ], F32, "Internal")
        ysc = {(m, d): p.dram("ysc_%s%d" % (m, d), [64, LSEQ], F32, "Internal") for m in ("s5", "rw", "ret", "gla") for d in range(2)}
        aux_d = [p.dram("aux%d" % i, [64, LSEQ], F32, "Internal") for i in range(3)]
        for L in range(depth):
            sfx = "_%d" % L
            adaw, adab = ein("ada_w" + sfx, [D, 6 * D]), ein("ada_b" + sfx, [128, 48])
            g1n, g2n = ein("norm1_g" + sfx, [128, 8]), ein("norm2_g" + sfx, [128, 8])
            win = ein("w_in" + sfx, [D, ZALL])
            wout = ein("w_out" + sfx, [D, D])
            gluw, glub = ein("glu_w" + sfx, [64, 4, 256]), ein("glu_b" + sfx, [64, 4])
            w1, w2 = ein("w1" + sfx, [D, DFF]), ein("w2" + sfx, [DFF, D])
            pks = [ein("pk%s_%d" % (sfx, j), [128, NPK]) for j in range(4)]
            xin = xT if L == 0 else x_d
            emit_d1(p, xin, cond, adaw, adab, g1n, win, z_d, tl=dense_tiles(512), ncol=ZALL)
            for j in slots:
                S = SCtx()
                S.slot = j
                S.zsrc = lambda r0, nr, t0, nt, j=j: z_d[j * ZROWS + r0:j * ZROWS + r0 + nr, t0:t0 + nt]
                S.cst_d, S.pk_d, S.ysc, S.aux_d = cst_d, pks[j], ysc, aux_d
                S.yout_d = Buf(y_d[j * 256:(j + 1) * 256, :], "yslot%d" % j)
                emit_s(p, S)
            last = L == depth - 1
            emit_d2(p, xin, y_d, cond, adaw, adab, g2n, wout, gluw, glub, w1, w2, x1_d, x_d,
                    fin if last else None, fo if last else None, tl=dense_tiles(256), slotmajor=True, fo_off=LCTX)
        p.finish()
    return nc


def _fm(v):
    return np.ascontiguousarray(np.asarray(v, np.float32).reshape(8, 128).T)


def prep_inputs(inp, b, depth=DEPTH):
    f32 = lambda a: np.ascontiguousarray(np.asarray(a, np.float32))
    m = {}
    m["xT"] = f32(np.concatenate([inp["ctx"][b], inp["x"][b]], 0).T)
    cc = np.stack([inp["c"][b], inp["c_ctx"]], -1)
    m["cond"] = f32(cc.reshape(8, 128, 2).transpose(1, 0, 2))
    m["consts"] = make_consts()
    m["final_g"] = _fm(inp["final_g"])
    allcols = np.concatenate([slot_cols(j) for j in range(4)])
    r = np.arange(1024)
    orig = ((r % 256) // 64) * 256 + (r // 256) * 64 + (r % 64)
    for L in range(depth):
        sfx = "_%d" % L
        m["ada_w" + sfx] = f32(inp["ada_w"][L])
        m["ada_b" + sfx] = f32(inp["ada_b"][L].reshape(48, 128).T)
        m["norm1_g" + sfx] = _fm(inp["norm1_g"][L])
        m["norm2_g" + sfx] = _fm(inp["norm2_g"][L])
        m["w_in" + sfx] = f32(inp["w_in"][L][:, allcols])
        m["w_out" + sfx] = f32(inp["w_out"][L][orig, :])
        m["glu_w" + sfx] = f32(inp["s5_glu_w"][L].reshape(4, 64, 256).transpose(1, 0, 2))
        m["glu_b" + sfx] = f32(inp["s5_glu_b"][L].reshape(4, 64).T)
        m["w1" + sfx] = f32(inp["mlp_w1"][L])
        m["w2" + sfx] = f32(inp["mlp_w2"][L])
        for j in range(4):
            m["pk%s_%d" % (sfx, j)] = pack_params(inp, L, j)
    return m


_NC_CACHE = {}


def kernel(**inputs):
    inp = {k: np.asarray(v) for k, v in inputs.items()}
    if "full" not in _NC_CACHE:
        _NC_CACHE["full"] = build_full()
    nc = _NC_CACHE["full"]
    in_maps = [prep_inputs(inp, b) for b in range(2)]
    res = run_bass_kernel_spmd(nc, in_maps, core_ids=[0, 1])
    out = np.stack([np.ascontiguousarray(res.results[b]["fo"].T) for b in range(2)], 0)
    return out.astype(np.float32)
```

```python
import numpy as np
from contextlib import ExitStack
import concourse.bass as bass
import concourse.mybir as mybir
from concourse.bass_utils import run_bass_kernel_spmd

F32 = mybir.dt.float32
BF16 = mybir.dt.bfloat16
AF = mybir.ActivationFunctionType
ALU = mybir.AluOpType

D = 1024
NLAT = 4096
NCTX = 64
NT = NLAT + NCTX
IN_TOTAL = 2960
DFF = 4096
EPS = 1e-6


class Buf:
    __slots__ = ("t", "w", "r", "name")

    def __init__(self, t, name=""):
        self.t = t
        self.w = None
        self.r = {}
        self.name = name

    def __getitem__(self, idx):
        return self.t[idx]


class Prog:
    ENG = ("pe", "dve", "act", "pool", "sp")

    def __init__(self, nc, es, n_dma_sems=32):
        self.nc = nc
        self.eng = {"pe": nc.tensor, "dve": nc.vector, "act": nc.scalar, "pool": nc.gpsimd, "sp": nc.sync}
        self.sem = {}
        self.cnt = {}
        for e in self.ENG:
            self.sem[e] = es.enter_context(nc.semaphore("sem_" + e))
            self.cnt[e] = 0
        self.dsem = [es.enter_context(nc.semaphore("dsem%d" % i)) for i in range(n_dma_sems)]
        self.dcnt = [0] * n_dma_sems
        self.dnext = 0
        self.waited = {e: {} for e in self.ENG}
        self.nbuf = 0

    def sb(self, es, shape, dtype=F32, name=None):
        self.nbuf += 1
        name = (name or "sb") + "_%d" % self.nbuf
        return Buf(es.enter_context(self.nc.sbuf_tensor(name, list(shape), dtype)), name)

    def ps(self, es, shape, dtype=F32, name=None):
        self.nbuf += 1
        name = (name or "ps") + "_%d" % self.nbuf
        return Buf(es.enter_context(self.nc.psum_tensor(name, list(shape), dtype)), name)

    def dram(self, name, shape, dtype=F32, kind="Internal"):
        return Buf(self.nc.dram_tensor(name, list(shape), dtype, kind=kind).ap(), name)

    def _semh(self, key):
        return self.sem[key] if isinstance(key, str) else self.dsem[key[1]]

    def _wait(self, e, tok):
        key, val = tok
        if self.waited[e].get(key, 0) >= val:
            return
        if key == e and e in ("pe",):
            return
        self.eng[e].wait_ge(self._semh(key), val)
        self.waited[e][key] = val

    def _deps(self, reads, writes):
        toks = []
        for b in reads:
            if b.w is not None:
                toks.append(b.w)
        for b in writes:
            if b.w is not None:
                toks.append(b.w)
            toks.extend(b.r.items())
        return toks

    def _mark(self, tok, reads, writes):
        key, val = tok
        for b in reads:
            if b.r.get(key, 0) < val:
                b.r[key] = val
        for b in writes:
            b.w = tok
            b.r = {}

    def op(self, e, fn, reads=(), writes=()):
        for tok in self._deps(reads, writes):
            self._wait(e, tok)
        ins = fn(self.eng[e])
        self.cnt[e] += 1
        ins.then_inc(self.sem[e], 1)
        self._mark((e, self.cnt[e]), reads, writes)

    def dma(self, q, out, in_, reads=(), writes=()):
        i = self.dnext
        self.dnext = (i + 1) % len(self.dsem)
        if self.dcnt[i] > 0:
            self._wait(q, (("d", i), self.dcnt[i] * 16))
        for tok in self._deps(reads, writes):
            self._wait(q, tok)
        ins = self.eng[q].dma_start(out=out, in_=in_)
        self.dcnt[i] += 1
        ins.then_inc(self.dsem[i], 16)
        self._mark((("d", i), self.dcnt[i] * 16), reads, writes)

    def barrier(self):
        for e in self.ENG:
            for e2 in self.ENG:
                if e2 != e and self.cnt[e2] > 0:
                    self._wait(e, (e2, self.cnt[e2]))
            for i in range(len(self.dsem)):
                if self.dcnt[i] > 0:
                    self._wait(e, (("d", i), self.dcnt[i] * 16))

    def finish(self):
        for i in range(len(self.dsem)):
            if self.dcnt[i] > 0:
                self._wait("sp", (("d", i), self.dcnt[i] * 16))
        for e2 in self.ENG:
            if e2 != "sp" and self.cnt[e2] > 0:
                self._wait("sp", (e2, self.cnt[e2]))


def _mm(p, out_buf, out_ap, lhsT_buf, lhsT_ap, rhs_buf, rhs_ap, start, stop, extra_reads=()):
    p.op("pe", lambda e: e.matmul(out_ap, lhsT=lhsT_ap, rhs=rhs_ap, start=start, stop=stop),
         reads=[lhsT_buf, rhs_buf] + list(extra_reads), writes=[out_buf])


def emit_mod(p, es, cond_d, adaw_d, adab_d, chunks, psum):
    n = len(chunks)
    modT = p.sb(es, [128, n * 8, 2], F32, "modT")
    with ExitStack() as es2:
        cond = p.sb(es2, [128, 8, 2], F32, "cond")
        sig = p.sb(es2, [128, 8, 2], F32, "sig")
        scond = p.sb(es2, [128, 8, 2], F32, "scond")
        adab = p.sb(es2, [128, 48], F32, "adab")
        stg = [p.sb(es2, [128, 8, 512], F32, "adastg") for _ in range(2)]
        p.dma("sp", cond[:], cond_d[:], reads=[cond_d], writes=[cond])
        p.dma("sp", adab[:], adab_d[:], reads=[adab_d], writes=[adab])
        p.op("act", lambda e: e.activation(out=sig[:], in_=cond[:], func=AF.Sigmoid), reads=[cond], writes=[sig])
        p.op("dve", lambda e: e.tensor_tensor(out=scond[:], in0=cond[:], in1=sig[:], op=ALU.mult),
             reads=[cond, sig], writes=[scond])
        adaw_v = adaw_d.t.rearrange("(kc p) n -> p kc n", p=128)
        k = 0
        for ci, a in enumerate(chunks):
            for half in range(2):
                st = stg[k % 2]
                k += 1
                c0 = a * 1024 + half * 512
                p.dma("sp", st[:], adaw_v[:, :, c0:c0 + 512], reads=[adaw_d], writes=[st])
                for cc in range(4):
                    idx = ci * 8 + half * 4 + cc
                    for kc in range(8):
                        _mm(p, psum, psum[:, idx, :], st, st[:, kc, cc * 128:(cc + 1) * 128],
                            scond, scond[:, kc, :], kc == 0, kc == 7)
        for ci, a in enumerate(chunks):
            p.op("dve", lambda e, ci=ci, a=a: e.tensor_tensor(
                out=modT[:, ci * 8:(ci + 1) * 8, :], in0=psum[:, ci * 8:(ci + 1) * 8, :],
                in1=adab[:, a * 8:(a + 1) * 8].unsqueeze(2).to_broadcast([128, 8, 2]), op=ALU.add),
                reads=[psum, adab], writes=[modT])
        p.barrier()
    return modT


def load_weight_bf16(p, es_stage, w_d, wbf, K, N, piece):
    kc_n = K // 128
    w_v = w_d.t.rearrange("(kc p) n -> p kc n", p=128)
    stg = [p.sb(es_stage, [128, kc_n, piece], F32, "wstg") for _ in range(2)]
    k = 0
    for c0 in range(0, N, piece):
        w = min(piece, N - c0)
        st = stg[k % 2]
        k += 1
        p.dma("sp", st[:, :, 0:w], w_v[:, :, c0:c0 + w], reads=[w_d], writes=[st])
        p.op("pool", lambda e, st=st, c0=c0, w=w: e.tensor_copy(out=wbf[:, :, c0:c0 + w], in_=st[:, :, 0:w]),
             reads=[st], writes=[wbf])


def emit_rsqrt(p, out, src, T, eps, parts=128, fast=False):
    if fast:
        p.op("act", lambda e: e.activation(out=out[0:parts, 0:T], in_=src[0:parts, 0:T], func=AF.Ln, bias=float(eps), scale=1.0),
             reads=[src], writes=[out])
        p.op("act", lambda e: e.activation(out=out[0:parts, 0:T], in_=out[0:parts, 0:T], func=AF.Exp, scale=-0.5),
             reads=[out], writes=[out])
        return
    p.op("act", lambda e: e.activation(out=out[0:parts, 0:T], in_=src[0:parts, 0:T], func=AF.Sqrt, bias=float(eps), scale=1.0),
         reads=[src], writes=[out])
    p.op("dve", lambda e: e.reciprocal(out=out[0:parts, 0:T], in_=out[0:parts, 0:T]), reads=[out], writes=[out])


def emit_norm_mod(p, x, T, sq, ssum, rstd, hn, hbf, ones, gsc, sh, sh_off, col):
    p.op("act", lambda e: e.activation(out=sq[:, :, 0:T], in_=x[:, :, 0:T], func=AF.Square), reads=[x], writes=[sq])
    for c in range(8):
        _mm(p, ssum, ssum[:, 0:T], ones, ones[:], sq, sq[:, c, 0:T], c == 0, c == 7)
    emit_rsqrt(p, rstd, ssum, T, EPS)
    p.op("dve", lambda e: e.tensor_tensor(out=hn[:, :, 0:T], in0=x[:, :, 0:T],
                                          in1=rstd[:, 0:T].unsqueeze(1).to_broadcast([128, 8, T]), op=ALU.mult),
         reads=[x, rstd], writes=[hn])
    for c in range(8):
        p.op("act", lambda e, c=c: e.activation(out=hbf[:, c, 0:T], in_=hn[:, c, 0:T], func=AF.Identity,
                                                bias=sh[:, sh_off + c, col:col + 1], scale=gsc[:, c, col:col + 1]),
             reads=[hn, gsc, sh], writes=[hbf])


def tiles_of(T):
    tl = [(t0, T, 0) for t0 in range(0, NLAT, T)]
    tl.append((NLAT, NCTX, 1))
    return tl


def emit_d1(p, xT_d, cond_d, adaw_d, adab_d, g_d, win_d, zT_d, tl=None, ncol=IN_TOTAL):
    T = 512
    tl = tl or tiles_of(T)
    nchunk = (ncol + 127) // 128
    with ExitStack() as es:
        psA = p.ps(es, [128, 16, 2], F32, "psmod")
        modT = emit_mod(p, es, cond_d, adaw_d, adab_d, [0, 1], psA)
        gam = p.sb(es, [128, 8], F32, "gam")
        gsc = p.sb(es, [128, 8, 2], F32, "gsc")
        ones = p.sb(es, [128, 128], BF16, "ones")
        wbf = p.sb(es, [128, 8, ncol], BF16, "winbf")
        p.dma("sp", gam[:], g_d[:], reads=[g_d], writes=[gam])
        p.op("pool", lambda e: e.memset(ones[:], 1.0 / D), writes=[ones])
        p.op("dve", lambda e: e.tensor_scalar(out=gsc[:], in0=modT[:, 8:16, :], scalar1=1.0, scalar2=None, op0=ALU.add),
             reads=[modT], writes=[gsc])
        p.op("dve", lambda e: e.tensor_tensor(out=gsc[:], in0=gsc[:], in1=gam[:].unsqueeze(2).to_broadcast([128, 8, 2]),
                                              op=ALU.mult), reads=[gsc, gam], writes=[gsc])
        with ExitStack() as es2:
            load_weight_bf16(p, es2, win_d, wbf, D, ncol, 424)
            p.barrier()
        xs = [p.sb(es, [128, 8, T], F32, "x") for _ in range(2)]
        sq = p.sb(es, [128, 8, T], BF16, "sq")
        hn = p.sb(es, [128, 8, T], F32, "hn")
        hbf = [p.sb(es, [128, 8, T], BF16, "hbf") for _ in range(2)]
        rstd = p.sb(es, [128, T], F32, "rstd")
        zo = [p.sb(es, [128, T], F32, "zo") for _ in range(4)]
        ssum = p.ps(es, [128, T], F32, "ssum")
        pz = [p.ps(es, [128, T], F32, "pz") for _ in range(4)]
        xv = xT_d.t.rearrange("(c p) t -> p c t", p=128)
        nz = 0
        for ti, (t0, Tt, col) in enumerate(tl):
            x = xs[ti % 2]
            h = hbf[ti % 2]
            p.dma("sp", x[:, :, 0:Tt], xv[:, :, t0:t0 + Tt], reads=[xT_d], writes=[x])
            emit_norm_mod(p, x, Tt, sq, ssum, rstd, hn, h, ones, gsc, modT, 0, col)
            for cc in range(nchunk):
                c0 = cc * 128
                M = min(128, ncol - c0)
                pzb = pz[nz % 4]
                zob = zo[nz % 4]
                for kc in range(8):
                    _mm(p, pzb, pzb[0:M, 0:Tt], wbf, wbf[:, kc, c0:c0 + M], h, h[:, kc, 0:Tt], kc == 0, kc == 7)
                if nz % 2 == 0:
                    p.op("dve", lambda e, pzb=pzb, zob=zob, M=M, Tt=Tt: e.tensor_copy(out=zob[0:M, 0:Tt], in_=pzb[0:M, 0:Tt]),
                         reads=[pzb], writes=[zob])
                else:
                    p.op("act", lambda e, pzb=pzb, zob=zob, M=M, Tt=Tt: e.activation(out=zob[0:M, 0:Tt], in_=pzb[0:M, 0:Tt],
                                                                                     func=AF.Copy),
                         reads=[pzb], writes=[zob])
                p.dma("act", zT_d[c0:c0 + M, t0:t0 + Tt], zob[0:M, 0:Tt], reads=[zob])
                nz += 1
        p.barrier()


def build_d1():
    nc = bass.Bass("TRN2", target_bir_lowering=False)
    with ExitStack() as es:
        p = Prog(nc, es)
        xT = p.dram("xT", [D, NT], F32, "ExternalInput")
        cond = p.dram("cond", [128, 8, 2], F32, "ExternalInput")
        adaw = p.dram("ada_w", [D, 6 * D], F32, "ExternalInput")
        adab = p.dram("ada_b", [128, 48], F32, "ExternalInput")
        g = p.dram("norm_g", [128, 8], F32, "ExternalInput")
        win = p.dram("w_in", [D, IN_TOTAL], F32, "ExternalInput")
        zT = p.dram("zT", [IN_TOTAL, NT], F32, "ExternalOutput")
        emit_d1(p, xT, cond, adaw, adab, g, win, zT)
        p.finish()
    return nc


def emit_d2(p, xT_d, yT_d, cond_d, adaw_d, adab_d, g_d, wout_d, gluw_d, glub_d, w1_d, w2_d, x1_d, xo_d, fin_d, fo_d, tl=None, slotmajor=False, fo_off=0):
    T = 256
    tl = tl or tiles_of(T)
    with ExitStack() as es:
        psA = p.ps(es, [128, 32, 2], F32, "psmod")
        modT = emit_mod(p, es, cond_d, adaw_d, adab_d, [2, 3, 4, 5], psA)
        gam = p.sb(es, [128, 8], F32, "gam")
        gsc = p.sb(es, [128, 8, 2], F32, "gsc")
        ones = p.sb(es, [128, 128], BF16, "ones")
        p.dma("sp", gam[:], g_d[:], reads=[g_d], writes=[gam])
        p.op("pool", lambda e: e.memset(ones[:], 1.0 / D), writes=[ones])
        p.op("dve", lambda e: e.tensor_scalar(out=gsc[:], in0=modT[:, 16:24, :], scalar1=1.0, scalar2=None, op0=ALU.add),
             reads=[modT], writes=[gsc])
        p.op("dve", lambda e: e.tensor_tensor(out=gsc[:], in0=gsc[:], in1=gam[:].unsqueeze(2).to_broadcast([128, 8, 2]),
                                              op=ALU.mult), reads=[gsc, gam], writes=[gsc])
        xv = xT_d.t.rearrange("(c p) t -> p c t", p=128)
        yv = yT_d.t.rearrange("(c p) t -> p c t", p=128)
        x1v = x1_d.t.rearrange("(c p) t -> p c t", p=128)
        xov = xo_d.t.rearrange("(c p) t -> p c t", p=128)
        with ExitStack() as esa:
            wo = p.sb(esa, [128, 8, D], BF16, "wo")
            if slotmajor:
                gw = p.sb(esa, [64, 4, 256], BF16, "gw")
                gb = p.sb(esa, [64, 4], F32, "gb")
                gwst = p.sb(esa, [64, 4, 256], F32, "gwst")
                p.dma("sp", gwst[:], gluw_d[:], writes=[gwst])
                p.op("pool", lambda e: e.tensor_copy(out=gw[:], in_=gwst[:]), reads=[gwst], writes=[gw])
            else:
                gw = p.sb(esa, [128, 2, 256], BF16, "gw")
                gb = p.sb(esa, [128, 2], F32, "gb")
            p.dma("sp", gb[:], glub_d[:], reads=[glub_d], writes=[gb])
            with ExitStack() as es2:
                load_weight_bf16(p, es2, wout_d, wo, D, D, 512)
                if not slotmajor:
                    load_weight_bf16(p, es2, gluw_d, gw, 256, 256, 256)
                p.barrier()
            xs = [p.sb(esa, [128, 8, T], F32, "x") for _ in range(2)]
            ys = [p.sb(esa, [128, 8, T], F32, "y") for _ in range(2)]
            ybf = [p.sb(esa, [128, 8, T], BF16, "ybf") for _ in range(2)]
            gate = p.sb(esa, [128, 4, T], F32, "gate")
            pg = [p.ps(esa, [128, T], F32, "pg") for _ in range(2)]
            pm = [p.ps(esa, [128, T], F32, "pm") for _ in range(4)]
            nm = 0
            for ti, (t0, Tt, col) in enumerate(tl):
                x = xs[ti % 2]
                y = ys[ti % 2]
                yb = ybf[ti % 2]
                p.dma("sp", x[:, :, 0:Tt], xv[:, :, t0:t0 + Tt], reads=[xT_d], writes=[x])
                p.dma("sp", y[:, :, 0:Tt], yv[:, :, t0:t0 + Tt], reads=[yT_d], writes=[y])
                p.op("pool", lambda e, y=y, yb=yb, Tt=Tt: e.tensor_copy(out=yb[:, :, 0:Tt], in_=y[:, :, 0:Tt]),
                     reads=[y], writes=[yb])
                if slotmajor:
                    for ob in range(4):
                        pgb = pg[ob % 2]
                        for a in range(4):
                            _mm(p, pgb, pgb[0:64, 0:Tt], gw, gw[:, a, ob * 64:(ob + 1) * 64], yb, yb[0:64, 2 * a, 0:Tt], a == 0, a == 3)
                        p.op("act", lambda e, ob=ob, pgb=pgb, Tt=Tt: e.activation(out=gate[0:64, ob, 0:Tt], in_=pgb[0:64, 0:Tt],
                                                                               func=AF.Sigmoid, bias=gb[:, ob:ob + 1], scale=1.0),
                             reads=[pgb, gb], writes=[gate])
                    p.op("dve", lambda e, y=y, yb=yb, Tt=Tt: e.tensor_tensor(
                        out=yb[0:64, :, 0:Tt].rearrange("p (a two) t -> p a two t", two=2)[:, :, 0, :],
                        in0=y[0:64, :, 0:Tt].rearrange("p (a two) t -> p a two t", two=2)[:, :, 0, :],
                        in1=gate[0:64, :, 0:Tt], op=ALU.mult), reads=[y, gate], writes=[yb])
                else:
                    for oc in range(2):
                        for kc in range(2):
                            _mm(p, pg[oc], pg[oc][:, 0:Tt], gw, gw[:, kc, oc * 128:(oc + 1) * 128], yb, yb[:, kc, 0:Tt],
                                kc == 0, kc == 1)
                        p.op("act", lambda e, oc=oc, Tt=Tt: e.activation(out=gate[:, oc, 0:Tt], in_=pg[oc][:, 0:Tt],
                                                                         func=AF.Sigmoid, bias=gb[:, oc:oc + 1], scale=1.0),
                             reads=[pg[oc], gb], writes=[gate])
                    p.op("dve", lambda e, y=y, yb=yb, Tt=Tt: e.tensor_tensor(out=yb[:, 0:2, 0:Tt], in0=y[:, 0:2, 0:Tt],
                                                                             in1=gate[:, :, 0:Tt], op=ALU.mult),
                         reads=[y, gate], writes=[yb])
                for oc in range(8):
                    pmb = pm[nm % 4]
                    nm += 1
                    for kc in range(8):
                        _mm(p, pmb, pmb[:, 0:Tt], wo, wo[:, kc, oc * 128:(oc + 1) * 128], yb, yb[:, kc, 0:Tt], kc == 0, kc == 7)
                    p.op("dve", lambda e, pmb=pmb, x=x, oc=oc, Tt=Tt, col=col: e.scalar_tensor_tensor(
                        out=x[:, oc, 0:Tt], in0=pmb[:, 0:Tt], scalar=modT[:, oc, col:col + 1], in1=x[:, oc, 0:Tt],
                        op0=ALU.mult, op1=ALU.add), reads=[pmb, x, modT], writes=[x])
                p.dma("act", x1v[:, :, t0:t0 + Tt], x[:, :, 0:Tt], reads=[x])
            p.barrier()
        with ExitStack() as esb:
            w1 = p.sb(esb, [128, 8, DFF], BF16, "w1")
            w2 = p.sb(esb, [128, 32, D], BF16, "w2")
            with ExitStack() as es2:
                load_weight_bf16(p, es2, w1_d, w1, D, DFF, 512)
                load_weight_bf16(p, es2, w2_d, w2, DFF, D, 128)
                p.barrier()
            xs = [p.sb(esb, [128, 8, T], F32, "x") for _ in range(2)]
            sq = p.sb(esb, [128, 8, T], BF16, "sq")
            hn = p.sb(esb, [128, 8, T], F32, "hn")
            hbf = p.sb(esb, [128, 8, T], BF16, "hbf")
            rstd = p.sb(esb, [128, T], F32, "rstd")
            rl = [p.sb(esb, [128, T], F32, "rl") for _ in range(2)]
            act = p.sb(esb, [128, 32, T], BF16, "act")
            fing = p.sb(esb, [128, 8], F32, "fing")
            ssum = p.ps(esb, [128, T], F32, "ssum")
            ph = [p.ps(esb, [128, T], F32, "ph") for _ in range(3)]
            po = [p.ps(esb, [128, T], F32, "po") for _ in range(2)]
            if fin_d is not None:
                p.dma("sp", fing[:], fin_d[:], reads=[fin_d], writes=[fing])
            nh = 0
            no = 0
            for ti, (t0, Tt, col) in enumerate(tl):
                x = xs[ti % 2]
                p.dma("sp", x[:, :, 0:Tt], x1v[:, :, t0:t0 + Tt], reads=[x1_d], writes=[x])
                emit_norm_mod(p, x, Tt, sq, ssum, rstd, hn, hbf, ones, gsc, modT, 8, col)
                for hc in range(32):
                    phb = ph[nh % 3]
                    rlb = rl[nh % 2]
                    nh += 1
                    for kc in range(8):
                        _mm(p, phb, phb[:, 0:Tt], w1, w1[:, kc, hc * 128:(hc + 1) * 128], hbf, hbf[:, kc, 0:Tt], kc == 0, kc == 7)
                    p.op("act", lambda e, phb=phb, rlb=rlb, Tt=Tt: e.activation(out=rlb[:, 0:Tt], in_=phb[:, 0:Tt], func=AF.Relu),
                         reads=[phb], writes=[rlb])
                    p.op("dve", lambda e, rlb=rlb, hc=hc, Tt=Tt: e.tensor_tensor(out=act[:, hc, 0:Tt], in0=rlb[:, 0:Tt],
                                                                                 in1=rlb[:, 0:Tt], op=ALU.mult),
                         reads=[rlb], writes=[act])
                for oc in range(8):
                    pob = po[no % 2]
                    no += 1
                    for hc in range(32):
                        _mm(p, pob, pob[:, 0:Tt], w2, w2[:, hc, oc * 128:(oc + 1) * 128], act, act[:, hc, 0:Tt], hc == 0, hc == 31)
                    p.op("dve", lambda e, pob=pob, x=x, oc=oc, Tt=Tt, col=col: e.scalar_tensor_tensor(
                        out=x[:, oc, 0:Tt], in0=pob[:, 0:Tt], scalar=modT[:, 24 + oc, col:col + 1], in1=x[:, oc, 0:Tt],
                        op0=ALU.mult, op1=ALU.add), reads=[pob, x, modT], writes=[x])
                p.dma("act", xov[:, :, t0:t0 + Tt], x[:, :, 0:Tt], reads=[x])
                if fin_d is not None and col == 0:
                    fov = fo_d.t.rearrange("(c p) t -> p c t", p=128)
                    p.op("act", lambda e, x=x, Tt=Tt: e.activation(out=sq[:, :, 0:Tt], in_=x[:, :, 0:Tt], func=AF.Square),
                         reads=[x], writes=[sq])
                    for c in range(8):
                        _mm(p, ssum, ssum[:, 0:Tt], ones, ones[:], sq, sq[:, c, 0:Tt], c == 0, c == 7)
                    emit_rsqrt(p, rstd, ssum, Tt, EPS)
                    p.op("dve", lambda e, x=x, Tt=Tt: e.tensor_tensor(out=hn[:, :, 0:Tt], in0=x[:, :, 0:Tt],
                                                                      in1=rstd[:, 0:Tt].unsqueeze(1).to_broadcast([128, 8, Tt]),
                                                                      op=ALU.mult), reads=[x, rstd], writes=[hn])
                    p.op("dve", lambda e, Tt=Tt: e.tensor_tensor(out=hn[:, :, 0:Tt], in0=hn[:, :, 0:Tt],
                                                                 in1=fing[:].unsqueeze(2).to_broadcast([128, 8, Tt]),
                                                                 op=ALU.mult), reads=[hn, fing], writes=[hn])
                    p.dma("act", fov[:, :, t0 - fo_off:t0 - fo_off + Tt], hn[:, :, 0:Tt], reads=[hn])
            p.barrier()


def build_d2(final):
    nc = bass.Bass("TRN2", target_bir_lowering=False)
    with ExitStack() as es:
        p = Prog(nc, es)
        xT = p.dram("xT", [D, NT], F32, "ExternalInput")
        yT = p.dram("yT", [D, NT], F32, "ExternalInput")
        cond = p.dram("cond", [128, 8, 2], F32, "ExternalInput")
        adaw = p.dram("ada_w", [D, 6 * D], F32, "ExternalInput")
        adab = p.dram("ada_b", [128, 48], F32, "ExternalInput")
        g = p.dram("norm_g", [128, 8], F32, "ExternalInput")
        wout = p.dram("w_out", [D, D], F32, "ExternalInput")
        gluw = p.dram("glu_w", [256, 256], F32, "ExternalInput")
        glub = p.dram("glu_b", [128, 2], F32, "ExternalInput")
        w1 = p.dram("w1", [D, DFF], F32, "ExternalInput")
        w2 = p.dram("w2", [DFF, D], F32, "ExternalInput")
        x1 = p.dram("x1", [D, NT], F32, "Internal")
        xo = p.dram("xo", [D, NT], F32, "ExternalOutput")
        fin = fo = None
        if final:
            fin = p.dram("final_g", [128, 8], F32, "ExternalInput")
            fo = p.dram("fo", [D, NLAT], F32, "ExternalOutput")
        emit_d2(p, xT, yT, cond, adaw, adab, g, wout, gluw, glub, w1, w2, x1, xo, fin, fo)
        p.finish()
    return nc


LCTX = 256
LLAT = 16384
LSEQ = LCTX + LLAT
TT = 512
CH = 64
ZROWS = 848
R_S5, R_RWR, R_RWK, R_RWV, R_RWWA, R_RWG = 0, 64, 128, 192, 256, 320
R_RETQ, R_RETK, R_RETV, R_RETG = 384, 448, 512, 576
R_GLAQ, R_GLAK, R_GLAV, R_GLAG, R_GLAA = 640, 672, 704, 768, 832
C_ID, C_M4, C_ML, C_CM, C_JS, C_PR, C_ROPE, C_JJ = 0, 128, 256, 320, 832, 960, 1024, 1536
NCONST = 2048


def make_consts():
    c = np.zeros((128, NCONST), np.float32)
    c[:, C_ID:C_ID + 128] = np.eye(128, dtype=np.float32)
    s = np.arange(64)[:, None]
    t = np.arange(64)[None, :]
    strict = (s < t).astype(np.float32)
    incl = (s <= t).astype(np.float32)
    m4 = np.block([[strict, incl], [strict, incl]])
    c[:, C_M4:C_M4 + 128] = m4
    c[0:64, C_ML:C_ML + 64] = (s > t).astype(np.float32)
    cm = np.ones(512, np.float32)
    cm[::64] = 0.0
    c[:, C_CM:C_CM + 512] = cm[None, :]
    js = np.zeros((128, 128), np.float32)
    for m in range(64):
        js[m + 64, m] = -1.0
        js[m, m + 64] = 1.0
    c[:, C_JS:C_JS + 128] = js
    pr = np.zeros((64, 64), np.float32)
    for h in (0, 32):
        for m in range(16):
            pr[h + m + 16, h + m] = -1.0
            pr[h + m, h + m + 16] = 1.0
    c[0:64, C_PR:C_PR + 64] = pr
    inv = (10000.0 ** (-np.arange(16, dtype=np.float32) / 16)).astype(np.float32)
    fr = np.concatenate([inv, inv])
    ang_r = np.arange(256, dtype=np.float32)[None, :] * fr[:, None]
    ang_c = np.arange(64, dtype=np.float32)[None, :] * fr[:, None]
    c[0:32, C_ROPE:C_ROPE + 256] = np.cos(ang_r)
    c[0:32, C_ROPE + 256:C_ROPE + 512] = np.sin(ang_r)
    c[32:64, C_ROPE:C_ROPE + 64] = np.cos(ang_c)
    c[32:64, C_ROPE + 64:C_ROPE + 128] = np.sin(ang_c)
    c[:, C_JJ:C_JJ + 512] = np.arange(1, 513, dtype=np.float32)[None, :]
    return c


class SCtx:
    pass


def s_tiles(rev):
    lat = [(LCTX + i * TT, TT, False) for i in range(LLAT // TT)]
    if rev:
        lat = lat[::-1]
    return [(0, LCTX, True)] + lat


def rv(ap_fn, rev):
    return ap_fn


def emit_chunkscan(p, S, pre, dk, W, dplr, Hst, yout_ap_fn):
    nch = W // CH
    cst = S.cst
    import os
    stop = int(os.environ.get("S_STOP", "9"))
    if stop <= 1:
        return yout_ap_fn(S.psY, S.psA[0:64].rearrange("p a b -> p (a b)"), S.psA)
    QR, KD, LW = pre["QR"], pre["KD"], pre["LW"]
    cs, ex, tmp = S.cs, S.ex, S.tmp
    ARp, BKm, BKh, Wc = S.ARp, S.BKm, S.BKh, S.Wc
    dv = lambda b: b
    p.op("dve", lambda e: e.tensor_tensor_scan(out=cs[0:dk, 0:W], data0=cst[0:dk, C_CM:C_CM + W], data1=LW[0:dk, 0:W],
                                               initial=0.0, op0=ALU.mult, op1=ALU.add), reads=[cst, LW], writes=[cs])
    cs3 = lambda: cs[0:dk, 0:W].rearrange("p (c j) -> p c j", j=CH)
    tot_b = lambda: cs3()[:, :, CH - 1:CH].to_broadcast([dk, nch, CH])

    def v3(buf, idx):
        return buf[0:dk, 0:nch, idx, :]

    def t3(buf):
        return buf[0:dk, 0:W].rearrange("p (c j) -> p c j", j=CH)

    p.op("act", lambda e: e.activation(out=ex[0:dk, 0:W], in_=cs[0:dk, 0:W], func=AF.Exp), reads=[cs], writes=[ex])
    p.op("dve", lambda e: e.tensor_tensor(out=v3(ARp, 1), in0=t3(QR), in1=t3(ex), op=ALU.mult), reads=[QR, ex], writes=[ARp])
    p.op("act", lambda e: e.activation(out=Wc[0:dk, 0:nch], in_=cs3()[:, :, CH - 1], func=AF.Exp), reads=[cs], writes=[Wc])
    if dplr:
        AV, BV = pre["AV"], pre["BV"]
        p.op("dve", lambda e: e.tensor_tensor(out=tmp[0:dk, 0:W], in0=cs[0:dk, 0:W], in1=LW[0:dk, 0:W], op=ALU.subtract),
             reads=[cs, LW], writes=[tmp])
        p.op("act", lambda e: e.activation(out=tmp[0:dk, 0:W], in_=tmp[0:dk, 0:W], func=AF.Exp), reads=[tmp], writes=[tmp])
        p.op("dve", lambda e: e.tensor_tensor(out=v3(ARp, 0), in0=t3(AV), in1=t3(tmp), op=ALU.mult), reads=[AV, tmp], writes=[ARp])
    p.op("act", lambda e: e.activation(out=ex[0:dk, 0:W], in_=cs[0:dk, 0:W], func=AF.Exp, scale=-1.0), reads=[cs], writes=[ex])
    p.op("dve", lambda e: e.tensor_tensor(out=v3(BKm, 1), in0=t3(KD), in1=t3(ex), op=ALU.mult), reads=[KD, ex], writes=[BKm])
    if dplr:
        p.op("pool", lambda e: e.tensor_tensor(out=v3(BKm, 0), in0=t3(BV), in1=t3(ex), op=ALU.mult), reads=[BV, ex], writes=[BKm])
    p.op("dve", lambda e: e.tensor_tensor(out=t3(tmp), in0=tot_b(), in1=cs3(), op=ALU.subtract), reads=[cs], writes=[tmp])
    p.op("act", lambda e: e.activation(out=tmp[0:dk, 0:W], in_=tmp[0:dk, 0:W], func=AF.Exp), reads=[tmp], writes=[tmp])
    p.op("pool", lambda e: e.tensor_tensor(out=v3(BKh, 1), in0=t3(KD), in1=t3(tmp), op=ALU.mult), reads=[KD, tmp], writes=[BKh])
    if dplr:
        p.op("pool", lambda e: e.tensor_tensor(out=v3(BKh, 0), in0=t3(BV), in1=t3(tmp), op=ALU.mult), reads=[BV, tmp], writes=[BKh])
    if stop <= 2:
        return yout_ap_fn(S.psY, S.psA[0:64].rearrange("p a b -> p (a b)"), S.psA)
    AT, VU, BKT = S.AT, S.VU, S.BKT
    for half in range((nch + 3) // 4):
        psb = S.psS[half % 2]
        cl = list(range(half * 4, min(nch, half * 4 + 4)))
        for c in cl:
            _mm(p, psb, psb[:, c - half * 4, :], BKm, BKm[0:dk, c, :, :], ARp, ARp[0:dk, c, :, :], True, True)
        n = len(cl)
        p.op("dve", lambda e, psb=psb, half=half, n=n: e.tensor_tensor(
            out=AT[:, half * 4:half * 4 + n, :], in0=psb[:, 0:n, :],
            in1=cst[:, C_M4:C_M4 + 128].unsqueeze(1).to_broadcast([128, n, 128]), op=ALU.mult),
            reads=[psb, cst], writes=[AT])
    if stop <= 3:
        return yout_ap_fn(S.psY, S.psA[0:64].rearrange("p a b -> p (a b)"), S.psA)
    Vp = pre["V"]
    for c in range(nch):
        p.op("pe", lambda e, c=c: e.transpose(S.psT[:, c, :], Vp[:, c, :, :].rearrange("p a b -> p (a b)"), cst[0:64, C_ID:C_ID + 64]),
             reads=[Vp, cst], writes=[S.psT])
    p.op("act", lambda e: e.activation(out=VU[64:128, 0:nch, :], in_=S.psT[64:128, 0:nch, :], func=AF.Copy),
         reads=[S.psT], writes=[VU])
    for c in range(nch):
        p.op("pe", lambda e, c=c: e.transpose(S.psT2[:, c, 0:dk], BKh[0:dk, c, :, :].rearrange("p a b -> p (a b)"),
                                              cst[0:dk, C_ID:C_ID + dk]), reads=[BKh, cst], writes=[S.psT2])
    p.op("act", lambda e: e.activation(out=BKT[:, 0:nch, 0:dk], in_=S.psT2[:, 0:nch, 0:dk], func=AF.Copy),
         reads=[S.psT2], writes=[BKT])
    if stop <= 4:
        return yout_ap_fn(S.psY, S.psA[0:64].rearrange("p a b -> p (a b)"), S.psA)
    if dplr:
        Pm, Nj, Aj = S.Pm, S.Nj, S.Aj
        for c in range(nch):
            _mm(p, S.psA, S.psA[:, c, :], ARp, ARp[0:dk, c, 0, :], BKm, BKm[0:dk, c, 0, :], True, True)
        p.op("dve", lambda e: e.tensor_tensor(out=Aj[0][:, 0:nch, :], in0=S.psA[:, 0:nch, :],
                                              in1=cst[0:64, C_ML:C_ML + 64].unsqueeze(1).to_broadcast([64, nch, 64]),
                                              op=ALU.mult), reads=[S.psA, cst], writes=[Aj[0]])
        p.op("pool", lambda e: e.tensor_copy(out=Nj[0][:, 0:nch, :], in_=AT[0:64, 0:nch, 0:64]), reads=[AT], writes=[Nj[0]])
        p.op("pool", lambda e: e.tensor_tensor(out=Pm[:, 0:nch, :], in0=AT[0:64, 0:nch, 0:64],
                                               in1=cst[0:64, C_ID:C_ID + 64].unsqueeze(1).to_broadcast([64, nch, 64]),
                                               op=ALU.add), reads=[AT, cst], writes=[Pm])
        cur = 0
        for lvl in range(5):
            nxt = 1 - cur
            last = lvl == 4
            for c in range(nch):
                _mm(p, S.psA, S.psA[:, c, :], Nj[cur], Nj[cur][:, c, :], Aj[cur], Aj[cur][:, c, :], True, True)
            p.op("dve", lambda e, nxt=nxt: e.tensor_copy(out=Aj[nxt][:, 0:nch, :], in_=S.psA[:, 0:nch, :]),
                 reads=[S.psA], writes=[Aj[nxt]])
            if not last:
                for c in range(nch):
                    _mm(p, S.psN, S.psN[:, c, :], Aj[cur], Aj[cur][:, c, :], Nj[cur], Nj[cur][:, c, :], True, True)
                p.op("act", lambda e, nxt=nxt: e.activation(out=Nj[nxt][:, 0:nch, :], in_=S.psN[:, 0:nch, :], func=AF.Copy),
                     reads=[S.psN], writes=[Nj[nxt]])
            for c in range(nch):
                _mm(p, S.psP, S.psP[:, c, :], Aj[nxt], Aj[nxt][:, c, :], Pm, Pm[:, c, :], True, True)
            p.op("dve", lambda e: e.tensor_tensor(out=Pm[:, 0:nch, :], in0=Pm[:, 0:nch, :], in1=S.psP[:, 0:nch, :], op=ALU.add),
                 reads=[Pm, S.psP], writes=[Pm])
            cur = nxt
    if stop <= 5:
        return yout_ap_fn(S.psY, S.psA[0:64].rearrange("p a b -> p (a b)"), S.psA)
    psY, psY2 = S.psY, S.psA
    psY2f = psY2[0:64].rearrange("p a b -> p (a b)")
    rs = slice(0, 128) if dplr else slice(64, 128)
    for c in range(nch):
        if dplr:
            bX, bX2, bU = S.psN, S.psP, S.psS[0]
            aX, aX2, aU = bX[0:64, 0, :], bX2[0:64, 0, :], bU[0:64, 0, 0:64]
            _mm(p, bX, aX, AT, AT[64:128, c, 0:64], VU, VU[64:128, c, :], True, True)
            _mm(p, bX2, aX2, ARp, ARp[0:dk, c, 0, :], Hst, Hst[0:dk, :], True, True)
            p.op("act", lambda e: e.activation(out=S.Xs[:, :], in_=aX, func=AF.Copy), reads=[bX], writes=[S.Xs])
            p.op("dve", lambda e: e.tensor_tensor(out=S.Xs[:, :], in0=S.Xs[:, :], in1=aX2, op=ALU.add),
                 reads=[S.Xs, bX2], writes=[S.Xs])
            _mm(p, bU, aU, S.Pm, S.Pm[:, c, :], S.Xs, S.Xs[:, :], True, True)
            p.op("act", lambda e, c=c: e.activation(out=VU[0:64, c, :], in_=aU, func=AF.Copy), reads=[bU], writes=[VU])
        _mm(p, psY, psY[:, c * CH:(c + 1) * CH], VU, VU[rs, c, :], AT, AT[rs, c, 64:128], True, True)
        _mm(p, psY2, psY2f[:, c * CH:(c + 1) * CH], Hst, Hst[0:dk, :], ARp, ARp[0:dk, c, 1, :], True, True)
        _mm(p, S.psH, S.psH[0:dk, :], BKT, BKT[rs, c, 0:dk], VU, VU[rs, c, :], True, True)
        p.op("dve", lambda e, c=c: e.scalar_tensor_tensor(out=Hst[0:dk, :], in0=Hst[0:dk, :], scalar=Wc[0:dk, c:c + 1],
                                                          in1=S.psH[0:dk, :], op0=ALU.mult, op1=ALU.add),
             reads=[Hst, Wc, S.psH], writes=[Hst])
    yout_ap_fn(psY, psY2f, psY2)


PK = {}
_pk_n = 0


def _pk(name, n):
    global _pk_n
    PK[name] = (_pk_n, n)
    _pk_n += n


for _rg in ("r", "k", "v", "wa", "g"):
    _pk("mu_" + _rg, 1)
    _pk("msk_" + _rg, 6)
for _n in ("w0", "a0"):
    _pk(_n, 2)
for _n in ("k_k", "k_a", "r_k", "rw_ln_g", "rw_ln_b", "ret_ln_g", "gla_ln_g", "s5_d"):
    _pk(_n, 1)
_pk("WA", 128)
_pk("g_up", 64)
_pk("ret_dl", 2)
_pk("gla_aup", 64)
_pk("gla_ab", 2)
_pk("s5col", 24)
_pk("s5row", 8 * 3 * 64)
_pk("s5B", 8 * 2 * 64)
_pk("s5C", 8 * 64)
NPK = _pk_n
RW_COL0 = {"r": 0, "k": 256, "v": 512, "wa": 768, "g": 832}


def pack_params(inp, L, slot):
    pk = np.zeros((128, NPK), np.float32)

    def put(name, arr, r0=0):
        c0, n = PK[name]
        arr = np.asarray(arr, np.float32)
        if arr.ndim == 1:
            arr = arr[:, None]
        assert arr.shape[1] == n, (name, arr.shape, n)
        pk[r0:r0 + arr.shape[0], c0:c0 + n] = arr

    j = slot
    mu = inp["rw_mu"][L]
    cols = {"r": np.arange(64 * j, 64 * j + 64), "k": 256 + np.arange(64 * j, 64 * j + 64),
            "v": 512 + np.arange(64 * j, 64 * j + 64), "wa": np.arange(768, 832), "g": np.arange(832, 896)}
    for rg, cc in cols.items():
        put("mu_" + rg, mu[cc])
        q = cc // 224
        m = np.zeros((64, 6), np.float32)
        for qi in range(4):
            m[:, qi] = (q == qi)
        m[:, 4] = cc < 448
        m[:, 5] = cc >= 448
        put("msk_" + rg, m)
    hs = slice(64 * j, 64 * j + 64)
    put("w0", inp["rw_w0"][L][:, hs].T)
    put("a0", inp["rw_a0"][L][:, hs].T)
    put("k_k", inp["rw_k_k"][L][hs])
    put("k_a", inp["rw_k_a"][L][hs])
    put("r_k", inp["rw_r_k"][L][j])
    put("rw_ln_g", inp["rw_ln_g"][L][hs])
    put("rw_ln_b", inp["rw_ln_b"][L][hs])
    put("ret_ln_g", inp["ret_ln_g"][L][hs])
    put("gla_ln_g", inp["gla_ln_g"][L][hs])
    put("s5_d", inp["s5_d"][L][hs])
    wa = np.zeros((64, 2, 64), np.float32)
    wa[0:32] = inp["rw_w_up"][L][:, :, hs].transpose(1, 0, 2)
    wa[32:64] = inp["rw_a_up"][L][:, :, hs].transpose(1, 0, 2)
    put("WA", wa.reshape(64, 128))
    put("g_up", inp["rw_g_up"][L][:, hs])
    put("ret_dl", np.repeat(inp["ret_decay_logit"][L][:, j][None, :], 64, 0))
    put("gla_aup", inp["gla_a_up"][L][:, :, 32 * j:32 * j + 32].transpose(1, 0, 2).reshape(16, 64))
    put("gla_ab", inp["gla_a_b"][L][:, 32 * j:32 * j + 32].T)
    s5col = np.zeros((128, 8, 3), np.float32)
    s5row = np.zeros((64, 8, 3, 64), np.float32)
    s5B = np.zeros((64, 8, 2, 64), np.float32)
    s5C = np.zeros((128, 8, 64), np.float32)
    for gl in range(4):
        g = 4 * j + gl
        for d in range(2):
            gd = gl * 2 + d
            lre = inp["s5_lam_re"][L][d, g]
            lim = inp["s5_lam_im"][L][d, g]
            ldt = inp["s5_log_dt"][L][d, g]
            s5col[:, gd, 0] = np.concatenate([lre, lre])
            s5col[:, gd, 1] = np.concatenate([lim, lim])
            s5col[:, gd, 2] = ldt
            s5row[:, gd, 0, :] = lre[None, :]
            s5row[:, gd, 1, :] = lim[None, :]
            s5row[:, gd, 2, :] = ldt
            s5B[16 * gl:16 * gl + 16, gd, 0, :] = inp["s5_b_re"][L][d, g].T
            s5B[16 * gl:16 * gl + 16, gd, 1, :] = inp["s5_b_im"][L][d, g].T
            s5C[0:64, gd, 16 * gl:16 * gl + 16] = inp["s5_c_re"][L][d, g].T
            s5C[64:128, gd, 16 * gl:16 * gl + 16] = inp["s5_c_im"][L][d, g].T
    put("s5col", s5col.reshape(128, 24))
    put("s5row", s5row.reshape(64, -1))
    put("s5B", s5B.reshape(64, -1))
    put("s5C", s5C.reshape(128, -1))
    return pk


def pkc(S, name, rows=64, r0=0):
    c0, n = PK[name]
    return S.pk[r0:r0 + rows, c0:c0 + n]


def s_alloc_scan(p, es, S):
    f = lambda shape, name: p.sb(es, shape, F32, name)
    S.cs, S.ex, S.tmp = f([64, TT], "cs"), f([64, TT], "ex"), f([64, TT], "tmp")
    S.ARp, S.BKm, S.BKh = f([64, 8, 2, 64], "ARp"), f([64, 8, 2, 64], "BKm"), f([64, 8, 2, 64], "BKh")
    S.Wc = f([64, 8], "Wc")
    S.AT, S.VU, S.BKT = f([128, 8, 128], "AT"), f([128, 8, 64], "VU"), f([128, 8, 64], "BKT")
    fb = lambda shape, name: p.sb(es, shape, BF16, name)
    S.Pm = fb([64, 8, 64], "Pm")
    S.Nj = [fb([64, 8, 64], "Nj") for _ in range(2)]
    S.Aj = [fb([64, 8, 64], "Aj") for _ in range(2)]
    S.Xs = fb([64, 64], "Xs")
    S.psS = [p.ps(es, [128, 4, 128], F32, "psS") for _ in range(2)]
    S.psT = p.ps(es, [128, 8, 64], F32, "psT")
    S.psT2 = S.psT
    S.psA = p.ps(es, [64, 8, 64], F32, "psA")
    S.psN = p.ps(es, [64, 8, 64], F32, "psN")
    S.psP = p.ps(es, [64, 8, 64], F32, "psP")
    misc = p.ps(es, [128, 8, 64], F32, "psmisc").t
    S.psX, S.psU, S.psH = Buf(misc[0:64, 0, :], "psX"), Buf(misc[0:64, 1, :], "psU"), Buf(misc[0:64, 2, :], "psH")
    S.psX2 = Buf(misc[0:64, 3, :], "psX2")
    S.psY = p.ps(es, [64, TT], F32, "psY")
    S.QR, S.KD, S.LW, S.AV, S.BV = (f([64, TT], n) for n in ("QR", "KD", "LW", "AV", "BV"))
    S.Vp = f([64, 8, 2, 64], "Vp")
    S.ones64 = f([64, 64], "ones64")
    S.mean64 = f([64, 64], "mean64")
    for b in (S.ARp, S.BKm, S.BKh, S.Vp, S.VU):
        p.op("pool", lambda e, b=b: e.memset(b[:], 0.0), writes=[b])
    p.op("pool", lambda e: e.memset(S.ones64[:], 1.0), writes=[S.ones64])
    p.op("pool", lambda e: e.memset(S.mean64[:], 1.0 / 64), writes=[S.mean64])


def s_load(p, S, dst, rows, r0, t0, nt, rev, stage, col0=0):
    if not rev:
        p.dma("sp", dst[0:rows, col0:col0 + nt], S.zsrc(r0, rows, t0, nt), writes=[dst])
    else:
        p.dma("sp", stage[0:rows, 0:nt], S.zsrc(r0, rows, t0, nt), writes=[stage])
        p.op("pool", lambda e: e.tensor_copy(out=dst[0:rows, col0:col0 + nt], in_=stage[0:rows, 0:nt][:, ::-1]),
             reads=[stage], writes=[dst])


def s_setup_scan(p, es, S):
    f = lambda shape, name: p.sb(es, shape, F32, name)
    S.coef = {}
    for rg in ("r", "k", "v", "wa", "g"):
        cf = f([64, 8], "coef_" + rg)
        mu = pkc(S, "mu_" + rg)
        p.op("dve", lambda e, cf=cf, mu=mu: e.tensor_scalar(out=cf[:, 0:1], in0=mu, scalar1=-1.0, scalar2=1.0,
                                                            op0=ALU.mult, op1=ALU.add), reads=[S.pkb], writes=[cf])
        p.op("dve", lambda e, cf=cf, mu=mu, rg=rg: e.tensor_tensor(out=cf[:, 1:7], in0=pkc(S, "msk_" + rg),
                                                                    in1=mu.to_broadcast([64, 6]), op=ALU.mult),
             reads=[S.pkb], writes=[cf])
        S.coef[rg] = cf
    S.der = f([64, 8], "der")
    der = S.der
    p.op("dve", lambda e: e.tensor_scalar(out=der[:, 0:1], in0=pkc(S, "k_a"), scalar1=-1.0, scalar2=1.0,
                                          op0=ALU.mult, op1=ALU.add), reads=[S.pkb], writes=[der])
    p.op("act", lambda e: e.activation(out=der[:, 1:3], in_=pkc(S, "ret_dl"), func=AF.Exp, scale=-1.0), reads=[S.pkb], writes=[der])
    p.op("act", lambda e: e.activation(out=der[:, 1:3], in_=der[:, 1:3], func=AF.Ln, bias=1.0, scale=1.0), reads=[der], writes=[der])
    p.op("dve", lambda e: e.tensor_scalar(out=der[:, 1:3], in0=der[:, 1:3], scalar1=-1.0, scalar2=None, op0=ALU.mult),
         reads=[der], writes=[der])
    p.op("dve", lambda e: e.tensor_scalar(out=der[0:32, 3:5], in0=pkc(S, "gla_ab", 32), scalar1=-1.0, scalar2=None, op0=ALU.mult),
         reads=[S.pkb], writes=[der])


def shift_active(slot, rg):
    if slot is None:
        return {1, 2, 3, 4, 5, 6}
    j = slot
    cc = {"r": np.arange(64 * j, 64 * j + 64), "k": 256 + np.arange(64 * j, 64 * j + 64),
          "v": 512 + np.arange(64 * j, 64 * j + 64), "wa": np.arange(768, 832), "g": np.arange(832, 896)}[rg]
    out = set(int(q) + 1 for q in np.unique(cc // 224))
    if (cc < 448).any():
        out.add(5)
    if (cc >= 448).any():
        out.add(6)
    return out


def s_prep_rwkv(p, S, d, t0, W, is_ctx, rev, zt, stage):
    cst = S.cst
    H = 0 if is_ctx else 64
    lo, hi = (0, LCTX) if is_ctx else (LCTX, LSEQ)
    a0_ = max(lo, t0 - H)
    a1_ = min(hi, t0 + W + H)
    rowmap = {"r": R_RWR, "k": R_RWK, "v": R_RWV, "wa": R_RWWA, "g": R_RWG}
    zm = S.zm
    for rg in ("r", "k", "v", "wa", "g"):
        z = zt[rg]
        full = (a0_ == t0 - H) and (a1_ == t0 + W + H)
        if not full:
            p.op("pool", lambda e, z=z: e.memset(z[:, 0:W + 2 * H], 0.0), writes=[z])
        off = a0_ - (t0 - H)
        n = a1_ - a0_
        if rev:
            off = (t0 + W + H) - a1_
        s_load(p, S, z, 64, rowmap[rg], a0_, n, rev, stage, col0=off)
        cf = S.coef[rg]
        o = zm[rg]
        iL, iR, iU, iD, iP, iN = (2, 1, 4, 3, 6, 5) if rev else (1, 2, 3, 4, 5, 6)
        p.op("dve", lambda e, z=z, o=o, cf=cf: e.tensor_scalar(out=o[:, 0:W], in0=z[:, H:H + W], scalar1=cf[:, 0:1], scalar2=None,
                                                               op0=ALU.mult), reads=[z, cf], writes=[o])

        def fma(oap, zap, ci, z=z, o=o, cf=cf):
            p.op("dve", lambda e: e.scalar_tensor_tensor(out=oap, in0=zap, scalar=cf[:, ci:ci + 1], in1=oap,
                                                         op0=ALU.mult, op1=ALU.add), reads=[z, o, cf], writes=[o])
        act_ = shift_active(getattr(S, "slot", None), rg)
        if is_ctx:
            if iP in act_:
                fma(o[:, 1:W], z[:, 0:W - 1], iP)
            if iN in act_:
                fma(o[:, 0:W - 1], z[:, 1:W], iN)
        else:
            if iU in act_:
                fma(o[:, 0:W], z[:, H - 64:H - 64 + W], iU)
            if iD in act_:
                fma(o[:, 0:W], z[:, H + 64:H + 64 + W], iD)
            o3 = o[:, 0:W].rearrange("p (r j) -> p r j", j=64)
            z3 = z[:, H:H + W].rearrange("p (r j) -> p r j", j=64)
            if iL in act_:
                fma(o3[:, :, 1:64], z3[:, :, 0:63], iL)
            if iR in act_:
                fma(o3[:, :, 0:63], z3[:, :, 1:64], iR)
    r, k, v, wa, gl = zm["r"], zm["k"], zm["v"], zm["wa"], zm["g"]
    nch = W // CH
    p.op("act", lambda e: e.activation(out=wa[0:32, 0:W], in_=wa[0:32, 0:W], func=AF.Tanh), reads=[wa], writes=[wa])
    WA = pkc(S, "WA").rearrange("p (d c) -> p d c", d=2)
    psw, psa = S.psS[0], S.psS[1]
    pswf = psw[0:64].rearrange("p a b -> p (a b)")
    psaf = psa[0:64].rearrange("p a b -> p (a b)")
    _mm(p, psw, pswf[:, 0:W], S.pkb, WA[0:32, d, :], wa, wa[0:32, 0:W], True, True)
    _mm(p, psa, psaf[:, 0:W], S.pkb, WA[32:64, d, :], wa, wa[32:64, 0:W], True, True)
    LW, QR, KD, AV, BV = S.LW, S.QR, S.KD, S.AV, S.BV
    asig = S.ex
    p.op("act", lambda e: e.activation(out=LW[:, 0:W], in_=pswf[:, 0:W], func=AF.Sigmoid, bias=pkc(S, "w0")[:, d:d + 1], scale=1.0),
         reads=[psw, S.pkb], writes=[LW])
    p.op("dve", lambda e: e.tensor_scalar(out=LW[:, 0:W], in0=LW[:, 0:W], scalar1=-float(np.exp(-0.5)), scalar2=None, op0=ALU.mult),
         reads=[LW], writes=[LW])
    p.op("act", lambda e: e.activation(out=asig[:, 0:W], in_=psaf[:, 0:W], func=AF.Sigmoid, bias=pkc(S, "a0")[:, d:d + 1], scale=1.0),
         reads=[psa, S.pkb], writes=[asig])
    kk, t2 = S.cs, S.tmp
    p.op("dve", lambda e: e.tensor_scalar(out=kk[:, 0:W], in0=k[:, 0:W], scalar1=pkc(S, "k_k"), scalar2=None, op0=ALU.mult),
         reads=[k, S.pkb], writes=[kk])
    p.op("dve", lambda e: e.tensor_tensor(out=t2[:, 0:W], in0=kk[:, 0:W], in1=kk[:, 0:W], op=ALU.mult), reads=[kk], writes=[t2])
    psk = S.psY
    _mm(p, psk, psk[:, 0:W], S.ones64, S.ones64[:], t2, t2[:, 0:W], True, True)
    emit_rsqrt(p, t2, psk, W, 1e-12, parts=64, fast=True)
    p.op("dve", lambda e: e.tensor_tensor(out=kk[:, 0:W], in0=kk[:, 0:W], in1=t2[:, 0:W], op=ALU.mult), reads=[kk, t2], writes=[kk])
    p.op("dve", lambda e: e.tensor_scalar(out=AV[:, 0:W], in0=kk[:, 0:W], scalar1=-1.0, scalar2=None, op0=ALU.mult),
         reads=[kk], writes=[AV])
    p.op("dve", lambda e: e.tensor_tensor(out=BV[:, 0:W], in0=kk[:, 0:W], in1=asig[:, 0:W], op=ALU.mult), reads=[kk, asig], writes=[BV])
    p.op("dve", lambda e: e.tensor_scalar(out=t2[:, 0:W], in0=asig[:, 0:W], scalar1=pkc(S, "k_a"), scalar2=S.der[:, 0:1],
                                          op0=ALU.mult, op1=ALU.add), reads=[asig, S.pkb, S.der], writes=[t2])
    p.op("dve", lambda e: e.tensor_tensor(out=KD[:, 0:W], in0=k[:, 0:W], in1=t2[:, 0:W], op=ALU.mult), reads=[k, t2], writes=[KD])
    p.op("pool", lambda e: e.tensor_copy(out=QR[:, 0:W], in_=r[:, 0:W]), reads=[r], writes=[QR])
    p.op("pool", lambda e: e.tensor_copy(out=S.Vp[:, 0:nch, 1, :], in_=v[:, 0:W].rearrange("p (c j) -> p c j", j=CH)),
         reads=[v], writes=[S.Vp])
    if not rev:
        aux = S.auxo
        p.dma("act", S.aux_d[0][:, t0:t0 + W], v[:, 0:W], reads=[v])
        p.op("act", lambda e: e.activation(out=gl[:, 0:W], in_=gl[:, 0:W], func=AF.Sigmoid), reads=[gl], writes=[gl])
        _mm(p, psw, pswf[:, 0:W], S.pkb, pkc(S, "g_up"), gl, gl[:, 0:W], True, True)
        p.op("act", lambda e: e.activation(out=aux[0][:, 0:W], in_=pswf[:, 0:W], func=AF.Copy), reads=[psw], writes=[aux[0]])
        p.dma("act", S.aux_d[1][:, t0:t0 + W], aux[0][:, 0:W], reads=[aux[0]])
        p.op("dve", lambda e: e.scalar_tensor_tensor(out=aux[1][:, 0:W], in0=r[:, 0:W], scalar=pkc(S, "r_k"), in1=k[:, 0:W],
                                                     op0=ALU.mult, op1=ALU.mult), reads=[r, k, S.pkb], writes=[aux[1]])
        _mm(p, psa, psaf[:, 0:W], S.ones64, S.ones64[:], aux[1], aux[1][:, 0:W], True, True)
        p.op("act", lambda e: e.activation(out=aux[1][:, 0:W], in_=psaf[:, 0:W], func=AF.Copy), reads=[psa], writes=[aux[1]])
        p.dma("act", S.aux_d[2][:, t0:t0 + W], aux[1][:, 0:W], reads=[aux[1]])
    return dict(QR=QR, KD=KD, V=S.Vp, LW=LW, AV=AV, BV=BV)


def s_rope(p, S, x, t0, W, rev):
    cst = S.cst
    nr = W // 64
    r0 = (t0 - LCTX) // 64
    ps = S.psS[0]
    psf = ps[0:64].rearrange("p a b -> p (a b)")
    _mm(p, ps, psf[:, 0:W], cst, cst[0:64, C_PR:C_PR + 64], x, x[:, 0:W], True, True)
    xr = S.tmp
    sl = slice(None, None, -1) if rev else slice(None)
    for (pp, tabc, tabs) in ((slice(0, 32), cst[0:32, C_ROPE + r0:C_ROPE + r0 + nr][:, sl].unsqueeze(2).to_broadcast([32, nr, 64]),
                              cst[0:32, C_ROPE + 256 + r0:C_ROPE + 256 + r0 + nr][:, sl].unsqueeze(2).to_broadcast([32, nr, 64])),
                             (slice(32, 64), cst[32:64, C_ROPE:C_ROPE + 64][:, sl].unsqueeze(1).to_broadcast([32, nr, 64]),
                              cst[32:64, C_ROPE + 64:C_ROPE + 128][:, sl].unsqueeze(1).to_broadcast([32, nr, 64]))):
        x3 = x[pp, 0:W].rearrange("p (r j) -> p r j", j=64)
        xr3 = xr[pp, 0:W].rearrange("p (r j) -> p r j", j=64)
        ps3 = psf[pp, 0:W].rearrange("p (r j) -> p r j", j=64)
        p.op("dve", lambda e, xr3=xr3, ps3=ps3, tabs=tabs: e.tensor_tensor(out=xr3, in0=ps3, in1=tabs, op=ALU.mult),
             reads=[ps, cst], writes=[xr])
        p.op("dve", lambda e, x3=x3, tabc=tabc: e.tensor_tensor(out=x3, in0=x3, in1=tabc, op=ALU.mult), reads=[x, cst], writes=[x])
    p.op("dve", lambda e: e.tensor_tensor(out=x[:, 0:W], in0=x[:, 0:W], in1=xr[:, 0:W], op=ALU.add), reads=[x, xr], writes=[x])


def s_prep_ret(p, S, d, t0, W, is_ctx, rev, stage):
    QR, KD, LW = S.QR, S.KD, S.LW
    nch = W // CH
    s_load(p, S, QR, 64, R_RETQ, t0, W, rev, stage)
    s_load(p, S, KD, 64, R_RETK, t0, W, rev, stage)
    v = S.AV
    s_load(p, S, v, 64, R_RETV, t0, W, rev, stage)
    p.op("pool", lambda e: e.tensor_copy(out=S.Vp[:, 0:nch, 1, :], in_=v[:, 0:W].rearrange("p (c j) -> p c j", j=CH)),
         reads=[v], writes=[S.Vp])
    if not is_ctx:
        s_rope(p, S, QR, t0, W, rev)
        s_rope(p, S, KD, t0, W, rev)
    p.op("dve", lambda e: e.tensor_scalar(out=KD[:, 0:W], in0=KD[:, 0:W], scalar1=0.125, scalar2=None, op0=ALU.mult),
         reads=[KD], writes=[KD])
    p.op("dve", lambda e: e.tensor_scalar(out=LW[:, 0:W], in0=QR[:, 0:W], scalar1=0.0, scalar2=S.der[:, 1 + d:2 + d],
                                          op0=ALU.mult, op1=ALU.add), reads=[QR, S.der], writes=[LW])
    return dict(QR=QR, KD=KD, V=S.Vp, LW=LW)


def s_prep_gla(p, S, d, t0, W, is_ctx, rev, stage):
    QR, KD, LW = S.QR, S.KD, S.LW
    nch = W // CH
    s_load(p, S, QR, 32, R_GLAQ, t0, W, rev, stage)
    s_load(p, S, KD, 32, R_GLAK, t0, W, rev, stage)
    v = S.AV
    s_load(p, S, v, 64, R_GLAV, t0, W, rev, stage)
    al = S.BV
    s_load(p, S, al, 16, R_GLAA, t0, W, rev, stage)
    p.op("pool", lambda e: e.tensor_copy(out=S.Vp[:, 0:nch, 1, :], in_=v[:, 0:W].rearrange("p (c j) -> p c j", j=CH)),
         reads=[v], writes=[S.Vp])
    p.op("dve", lambda e: e.tensor_scalar(out=KD[0:32, 0:W], in0=KD[0:32, 0:W], scalar1=float(32 ** -0.5), scalar2=None, op0=ALU.mult),
         reads=[KD], writes=[KD])
    ps = S.psS[0]
    psf = ps[0:32].rearrange("p a b -> p (a b)")
    aup = pkc(S, "gla_aup", 16).rearrange("p (d c) -> p d c", d=2)
    _mm(p, ps, psf[:, 0:W], S.pkb, aup[:, d, :], al, al[0:16, 0:W], True, True)
    p.op("act", lambda e: e.activation(out=LW[0:32, 0:W], in_=psf[:, 0:W], func=AF.Exp, bias=S.der[0:32, 3 + d:4 + d], scale=-1.0),
         reads=[ps, S.der], writes=[LW])
    p.op("act", lambda e: e.activation(out=LW[0:32, 0:W], in_=LW[0:32, 0:W], func=AF.Ln, bias=1.0, scale=1.0), reads=[LW], writes=[LW])
    p.op("dve", lambda e: e.tensor_scalar(out=LW[0:32, 0:W], in0=LW[0:32, 0:W], scalar1=-1.0 / 16, scalar2=None, op0=ALU.mult),
         reads=[LW], writes=[LW])
    return dict(QR=QR, KD=KD, V=S.Vp, LW=LW)


def emit_s_scans(p, S, mixers=("rw", "ret", "gla")):
    with ExitStack() as es:
        s_alloc_scan(p, es, S)
        s_setup_scan(p, es, S)
        f = lambda shape, name: p.sb(es, shape, F32, name)
        zt = {rg: f([64, TT + 128], "z_" + rg) for rg in ("r", "k", "v", "wa", "g")}
        S.zm = {rg: f([64, TT], "zm_" + rg) for rg in ("r", "k", "v", "wa", "g")}
        stage = f([64, TT + 128], "stage")
        S.auxo = [f([64, TT], "auxo") for _ in range(2)]
        yo = [f([64, TT], "yo") for _ in range(2)]
        Hst = {m: f([64, 64], "H_" + m) for m in mixers}
        ny = 0
        for d in getattr(S, "dirs", (0, 1)):
            rev = d == 1
            for m in mixers:
                p.op("pool", lambda e, m=m: e.memset(Hst[m][:], 0.0), writes=[Hst[m]])
            for (t0, W, is_ctx) in s_tiles(rev):
                for m in mixers:
                    if m == "rw":
                        pre = s_prep_rwkv(p, S, d, t0, W, is_ctx, rev, zt, stage)
                        dk, dplr = 64, True
                    elif m == "ret":
                        pre = s_prep_ret(p, S, d, t0, W, is_ctx, rev, stage)
                        dk, dplr = 64, False
                    else:
                        pre = s_prep_gla(p, S, d, t0, W, is_ctx, rev, stage)
                        dk, dplr = 32, False
                    yb = yo[ny % 2]
                    ny += 1

                    def yout(psY, psY2f, psY2, yb=yb, m=m, d=d, t0=t0, W=W, rev=rev):
                        src = psY[:, 0:W][:, ::-1] if rev else psY[:, 0:W]
                        src2 = psY2f[:, 0:W][:, ::-1] if rev else psY2f[:, 0:W]
                        p.op("act", lambda e: e.activation(out=yb[:, 0:W], in_=src, func=AF.Copy), reads=[psY], writes=[yb])
                        p.op("dve", lambda e: e.tensor_tensor(out=yb[:, 0:W], in0=yb[:, 0:W], in1=src2, op=ALU.add),
                             reads=[yb, psY2], writes=[yb])
                        p.dma("act", S.ysc[(m, d)][:, t0:t0 + W], yb[:, 0:W], reads=[yb])
                    emit_chunkscan(p, S, pre, dk, W, dplr, Hst[m], yout)
        p.barrier()


def emit_s_post(p, S, do_s5=True, mixers=("rw", "ret", "gla")):
    with ExitStack() as es:
        f = lambda shape, name: p.sb(es, shape, F32, name)
        mean64 = f([64, 64], "mean64")
        p.op("pool", lambda e: e.memset(mean64[:], 1.0 / 64), writes=[mean64])
        yf = [f([64, TT], "yf") for _ in range(2)]
        yb = [f([64, TT], "yb") for _ in range(2)]
        ax = [[f([64, TT], "ax") for _ in range(3)] for _ in range(2)]
        sq = f([64, TT], "sq")
        rs = f([64, TT], "rs")
        ps1 = p.ps(es, [64, TT], F32, "ps1")
        ps2 = p.ps(es, [64, TT], F32, "ps2")
        k = 0
        for (t0, W, is_ctx) in s_tiles(False):
            for mi, m in enumerate(("s5",) * do_s5 + tuple(mixers)):
                a, b, x3 = yf[k % 2], yb[k % 2], ax[k % 2]
                k += 1
                p.dma("sp", a[:, 0:W], S.ysc[(m, 0)][:, t0:t0 + W], writes=[a])
                p.dma("sp", b[:, 0:W], S.ysc[(m, 1 if 1 in getattr(S, "dirs", (0, 1)) else 0)][:, t0:t0 + W], writes=[b])
                p.op("pool", lambda e, a=a, b=b: e.tensor_tensor(out=a[:, 0:W], in0=a[:, 0:W], in1=b[:, 0:W], op=ALU.add),
                     reads=[a, b], writes=[a])
                if m == "rw":
                    for i in range(3):
                        p.dma("sp", x3[i][:, 0:W], S.aux_d[i][:, t0:t0 + W], writes=[x3[i]])
                    _mm(p, ps1, ps1[:, 0:W], mean64, mean64[:], a, a[:, 0:W], True, True)
                    p.op("dve", lambda e, a=a: e.tensor_tensor(out=a[:, 0:W], in0=a[:, 0:W], in1=ps1[:, 0:W], op=ALU.subtract),
                         reads=[a, ps1], writes=[a])
                    p.op("act", lambda e, a=a: e.activation(out=sq[:, 0:W], in_=a[:, 0:W], func=AF.Square), reads=[a], writes=[sq])
                    _mm(p, ps2, ps2[:, 0:W], mean64, mean64[:], sq, sq[:, 0:W], True, True)
                    emit_rsqrt(p, rs, ps2, W, 64e-5, parts=64, fast=True)
                    p.op("dve", lambda e, a=a: e.tensor_tensor(out=a[:, 0:W], in0=a[:, 0:W], in1=rs[:, 0:W], op=ALU.mult),
                         reads=[a, rs], writes=[a])
                    p.op("dve", lambda e, a=a: e.tensor_scalar(out=a[:, 0:W], in0=a[:, 0:W], scalar1=pkc(S, "rw_ln_g"),
                                                               scalar2=pkc(S, "rw_ln_b"), op0=ALU.mult, op1=ALU.add),
                         reads=[a, S.pk], writes=[a])
                    p.op("pool", lambda e, x3=x3: e.tensor_tensor(out=x3[2][:, 0:W], in0=x3[2][:, 0:W], in1=x3[0][:, 0:W], op=ALU.mult),
                         reads=[x3[2], x3[0]], writes=[x3[2]])
                    p.op("dve", lambda e, a=a, x3=x3: e.tensor_tensor(out=a[:, 0:W], in0=a[:, 0:W], in1=x3[2][:, 0:W], op=ALU.add),
                         reads=[a, x3[2]], writes=[a])
                    p.op("dve", lambda e, a=a, x3=x3: e.tensor_tensor(out=a[:, 0:W], in0=a[:, 0:W], in1=x3[1][:, 0:W], op=ALU.mult),
                         reads=[a, x3[1]], writes=[a])
                    r0 = 64
                elif m in ("ret", "gla"):
                    g = x3[0]
                    p.dma("sp", g[:, 0:W], S.zsrc(R_RETG if m == "ret" else R_GLAG, 64, t0, W), writes=[g])
                    p.op("act", lambda e, a=a: e.activation(out=sq[:, 0:W], in_=a[:, 0:W], func=AF.Square), reads=[a], writes=[sq])
                    _mm(p, ps1, ps1[:, 0:W], mean64, mean64[:], sq, sq[:, 0:W], True, True)
                    emit_rsqrt(p, rs, ps1, W, 1e-6, parts=64, fast=True)
                    p.op("dve", lambda e, a=a, m=m: e.scalar_tensor_tensor(out=a[:, 0:W], in0=a[:, 0:W], scalar=pkc(S, m + "_ln_g"),
                                                                           in1=rs[:, 0:W], op0=ALU.mult, op1=ALU.mult),
                         reads=[a, rs, S.pk], writes=[a])
                    p.op("act", lambda e, g=g, x3=x3: e.activation(out=x3[1][:, 0:W], in_=g[:, 0:W], func=AF.Sigmoid),
                         reads=[g], writes=[x3[1]])
                    p.op("pool", lambda e, g=g, x3=x3: e.tensor_tensor(out=g[:, 0:W], in0=g[:, 0:W], in1=x3[1][:, 0:W], op=ALU.mult),
                         reads=[g, x3[1]], writes=[g])
                    p.op("dve", lambda e, a=a, g=g: e.tensor_tensor(out=a[:, 0:W], in0=a[:, 0:W], in1=g[:, 0:W], op=ALU.mult),
                         reads=[a, g], writes=[a])
                    r0 = 128 if m == "ret" else 192
                else:
                    u = x3[0]
                    p.dma("sp", u[:, 0:W], S.zsrc(R_S5, 64, t0, W), writes=[u])
                    p.op("dve", lambda e, a=a, u=u: e.scalar_tensor_tensor(out=a[:, 0:W], in0=u[:, 0:W], scalar=pkc(S, "s5_d"),
                                                                           in1=a[:, 0:W], op0=ALU.mult, op1=ALU.add),
                         reads=[a, u, S.pk], writes=[a])
                    p.op("act", lambda e, a=a: e.activation(out=sq[:, 0:W], in_=a[:, 0:W], func=AF.Square), reads=[a], writes=[sq])
                    p.op("dve", lambda e: e.tensor_scalar(out=sq[:, 0:W], in0=sq[:, 0:W], scalar1=0.044715, scalar2=1.0,
                                                          op0=ALU.mult, op1=ALU.add), reads=[sq], writes=[sq])
                    p.op("dve", lambda e, a=a: e.tensor_tensor(out=sq[:, 0:W], in0=sq[:, 0:W], in1=a[:, 0:W], op=ALU.mult),
                         reads=[sq, a], writes=[sq])
                    p.op("act", lambda e: e.activation(out=sq[:, 0:W], in_=sq[:, 0:W], func=AF.Tanh, scale=0.7978845608028654),
                         reads=[sq], writes=[sq])
                    p.op("dve", lambda e: e.tensor_scalar(out=sq[:, 0:W], in0=sq[:, 0:W], scalar1=1.0, scalar2=0.5,
                                                          op0=ALU.add, op1=ALU.mult), reads=[sq], writes=[sq])
                    p.op("dve", lambda e, a=a: e.tensor_tensor(out=a[:, 0:W], in0=a[:, 0:W], in1=sq[:, 0:W], op=ALU.mult),
                         reads=[a, sq], writes=[a])
                    r0 = 0
                p.dma("act", S.yout_d[r0:r0 + 64, t0:t0 + W], a[:, 0:W], reads=[a])
        p.barrier()


def emit_s(p, S, do_s5=True, mixers=("rw", "ret", "gla")):
    with ExitStack() as es:
        S.cst = p.sb(es, [128, NCONST], F32, "cst")
        S.pk = p.sb(es, [128, NPK], F32, "pk")
        S.pkb = S.pk
        p.dma("sp", S.cst[:], S.cst_d[:], writes=[S.cst])
        p.dma("sp", S.pk[:], S.pk_d[:], writes=[S.pk])
        if do_s5:
            emit_s_s5(p, S)
        emit_s_scans(p, S, mixers)
        emit_s_post(p, S, do_s5, mixers)


def build_s(do_s5=True, mixers=("rw", "ret", "gla"), dirs=(0, 1), slot=None):
    nc = bass.Bass("TRN2", target_bir_lowering=False)
    with ExitStack() as es:
        p = Prog(nc, es)
        S = SCtx()
        S.dirs = dirs
        S.slot = slot
        z = p.dram("zslot", [ZROWS, LSEQ], F32, "ExternalInput")
        S.zsrc = lambda r0, nr, t0, nt: z[r0:r0 + nr, t0:t0 + nt]
        S.cst_d = p.dram("consts", [128, NCONST], F32, "ExternalInput")
        S.pk_d = p.dram("pk", [128, NPK], F32, "ExternalInput")
        S.ysc = {(m, d): p.dram("ysc_%s%d" % (m, d), [64, LSEQ], F32, "Internal") for m in ("s5", "rw", "ret", "gla") for d in range(2)}
        S.aux_d = [p.dram("aux%d" % i, [64, LSEQ], F32, "Internal") for i in range(3)]
        S.yout_d = p.dram("yslot", [256, LSEQ], F32, "ExternalOutput")
        emit_s(p, S, do_s5, mixers)
        p.finish()
    return nc


def slot_cols(j):
    a = np.arange
    rw, ret, gla = 256, 1152, 2176
    return np.concatenate([
        a(64 * j, 64 * j + 64),
        rw + a(64 * j, 64 * j + 64), rw + 256 + a(64 * j, 64 * j + 64), rw + 512 + a(64 * j, 64 * j + 64),
        rw + a(768, 832), rw + a(832, 896),
        ret + a(64 * j, 64 * j + 64), ret + 256 + a(64 * j, 64 * j + 64), ret + 512 + a(64 * j, 64 * j + 64),
        ret + 768 + a(64 * j, 64 * j + 64),
        gla + a(32 * j, 32 * j + 32), gla + 128 + a(32 * j, 32 * j + 32), gla + 256 + a(64 * j, 64 * j + 64),
        gla + 512 + a(64 * j, 64 * j + 64), gla + a(768, 784)])


TWO_PI = float(2 * np.pi)


def emit_sincos(p, ang, s_out, c_out, wk, f):
    ki, a, b, c, d = wk["ki"], wk["a"], wk["b"], wk["c"], wk["d"]
    V = lambda e_, fn, rd, wr: p.op(e_, fn, reads=rd, writes=wr)
    V("dve", lambda e: e.tensor_scalar(out=f(a), in0=f(ang), scalar1=float(1.0 / TWO_PI), scalar2=None, op0=ALU.mult), [ang], [a])
    V("dve", lambda e: e.tensor_copy(out=f(ki), in_=f(a)), [a], [ki])
    V("dve", lambda e: e.tensor_copy(out=f(a), in_=f(ki)), [ki], [a])
    V("dve", lambda e: e.scalar_tensor_tensor(out=f(a), in0=f(a), scalar=-TWO_PI, in1=f(ang), op0=ALU.mult, op1=ALU.add), [a, ang], [a])
    V("act", lambda e: e.activation(out=f(b), in_=f(a), func=AF.Sin, scale=0.25), [a], [b])
    V("act", lambda e: e.activation(out=f(c), in_=f(a), func=AF.Sin, scale=0.25, bias=float(np.pi / 2)), [a], [c])
    for _ in range(2):
        V("dve", lambda e: e.tensor_tensor(out=f(d), in0=f(b), in1=f(c), op=ALU.mult), [b, c], [d])
        V("dve", lambda e: e.tensor_tensor(out=f(b), in0=f(b), in1=f(b), op=ALU.mult), [b], [b])
        V("dve", lambda e: e.tensor_tensor(out=f(c), in0=f(c), in1=f(c), op=ALU.mult), [c], [c])
        V("dve", lambda e: e.tensor_tensor(out=f(c), in0=f(c), in1=f(b), op=ALU.subtract), [c, b], [c])
        V("dve", lambda e: e.tensor_scalar(out=f(b), in0=f(d), scalar1=2.0, scalar2=None, op0=ALU.mult), [d], [b])
    V("dve", lambda e: e.tensor_copy(out=f(s_out), in_=f(b)), [b], [s_out])
    V("dve", lambda e: e.tensor_copy(out=f(c_out), in_=f(c)), [c], [c_out])


def emit_s_s5(p, S):
    cst = S.cst
    with ExitStack() as es:
        f = lambda shape, name: p.sb(es, shape, F32, name)
        col = pkc(S, "s5col", 128).rearrange("p (g k) -> p g k", k=3)
        dtc, rho, th = f([128, 8], "dtc"), f([128, 8], "rho"), f([128, 8], "th")
        p.op("act", lambda e: e.activation(out=dtc[:], in_=col[:, :, 2], func=AF.Exp), reads=[S.pk], writes=[dtc])
        p.op("dve", lambda e: e.tensor_scalar(out=rho[:], in0=col[:, :, 0], scalar1=-1e-4, scalar2=None, op0=ALU.min), reads=[S.pk], writes=[rho])
        p.op("dve", lambda e: e.tensor_tensor(out=rho[:], in0=rho[:], in1=dtc[:], op=ALU.mult), reads=[rho, dtc], writes=[rho])
        p.op("act", lambda e: e.activation(out=rho[:], in_=rho[:], func=AF.Exp), reads=[rho], writes=[rho])
        p.op("dve", lambda e: e.tensor_tensor(out=th[:], in0=col[:, :, 1], in1=dtc[:], op=ALU.mult), reads=[S.pk, dtc], writes=[th])
        COS, SIN = f([128, 8, TT], "COS"), f([128, 8, TT], "SIN")
        ang = f([128, TT], "ang")
        wk = {n: f([128, TT], "wk" + n) for n in ("a", "b", "c", "d")}
        wk["ki"] = p.sb(es, [128, TT], mybir.dt.int32, "wkki")
        for gd in range(8):
            p.op("dve", lambda e, gd=gd: e.tensor_scalar(out=ang[:], in0=cst[:, C_JJ:C_JJ + TT], scalar1=th[:, gd:gd + 1], scalar2=None,
                                                         op0=ALU.mult), reads=[cst, th], writes=[ang])
            emit_sincos(p, ang, _Sl(SIN, gd), _Sl(COS, gd), wk, lambda b: b[:] if not isinstance(b, _Sl) else b.ap())
        row = pkc(S, "s5row", 64).rearrange("p (g k n) -> p g k n", k=3, n=64)
        Bt = pkc(S, "s5B", 64).rearrange("p (g k n) -> p g k n", k=2, n=64)
        R = lambda name: f([64, 8, 64], name)
        dtr, lr, x, ar, ai, den, cr, ci, t1, t2 = (R(n) for n in ("dtr", "lr", "x", "ar", "ai", "den", "cr", "ci", "t1", "t2"))
        V = lambda e_, fn, rd, wr: p.op(e_, fn, reads=rd, writes=wr)
        V("act", lambda e: e.activation(out=dtr[:], in_=row[:, :, 2, :], func=AF.Exp), [S.pk], [dtr])
        V("dve", lambda e: e.tensor_scalar(out=lr[:], in0=row[:, :, 0, :], scalar1=-1e-4, scalar2=None, op0=ALU.min), [S.pk], [lr])
        V("dve", lambda e: e.tensor_tensor(out=x[:], in0=lr[:], in1=dtr[:], op=ALU.mult), [lr, dtr], [x])
        V("act", lambda e: e.activation(out=x[:], in_=x[:], func=AF.Exp), [x], [x])
        V("dve", lambda e: e.tensor_tensor(out=t1[:], in0=row[:, :, 1, :], in1=dtr[:], op=ALU.mult), [S.pk, dtr], [t1])
        f3 = lambda b: (b[:] if b.t.shape[0] == 64 else b[0:64, 0:512].rearrange("p (g n) -> p g n", n=64))
        emit_sincos(p, t1, ai, ar, wk, f3)
        V("dve", lambda e: e.tensor_tensor(out=ar[:], in0=ar[:], in1=x[:], op=ALU.mult), [ar, x], [ar])
        V("dve", lambda e: e.tensor_tensor(out=ai[:], in0=ai[:], in1=x[:], op=ALU.mult), [ai, x], [ai])
        V("dve", lambda e: e.tensor_scalar(out=ar[:], in0=ar[:], scalar1=-1.0, scalar2=None, op0=ALU.add), [ar], [ar])
        lim = row[:, :, 1, :]
        V("dve", lambda e: e.tensor_tensor(out=den[:], in0=lr[:], in1=lr[:], op=ALU.mult), [lr], [den])
        V("dve", lambda e: e.tensor_tensor(out=t1[:], in0=lim, in1=lim, op=ALU.mult), [S.pk], [t1])
        V("dve", lambda e: e.tensor_tensor(out=den[:], in0=den[:], in1=t1[:], op=ALU.add), [den, t1], [den])
        V("dve", lambda e: e.reciprocal(out=den[:], in_=den[:]), [den], [den])
        V("dve", lambda e: e.tensor_tensor(out=cr[:], in0=ar[:], in1=lr[:], op=ALU.mult), [ar, lr], [cr])
        V("dve", lambda e: e.tensor_tensor(out=t1[:], in0=ai[:], in1=lim, op=ALU.mult), [ai, S.pk], [t1])
        V("dve", lambda e: e.tensor_tensor(out=cr[:], in0=cr[:], in1=t1[:], op=ALU.add), [cr, t1], [cr])
        V("dve", lambda e: e.tensor_tensor(out=cr[:], in0=cr[:], in1=den[:], op=ALU.mult), [cr, den], [cr])
        V("dve", lambda e: e.tensor_tensor(out=ci[:], in0=ai[:], in1=lr[:], op=ALU.mult), [ai, lr], [ci])
        V("dve", lambda e: e.tensor_tensor(out=t1[:], in0=ar[:], in1=lim, op=ALU.mult), [ar, S.pk], [t1])
        V("dve", lambda e: e.tensor_tensor(out=ci[:], in0=ci[:], in1=t1[:], op=ALU.subtract), [ci, t1], [ci])
        V("dve", lambda e: e.tensor_tensor(out=ci[:], in0=ci[:], in1=den[:], op=ALU.mult), [ci, den], [ci])
        LB, LW_ = f([64, 8, 128], "LB"), f([64, 8, 128], "LWs5")
        Bre, Bim = Bt[:, :, 0, :], Bt[:, :, 1, :]
        V("dve", lambda e: e.tensor_tensor(out=t1[:], in0=cr[:], in1=Bre, op=ALU.mult), [cr, S.pk], [t1])
        V("dve", lambda e: e.tensor_tensor(out=t2[:], in0=ci[:], in1=Bim, op=ALU.mult), [ci, S.pk], [t2])
        V("dve", lambda e: e.tensor_tensor(out=LB[:, :, 0:64], in0=t1[:], in1=t2[:], op=ALU.subtract), [t1, t2], [LB])
        V("dve", lambda e: e.tensor_tensor(out=t1[:], in0=cr[:], in1=Bim, op=ALU.mult), [cr, S.pk], [t1])
        V("dve", lambda e: e.tensor_tensor(out=t2[:], in0=ci[:], in1=Bre, op=ALU.mult), [ci, S.pk], [t2])
        V("dve", lambda e: e.tensor_tensor(out=LB[:, :, 64:128], in0=t1[:], in1=t2[:], op=ALU.add), [t1, t2], [LB])
        V("dve", lambda e: e.tensor_copy(out=LW_[:, :, 0:64], in_=LB[:, :, 64:128]), [LB], [LW_])
        V("dve", lambda e: e.tensor_scalar(out=LW_[:, :, 64:128], in0=LB[:, :, 0:64], scalar1=-1.0, scalar2=None, op0=ALU.mult), [LB], [LW_])
        LBb = p.sb(es, [64, 8, 128], BF16, "LBb")
        LWb = p.sb(es, [64, 8, 128], BF16, "LWb")
        V("dve", lambda e: e.tensor_copy(out=LBb[:], in_=LB[:]), [LB], [LBb])
        V("dve", lambda e: e.tensor_copy(out=LWb[:], in_=LW_[:]), [LW_], [LWb])
        Cm = f([128, 8, 64], "Cm")
        Cpk = pkc(S, "s5C", 128).rearrange("p (g c) -> p g c", c=64)
        V("dve", lambda e: e.tensor_copy(out=Cm[0:64], in_=Cpk[0:64]), [S.pk], [Cm])
        V("dve", lambda e: e.tensor_scalar(out=Cm[64:128], in0=Cpk[64:128], scalar1=-1.0, scalar2=None, op0=ALU.mult), [S.pk], [Cm])
        u_t = [f([64, TT], "u") for _ in range(2)]
        ub_t = [p.sb(es, [64, TT], BF16, "ub") for _ in range(2)]
        stage = f([64, TT], "stage5")
        cb = [f([128, TT], "c5") for _ in range(2)]
        tb = [f([128, TT], "t5") for _ in range(2)]
        gb = [f([128, TT], "g5") for _ in range(2)]
        hb = [f([128, TT], "h5") for _ in range(2)]
        yo = [f([64, TT], "yo5") for _ in range(2)]
        carry = f([128, 4], "carry")
        psB = [p.ps(es, [128, TT], F32, "psB") for _ in range(2)]
        psW = [p.ps(es, [128, TT], F32, "psW") for _ in range(2)]
        psG = [p.ps(es, [128, TT], F32, "psG") for _ in range(2)]
        psYs = p.ps(es, [64, TT], F32, "psYs")
        k = 0
        for d in getattr(S, "dirs", (0, 1)):
            rev = d == 1
            p.op("pool", lambda e: e.memset(carry[:], 0.0), writes=[carry])
            for ti, (t0, W, is_ctx) in enumerate(s_tiles(rev)):
                u = u_t[ti % 2]
                s_load(p, S, u, 64, R_S5, t0, W, rev, stage)
                ub = ub_t[ti % 2]
                V("act", lambda e, u=u, ub=ub: e.activation(out=ub[:, 0:W], in_=u[:, 0:W], func=AF.Copy), [u], [ub])
                for gl in range(4):
                    gd = gl * 2 + d
                    c5, t5, g5, h5 = cb[k % 2], tb[k % 2], gb[k % 2], hb[k % 2]
                    pB, pW, pG = psB[k % 2], psW[k % 2], psG[k % 2]
                    k += 1
                    _mm(p, pB, pB[:, 0:W], LBb, LBb[:, gd, :], ub, ub[:, 0:W], True, True)
                    _mm(p, pW, pW[:, 0:W], LWb, LWb[:, gd, :], ub, ub[:, 0:W], True, True)
                    V("dve", lambda e, c5=c5, pB=pB, gd=gd: e.tensor_tensor(out=c5[:, 0:W], in0=pB[:, 0:W], in1=COS[:, gd, 0:W], op=ALU.mult),
                      [pB, COS], [c5])
                    V("dve", lambda e, t5=t5, pW=pW, gd=gd: e.tensor_tensor(out=t5[:, 0:W], in0=pW[:, 0:W], in1=SIN[:, gd, 0:W], op=ALU.mult),
                      [pW, SIN], [t5])
                    V("pool", lambda e, c5=c5, t5=t5: e.tensor_tensor(out=c5[:, 0:W], in0=c5[:, 0:W], in1=t5[:, 0:W], op=ALU.add), [c5, t5], [c5])
                    V("dve", lambda e, g5=g5, c5=c5, gd=gd, gl=gl: e.tensor_tensor_scan(
                        out=g5[:, 0:W], data0=rho[:, gd:gd + 1].to_broadcast([128, W]), data1=c5[:, 0:W],
                        initial=carry[:, gl:gl + 1], op0=ALU.mult, op1=ALU.add), [rho, c5, carry], [g5])
                    _mm(p, pG, pG[:, 0:W], cst, cst[:, C_JS:C_JS + 128], g5, g5[:, 0:W], True, True)
                    V("pool", lambda e, h5=h5, g5=g5, gd=gd: e.tensor_tensor(out=h5[:, 0:W], in0=g5[:, 0:W], in1=COS[:, gd, 0:W], op=ALU.mult),
                      [g5, COS], [h5])
                    V("dve", lambda e, t5=t5, pG=pG, gd=gd: e.tensor_tensor(out=t5[:, 0:W], in0=pG[:, 0:W], in1=SIN[:, gd, 0:W], op=ALU.mult),
                      [pG, SIN], [t5])
                    V("dve", lambda e, h5=h5, t5=t5: e.tensor_tensor(out=h5[:, 0:W], in0=h5[:, 0:W], in1=t5[:, 0:W], op=ALU.add), [h5, t5], [h5])
                    V("act", lambda e, h5=h5, gl=gl: e.activation(out=carry[:, gl:gl + 1], in_=h5[:, W - 1:W], func=AF.Copy), [h5], [carry])
                    _mm(p, psYs, psYs[:, 0:W], Cm, Cm[:, gd, :], h5, h5[:, 0:W], gl == 0, gl == 3)
                yb = yo[ti % 2]
                src = psYs[:, 0:W][:, ::-1] if rev else psYs[:, 0:W]
                V("act", lambda e, yb=yb, src=src: e.activation(out=yb[:, 0:W], in_=src, func=AF.Copy), [psYs], [yb])
                p.dma("act", S.ysc[("s5", d)][:, t0:t0 + W], yb[:, 0:W], reads=[yb])
        p.barrier()


class _Sl:
    def __init__(self, buf, gd):
        self.__dict__["b"] = buf
        self.__dict__["gd"] = gd

    def ap(self):
        return self.b[:, self.gd, :]

    def __getattr__(self, k):
        return getattr(self.b, k)

    def __setattr__(self, k, v):
        setattr(self.b, k, v)


DEPTH = 4
ZALL = 4 * ZROWS


def dense_tiles(T):
    return [(0, LCTX, 1)] + [(LCTX + i * T, T, 0) for i in range(LLAT // T)]


def build_full(depth=DEPTH, slots=(0, 1, 2, 3)):
    nc = bass.Bass("TRN2", target_bir_lowering=False)
    with ExitStack() as es:
        p = Prog(nc, es)
        ein = lambda name, shape: p.dram(name, shape, F32, "ExternalInput")
        xT = ein("xT", [D, LSEQ])
        cond = ein("cond", [128, 8, 2])
        cst_d = ein("consts", [128, NCONST])
        fin = ein("final_g", [128, 8])
        fo = p.dram("fo", [D, LLAT], F32, "ExternalOutput")
        z_d = p.dram("z_scr", [ZALL, LSEQ], F32, "Internal")
        y_d = p.dram("y_scr", [D, LSEQ], F32, "Internal")
        x_d = p.dram("x_scr", [D, LSEQ], F32, "Internal")
        x1_d = p.dram("x1_scr", [D, LSEQ], F32, "Internal")
        ysc = {(m, d): p.dram("ysc_%s%d" % (m, d), [64, LSEQ], F32, "Internal") for m in ("s5", "rw", "ret", "gla") for d in range(2)}
        aux_d = [p.dram("aux%d" % i, [64, LSEQ], F32, "Internal") for i in range(3)]
        for L in range(depth):
            sfx = "_%d" % L
            adaw, adab = ein("ada_w" + sfx, [D, 6 * D]), ein("ada_b" + sfx, [128, 48])
            g1n, g2n = ein("norm1_g" + sfx, [128, 8]), ein("norm2_g" + sfx, [128, 8])
            win = ein("w_in" + sfx, [D, ZALL])
            wout = ein("w_out" + sfx, [D, D])
            gluw, glub = ein("glu_w" + sfx, [64, 4, 256]), ein("glu_b" + sfx, [64, 4])
            w1, w2 = ein("w1" + sfx, [D, DFF]), ein("w2" + sfx, [DFF, D])
            pks = [ein("pk%s_%d" % (sfx, j), [128, NPK]) for j in range(4)]
            xin = xT if L == 0 else x_d
            emit_d1(p, xin, cond, adaw, adab, g1n, win, z_d, tl=dense_tiles(512), ncol=ZALL)
            for j in slots:
                S = SCtx()
                S.slot = j
                S.zsrc = lambda r0, nr, t0, nt, j=j: z_d[j * ZROWS + r0:j * ZROWS + r0 + nr, t0:t0 + nt]
                S.cst_d, S.pk_d, S.ysc, S.aux_d = cst_d, pks[j], ysc, aux_d
                S.yout_d = Buf(y_d[j * 256:(j + 1) * 256, :], "yslot%d" % j)
                emit_s(p, S)
            last = L == depth - 1
            emit_d2(p, xin, y_d, cond, adaw, adab, g2n, wout, gluw, glub, w1, w2, x1_d, x_d,
                    fin if last else None, fo if last else None, tl=dense_tiles(256), slotmajor=True, fo_off=LCTX)
        p.finish()
    return nc


def _fm(v):
    return np.ascontiguousarray(np.asarray(v, np.float32).reshape(8, 128).T)


def prep_inputs(inp, b, depth=DEPTH):
    f32 = lambda a: np.ascontiguousarray(np.asarray(a, np.float32))
    m = {}
    m["xT"] = f32(np.concatenate([inp["ctx"][b], inp["x"][b]], 0).T)
    cc = np.stack([inp["c"][b], inp["c_ctx"]], -1)
    m["cond"] = f32(cc.reshape(8, 128, 2).transpose(1, 0, 2))
    m["consts"] = make_consts()
    m["final_g"] = _fm(inp["final_g"])
    allcols = np.concatenate([slot_cols(j) for j in range(4)])
    r = np.arange(1024)
    orig = ((r % 256) // 64) * 256 + (r // 256) * 64 + (r % 64)
    for L in range(depth):
        sfx = "_%d" % L
        m["ada_w" + sfx] = f32(inp["ada_w"][L])
        m["ada_b" + sfx] = f32(inp["ada_b"][L].reshape(48, 128).T)
        m["norm1_g" + sfx] = _fm(inp["norm1_g"][L])
        m["norm2_g" + sfx] = _fm(inp["norm2_g"][L])
        m["w_in" + sfx] = f32(inp["w_in"][L][:, allcols])
        m["w_out" + sfx] = f32(inp["w_out"][L][orig, :])
        m["glu_w" + sfx] = f32(inp["s5_glu_w"][L].reshape(4, 64, 256).transpose(1, 0, 2))
        m["glu_b" + sfx] = f32(inp["s5_glu_b"][L].reshape(4, 64).T)
        m["w1" + sfx] = f32(inp["mlp_w1"][L])
        m["w2" + sfx] = f32(inp["mlp_w2"][L])
        for j in range(4):
            m["pk%s_%d" % (sfx, j)] = pack_params(inp, L, j)
    return m


_NC_CACHE = {}


def kernel(**inputs):
    inp = {k: np.asarray(v) for k, v in inputs.items()}
    if "full" not in _NC_CACHE:
        _NC_CACHE["full"] = build_full()
    nc = _NC_CACHE["full"]
    in_maps = [prep_inputs(inp, b) for b in range(2)]
    res = run_bass_kernel_spmd(nc, in_maps, core_ids=[0, 1])
    out = np.stack([np.ascontiguousarray(res.results[b]["fo"].T) for b in range(2)], 0)
    return out.astype(np.float32)
```
